# Optimizing a Trainium2 kernel written in Bass

```python
import jax, jax.numpy as jnp
from jax import lax
import numpy as np

D_MODEL = 1024
BATCH = 16
SEQ = 4096
DEPTH = 4
DEC_BATCH = 1
DEC_SEQ = 16384
PAST_LEN = 128

GRID_W = 64
NA_HEADS = 8
NA_HEAD_DIM = 64
NA_WIDTH = NA_HEADS * NA_HEAD_DIM
NA_WH = 8
NA_WW = 16
RET_HEADS = 4
RET_QK_DIM = 128
RET_V_DIM = 256
RET_QK_WIDTH = RET_HEADS * RET_QK_DIM
RET_V_WIDTH = RET_HEADS * RET_V_DIM
RET_CHUNK = 128
ROPE_BASE = 10000.0
D_FF = 2816
CONV_WIDTH = 3
EPS = 1e-6
SPLIT_SIZES = (NA_WIDTH, NA_WIDTH, NA_WIDTH,
               RET_QK_WIDTH, RET_QK_WIDTH, RET_V_WIDTH, RET_V_WIDTH,
               D_MODEL, D_MODEL)
D_IN = sum(SPLIT_SIZES)

kernel_name = "hybrid_natten_retnet_encoder"


def rms_norm(x, g):
    xf = x.astype(jnp.float32)
    y = xf * lax.rsqrt(jnp.mean(xf * xf, axis=-1, keepdims=True) + EPS)
    return (y * g.astype(jnp.float32)).astype(x.dtype)


def rope_tables(seq_len):
    inv_freq = ROPE_BASE ** (-jnp.arange(0, RET_QK_DIM, 2, dtype=jnp.float32) / RET_QK_DIM)
    ang = jnp.arange(seq_len, dtype=jnp.float32)[:, None] * inv_freq[None, :]
    return jnp.cos(ang)[:, None, :], jnp.sin(ang)[:, None, :]


def apply_rope(x, cos, sin):
    x1, x2 = jnp.split(x, 2, axis=-1)
    return jnp.concatenate([x1 * cos - x2 * sin, x1 * sin + x2 * cos], axis=-1)


def neighborhood_attention(q, k, v, rel_bias):
    B, S, H, dh = q.shape
    rows = S // GRID_W
    wh = min(NA_WH, rows)
    scale = dh ** -0.5
    q = q.reshape(B, rows, GRID_W, H, dh)
    k = k.reshape(B, rows, GRID_W, H, dh)
    v = v.reshape(B, rows, GRID_W, H, dh)
    col = np.arange(GRID_W)
    col_start = np.clip(col - NA_WW // 2, 0, GRID_W - NA_WW)
    col_idx = col_start[:, None] + np.arange(NA_WW)[None, :]
    col_bias_idx = (col_idx - col[:, None]) + (NA_WW - 1)

    def one_row(r):
        rs = jnp.clip(r - wh // 2, 0, rows - wh)
        q_r = lax.dynamic_index_in_dim(q, r, axis=1, keepdims=False)
        k_band = lax.dynamic_slice_in_dim(k, rs, wh, axis=1)
        v_band = lax.dynamic_slice_in_dim(v, rs, wh, axis=1)
        k_win = k_band[:, :, col_idx]
        v_win = v_band[:, :, col_idx]
        s = jnp.einsum('bqhd,bwqchd->bhqwc', q_r, k_win).astype(jnp.float32) * scale
        row_bias_idx = (rs + jnp.arange(wh) - r) + (NA_WH - 1)
        bias = rel_bias[:, row_bias_idx[:, None, None], col_bias_idx[None, :, :]]
        s = s + jnp.transpose(bias, (0, 2, 1, 3)).astype(jnp.float32)[None]
        p = jax.nn.softmax(s.reshape(B, H, GRID_W, wh * NA_WW), axis=-1)
        p = p.reshape(B, H, GRID_W, wh, NA_WW).astype(v.dtype)
        return jnp.einsum('bhqwc,bwqchd->bqhd', p, v_win)

    out = lax.map(one_row, jnp.arange(rows))
    return jnp.transpose(out, (1, 0, 2, 3, 4)).reshape(B, S, H * dh)


def retention_scan(q, k, v, log_gamma, inclusive):
    B, H, S, dk = q.shape
    dv = v.shape[-1]
    C = RET_CHUNK
    N = S // C
    qc = jnp.transpose(q.reshape(B, H, N, C, dk), (2, 0, 1, 3, 4))
    kc = jnp.transpose(k.reshape(B, H, N, C, dk), (2, 0, 1, 3, 4))
    vc = jnp.transpose(v.reshape(B, H, N, C, dv), (2, 0, 1, 3, 4))
    idx = jnp.arange(C, dtype=jnp.float32)
    diff = idx[:, None] - idx[None, :]
    mask = diff >= 0 if inclusive else diff > 0
    lg = log_gamma[:, None, None]
    d_intra = jnp.where(mask[None], jnp.exp(jnp.maximum(diff, 0.0)[None] * lg), 0.0)
    q_decay = jnp.exp((idx + 1.0)[None, :] * log_gamma[:, None])[None, :, :, None]
    k_decay = jnp.exp((C - 1.0 - idx)[None, :] * log_gamma[:, None])[None, :, :, None]
    chunk_decay = jnp.exp(C * log_gamma)[None, :, None, None]

    def step(state, inp):
        q_i, k_i, v_i = inp
        intra = jnp.einsum('bhid,bhjd->bhij', q_i, k_i) * d_intra[None]
        o = jnp.einsum('bhij,bhjv->bhiv', intra, v_i) + \
            jnp.einsum('bhid,bhdv->bhiv', q_i * q_decay, state)
        state = state * chunk_decay + jnp.einsum('bhjd,bhjv->bhdv', k_i * k_decay, v_i)
        return state, o

    state0 = jnp.zeros((B, H, dk, dv), jnp.float32)
    _, o = lax.scan(step, state0, (qc, kc, vc))
    return jnp.transpose(o, (1, 2, 0, 3, 4)).reshape(B, H, S, dv)


def bidirectional_retention(q, k, v, lg_fwd, lg_bwd):
    q = jnp.transpose(q, (0, 2, 1, 3))
    k = jnp.transpose(k, (0, 2, 1, 3))
    v = jnp.transpose(v, (0, 2, 1, 3))
    fwd = retention_scan(q, k, v, lg_fwd, True)
    bwd = retention_scan(q[:, :, ::-1], k[:, :, ::-1], v[:, :, ::-1], lg_bwd, False)[:, :, ::-1]
    return fwd + bwd


def head_group_norm(o, g):
    mu = jnp.mean(o, axis=-1, keepdims=True)
    var = jnp.mean(jnp.square(o - mu), axis=-1, keepdims=True)
    y = (o - mu) * lax.rsqrt(var + EPS)
    B, H, S, dv = o.shape
    y = jnp.transpose(y, (0, 2, 1, 3)).reshape(B, S, H * dv)
    return y * g.astype(jnp.float32)


def centred_depthwise_conv(u, w):
    up = jnp.pad(u, ((0, 0), (1, 1), (0, 0)))
    return up[:, :-2] * w[0] + up[:, 1:-1] * w[1] + up[:, 2:] * w[2]


def trunk(x, norm_mix_g, w_in, na_rel_bias, ret_decay_fwd, ret_decay_bwd, ret_norm_g,
          w_branch_attn, w_branch_ret, w_out, norm_ffn_g, w_up, ffn_conv_w, w_down, norm_final_g):
    B, S, _ = x.shape
    cos, sin = rope_tables(S)
    split_at = np.cumsum(SPLIT_SIZES)[:-1].tolist()
    for l in range(DEPTH):
        h = rms_norm(x, norm_mix_g[l])
        proj = h @ w_in[l]
        na_q, na_k, na_v, r_q, r_k, r_v, r_g, gate_a, gate_r = jnp.split(proj, split_at, axis=-1)
        a = neighborhood_attention(na_q.reshape(B, S, NA_HEADS, NA_HEAD_DIM),
                                   na_k.reshape(B, S, NA_HEADS, NA_HEAD_DIM),
                                   na_v.reshape(B, S, NA_HEADS, NA_HEAD_DIM),
                                   na_rel_bias[l])
        rq = apply_rope(r_q.astype(jnp.float32).reshape(B, S, RET_HEADS, RET_QK_DIM), cos, sin)
        rk = apply_rope(r_k.astype(jnp.float32).reshape(B, S, RET_HEADS, RET_QK_DIM), cos, sin) * (RET_QK_DIM ** -0.5)
        rv = r_v.astype(jnp.float32).reshape(B, S, RET_HEADS, RET_V_DIM)
        ro = bidirectional_retention(rq, rk, rv,
                                     jax.nn.log_sigmoid(ret_decay_fwd[l].astype(jnp.float32)),
                                     jax.nn.log_sigmoid(ret_decay_bwd[l].astype(jnp.float32)))
        ro = (jax.nn.silu(r_g.astype(jnp.float32)) * head_group_norm(ro, ret_norm_g[l])).astype(x.dtype)
        y_a = a @ w_branch_attn[l]
        y_r = ro @ w_branch_ret[l]
        mixed = jax.nn.sigmoid(gate_a) * y_a + jax.nn.sigmoid(gate_r) * y_r
        x = x + mixed @ w_out[l]
        h = rms_norm(x, norm_ffn_g[l])
        u = centred_depthwise_conv(h @ w_up[l], ffn_conv_w[l])
        gate, val = jnp.split(u, 2, axis=-1)
        x = x + (jax.nn.gelu(gate) * val) @ w_down[l]
    return rms_norm(x, norm_final_g)


def setup_inputs(seed: int = 0) -> dict:
    key = jax.random.key(seed)
    ks = jax.random.split(key, 16)
    f32 = jnp.float32

    def nrm(k, shape, scale):
        return jax.random.normal(k, shape, f32) * scale

    base_decay = jnp.log(2.0 ** (5.0 + jnp.arange(RET_HEADS, dtype=f32)) - 1.0)
    return {
        "x_prompt": nrm(ks[0], (BATCH, SEQ, D_MODEL), 1.0),
        "x_sample": nrm(ks[1], (DEC_BATCH, DEC_SEQ, D_MODEL), 1.0),
        "norm_mix_g": 1.0 + nrm(ks[2], (DEPTH, D_MODEL), 0.02),
        "w_in": nrm(ks[3], (DEPTH, D_MODEL, D_IN), D_MODEL ** -0.5),
        "na_rel_bias": nrm(ks[4], (DEPTH, NA_HEADS, 2 * NA_WH - 1, 2 * NA_WW - 1), 0.1),
        "ret_decay_fwd": base_decay[None, :] + nrm(ks[5], (DEPTH, RET_HEADS), 0.1),
        "ret_decay_bwd": base_decay[None, :] + nrm(ks[6], (DEPTH, RET_HEADS), 0.1),
        "ret_norm_g": 1.0 + nrm(ks[7], (DEPTH, RET_V_WIDTH), 0.02),
        "w_branch_attn": nrm(ks[8], (DEPTH, NA_WIDTH, D_MODEL), NA_WIDTH ** -0.5),
        "w_branch_ret": nrm(ks[9], (DEPTH, RET_V_WIDTH, D_MODEL), RET_V_WIDTH ** -0.5),
        "w_out": nrm(ks[10], (DEPTH, D_MODEL, D_MODEL), D_MODEL ** -0.5),
        "norm_ffn_g": 1.0 + nrm(ks[11], (DEPTH, D_MODEL), 0.02),
        "w_up": nrm(ks[12], (DEPTH, D_MODEL, 2 * D_FF), D_MODEL ** -0.5),
        "ffn_conv_w": nrm(ks[13], (DEPTH, CONV_WIDTH, 2 * D_FF), CONV_WIDTH ** -0.5),
        "w_down": nrm(ks[14], (DEPTH, D_FF, D_MODEL), D_FF ** -0.5),
        "norm_final_g": 1.0 + nrm(ks[15], (D_MODEL,), 0.02),
    }


def reference(x_prompt, x_sample, norm_mix_g, w_in, na_rel_bias, ret_decay_fwd, ret_decay_bwd,
              ret_norm_g, w_branch_attn, w_branch_ret, w_out, norm_ffn_g, w_up, ffn_conv_w,
              w_down, norm_final_g):
    y_prompt = trunk(x_prompt, norm_mix_g, w_in, na_rel_bias, ret_decay_fwd, ret_decay_bwd, ret_norm_g,
                     w_branch_attn, w_branch_ret, w_out, norm_ffn_g, w_up, ffn_conv_w, w_down, norm_final_g)
    y_sample = trunk(x_sample, norm_mix_g, w_in, na_rel_bias, ret_decay_fwd, ret_decay_bwd, ret_norm_g,
                     w_branch_attn, w_branch_ret, w_out, norm_ffn_g, w_up, ffn_conv_w, w_down, norm_final_g)
    return (y_prompt, y_sample)
```

```python
import numpy as np
import concourse.bass as bass
import concourse.mybir as mybir

F32 = mybir.dt.float32
BF16 = mybir.dt.bfloat16
U8 = mybir.dt.uint8
AF = mybir.ActivationFunctionType
ALU = mybir.AluOpType
AX = mybir.AxisListType

DT_SIZE = {F32: 4, BF16: 2, U8: 1}


class Res:
    __slots__ = ("w", "rc", "rd", "name")

    def __init__(self, name=""):
        self.w = None
        self.rc = {}
        self.rd = []
        self.name = name


class Tile:
    __slots__ = ("ap", "res", "off")

    def __init__(self, ap, res, off=None):
        self.ap = ap
        self.res = res
        self.off = off

    def __getitem__(self, k):
        return self.ap[k]


class Op:
    __slots__ = ("eng", "fn", "reads", "writes", "dma", "deps", "need_inc", "inc", "key", "snap", "barrier")


ENGS = ("pe", "act", "dve", "pool", "sp")


class Prog:
    def __init__(self, nc, n_dma_sems=40):
        self.nc = nc
        self.ops = []
        self.eobj = {"pe": nc.tensor, "act": nc.scalar, "dve": nc.vector, "pool": nc.gpsimd, "sp": nc.sync}
        self.n_dma_sems = n_dma_sems
        self.dma_q = ("sp", "pool", "act")
        self.dma_cnt = {q: 0 for q in self.dma_q}
        self.touched = []
        self.sems = {}
        for e in ENGS:
            self.sems[("c", e)] = nc.alloc_semaphore("c_" + e)
        for i in range(n_dma_sems):
            self.sems[("d", "sp", i)] = nc.alloc_semaphore("d_sp_%d" % i)
        self.tot = dict(n_ops=0, n_wait=0, n_seg=0)

    def add(self, eng, fn, reads=(), writes=(), dma=False):
        op = Op()
        op.eng = eng
        op.fn = fn
        op.reads = [r.res if isinstance(r, Tile) else r for r in reads]
        op.writes = [r.res if isinstance(r, Tile) else r for r in writes]
        op.dma = dma
        op.need_inc = dma
        op.barrier = False
        op.deps = None
        op.snap = None
        self.ops.append(op)
        self.touched.extend(op.reads)
        self.touched.extend(op.writes)
        return op

    def barrier(self):
        self._bar = getattr(self, "_bar", 0) + 1
        for e in ENGS:
            op = self.add(e, None)
            op.barrier = self._bar

    def finalize(self):
        nc = self.nc
        ops = self.ops
        nsem = self.n_dma_sems
        dma_sem_last = {}
        dma_sem_cnt = {}
        qcount = {q: 0 for q in self.dma_q}
        last_on_eng = {e: None for e in ENGS}
        dma_since_barrier = []
        bar_snap = {}
        for i, op in enumerate(ops):
            deps = set()
            if op.barrier:
                if bar_snap.get("id") != op.barrier:
                    bar_snap = dict(id=op.barrier, last=dict(last_on_eng), dmas=list(dma_since_barrier))
                    dma_since_barrier = []
                for e in ENGS:
                    if e != op.eng and bar_snap["last"][e] is not None:
                        deps.add(bar_snap["last"][e])
                deps.update(bar_snap["dmas"])
                op.deps = sorted(deps, reverse=True)
                for d in op.deps:
                    ops[d].need_inc = True
                continue
            for r in op.reads:
                if r.w is not None:
                    deps.add(r.w)
            for r in op.writes:
                if r.w is not None:
                    deps.add(r.w)
                deps.update(r.rc.values())
                deps.update(r.rd)
            if op.dma:
                slot = (op.eng, qcount[op.eng] % nsem)
                qcount[op.eng] += 1
                if slot in dma_sem_last:
                    deps.add(dma_sem_last[slot])
                dma_sem_last[slot] = i
                dma_sem_cnt[slot] = dma_sem_cnt.get(slot, 0) + 16
                op.key = ("d",) + slot
                op.inc = dma_sem_cnt[slot]
                dma_since_barrier.append(i)
            deps.discard(i)
            out = []
            rset = set(id(r) for r in op.reads)
            for d in deps:
                po = ops[d]
                if po.eng == op.eng and not po.dma and not op.dma:
                    if op.eng == "pe":
                        continue
                    raw = False
                    for w in po.writes:
                        if id(w) in rset:
                            raw = True
                            break
                    if not raw:
                        continue
                out.append(d)
            op.deps = sorted(out, reverse=True)
            for d in op.deps:
                ops[d].need_inc = True
            if op.dma:
                for r in op.reads:
                    r.rd.append(i)
            else:
                for r in op.reads:
                    r.rc[op.eng] = i
            for r in op.writes:
                r.w = i
                r.rc = {}
                r.rd = []
            last_on_eng[op.eng] = i
        ccount = {e: 0 for e in ENGS}
        for op in ops:
            if not op.dma and op.need_inc:
                ccount[op.eng] += 1
                op.inc = ccount[op.eng]
                op.key = ("c", op.eng)
        sems = self.sems
        known = {e: {} for e in ENGS}
        nwait = 0
        for op in ops:
            e = op.eng
            k = known[e]
            eo = self.eobj[e]
            for d in op.deps:
                po = ops[d]
                key, val = po.key, po.inc
                if k.get(key, 0) >= val:
                    continue
                eo.wait_ge(sems[key], val)
                nwait += 1
                k[key] = val
                if po.snap is not None:
                    for kk, vv in po.snap.items():
                        if k.get(kk, 0) < vv:
                            k[kk] = vv
            ins = None
            if op.fn is not None:
                ins = op.fn()
            if op.need_inc:
                if ins is None:
                    ins = eo.nop()
                ins.then_inc(sems[op.key], 16 if op.dma else 1)
                snap = dict(k)
                snap[op.key] = op.inc
                op.snap = snap
        sp = self.eobj["sp"]
        ksp = known["sp"]
        for slot, cnt in dma_sem_cnt.items():
            key = ("d",) + slot
            if ksp.get(key, 0) < cnt:
                sp.wait_ge(sems[key], cnt)
        self.stats = dict(n_ops=len(ops), n_wait=nwait)
        self._used = [("d",) + slot for slot in dma_sem_cnt] + [("c", e) for e in ENGS if ccount[e] > 0]
        return self.stats

    def emit(self):
        nc = self.nc
        st = self.finalize()
        nc.all_engine_barrier()
        for key in self._used:
            nc.gpsimd.sem_clear(self.sems[key])
        nc.all_engine_barrier()
        for r in self.touched:
            r.w = None
            r.rc = {}
            r.rd = []
        self.touched = []
        self.ops = []
        self.tot["n_ops"] += st["n_ops"]
        self.tot["n_wait"] += st["n_wait"]
        self.tot["n_seg"] += 1


class Arena:
    def __init__(self, nc, nbytes):
        self.nc = nc
        self.t = nc.alloc_sbuf_tensor("arena", [128, nbytes], U8)
        self.n = nbytes
        self.off = 0
        self.peak = 0

    def mark(self):
        return self.off

    def reset(self, m):
        self.off = m

    def view(self, tile, free_shape, dtype):
        return self.alloc("view", free_shape, dtype, res=tile.res, at=tile.off)

    def alloc(self, name, free_shape, dtype, parts=128, res=None, at=None):
        sz = DT_SIZE[dtype]
        n = int(np.prod(free_shape)) * sz
        if at is None:
            off = (self.off + 31) // 32 * 32
            assert off + n <= self.n, f"SBUF arena overflow allocating {name}: {off}+{n} > {self.n}"
            self.off = off + n
            self.peak = max(self.peak, self.off)
        else:
            off = at
        ap = self.t[0:parts, off:off + n].bitcast(dtype)
        if len(free_shape) == 2:
            ap = ap.rearrange("p (a b) -> p a b", b=free_shape[1])
        elif len(free_shape) == 3:
            ap = ap.rearrange("p (a b c) -> p a b c", b=free_shape[1], c=free_shape[2])
        elif len(free_shape) == 4:
            ap = ap.rearrange("p (a b c d) -> p a b c d", b=free_shape[1], c=free_shape[2], d=free_shape[3])
        return Tile(ap, res if res is not None else Res(name), off)


from functools import partial
from concourse.bass_utils import run_bass_kernel_spmd

D = 1024
DIN = 6656
DFF = 2816
UT = 4096
NEG = -30000.0
C_NAQ, C_NAK, C_NAV, C_RQ, C_RK, C_RV, C_RG, C_GA, C_GR = 0, 512, 1024, 1536, 2048, 2560, 3584, 4608, 5632
CT_M1, CT_M2, CT_I1, CT_I2, CT_J1, CT_J2, CT_N = 0, 128, 256, 384, 512, 513, 514
RSCALE = 128 ** -0.5
GELU_C = 1.5957691216057308


def build(NU, L, dbg=False):
    nc = bass.Bass("TRN2", target_bir_lowering=False)
    T = NU * UT
    NB = T // 512
    NCH = T // 128
    NFB = T // 256
    P = Prog(nc)
    A = Arena(nc, 207000)

    def dram(name, shape, dtype, kind="Internal"):
        if dbg and kind == "Internal":
            kind = "ExternalOutput"
        return nc.dram_tensor(name, shape, dtype, kind=kind).ap()

    def SL(a, n):
        return slice(a, a + n)

    def uview(apx, tok_axis, win):
        dims = [list(d) for d in apx.ap]
        s = dims[tok_axis][0]
        new = [[UT * s, NU]] + dims[:tok_axis] + [[s, win]] + dims[tok_axis + 1:]
        return bass.AP(apx.tensor, apx.offset, new)

    x_d = dram("x", [T, D], F32, "ExternalInput")
    rope_d = dram("rope", [2, 128, T], F32, "ExternalInput")
    links_d = dram("links", [NU, 128, 2], F32, "ExternalInput")
    qx_d = dram("qx", [12, NU * 5 * 128], F32, "ExternalInput")
    kx_d = dram("kx", [12, 768], F32, "ExternalInput")
    ctab_d = dram("ctab", [128, CT_N], F32, "ExternalInput")
    gmix_d = dram("norm_mix_g", [L, 8, 128], F32, "ExternalInput")
    gffn_d = dram("norm_ffn_g", [L, 8, 128], F32, "ExternalInput")
    gfin_d = dram("norm_final_g", [8, 128], F32, "ExternalInput")
    convw_d = dram("ffn_conv_w", [L, 3, 44, 128], F32, "ExternalInput")
    decf_d = dram("ret_decay_fwd", [L * 4], F32, "ExternalInput")
    decb_d = dram("ret_decay_bwd", [L * 4], F32, "ExternalInput")
    y_d = dram("y", [T, D], F32, "ExternalOutput")

    xTs = dram("xTs", [8, 128, T], F32)
    hTs = dram("hTs", [8, 128, T], BF16)
    naqT = dram("naqT", [4, 128, T], BF16)
    nakT = dram("nakT", [4, 128, T + 512], BF16)
    navs = dram("navs", [T + 512, 520], BF16)
    rqT = dram("rqT", [4, 128, T], BF16)
    rkT = dram("rkT", [4, 128, T], BF16)
    rvs = dram("rvs", [T, 1024], BF16)
    rgs = dram("rgs", [T, 1024], BF16)
    gTs = dram("gTs", [16, 128, T], BF16)
    Bst = dram("Bst", [NCH, 128, 1024], BF16)
    aTs = dram("aTs", [4, 128, T], BF16)
    roTs = dram("roTs", [8, 128, T], BF16)
    h2Ts = dram("h2Ts", [8, 128, T + 128], BF16)
    ptab = dram("ptab", [L + 1, 128, 160], F32)
    NWL = 1024 * DIN + 512 * D + D * D + D * D + D * 2 * DFF + DFF * D + 2 * 128 * 8 * 768 + D
    wall_d = dram("wall", [L, NWL], F32, "ExternalInput")
    wcur = dram("wcur", [NWL], F32)
    _wo = [0]

    def wview(rows, cols):
        v = bass.AP(wcur.tensor, _wo[0], [[cols, rows], [1, cols]])
        _wo[0] += rows * cols
        return v
    w_in_c = wview(D, DIN)
    w_ba_c = wview(512, D)
    w_br_c = wview(D, D)
    w_out_c = wview(D, D)
    w_up_c = wview(D, 2 * DFF)
    w_dn_c = wview(DFF, D)
    ta_c = bass.AP(wcur.tensor, _wo[0], [[128 * 8 * 768, 2], [8 * 768, 128], [768, 8], [1, 768]])
    _wo[0] += 2 * 128 * 8 * 768
    rng_c = bass.AP(wcur.tensor, _wo[0], [[1, D]])
    xU = uview(x_d, 0, UT)
    yU = uview(y_d, 0, UT)
    rope_dU = uview(rope_d, 2, UT)
    qxU = uview(qx_d, 1, 640) if False else bass.AP(qx_d.tensor, qx_d.offset, [[640, NU], [NU * 640, 12], [1, 640]])
    xTsU = uview(xTs, 2, UT)
    hTsU = uview(hTs, 2, UT)
    naqTU = uview(naqT, 2, UT)
    nakTU = uview(nakT, 2, UT + 512)
    navsU = uview(navs, 0, UT + 512)
    rqTU = uview(rqT, 2, UT)
    rkTU = uview(rkT, 2, UT)
    rvsU = uview(rvs, 0, UT)
    rgsU = uview(rgs, 0, UT)
    gTsU = uview(gTs, 2, UT)
    BstU = Bst.rearrange("(u c) p f -> u c p f", u=NU)
    aTsU = uview(aTs, 2, UT)
    roTsU = uview(roTs, 2, UT)
    h2TsU = uview(h2Ts, 2, UT + 128)

    pp = [nc.alloc_psum_tensor("pp%d" % i, [128, 1024], F32) for i in range(4)]
    psr = [Res("ps%d" % i) for i in range(8)]

    def bank(i):
        return pp[i // 2][:, (i % 2) * 512:(i % 2) * 512 + 512]

    def dbank(j):
        return pp[j][:, :]

    rot = [0]

    def next_bank():
        i = rot[0] % 8
        rot[0] += 1
        return bank(i), psr[i]

    def mm_group(out_ap, pairs):
        n = len(pairs)
        ins = None
        for i, (a, b) in enumerate(pairs):
            ins = nc.tensor.matmul(out_ap, a, b, start=(i == 0), stop=(i == n - 1))
        return ins

    def dma(out, in_, reads=(), writes=(), **kw):
        return P.add("sp", lambda: nc.sync.dma_start(out=out, in_=in_, **kw), reads=reads, writes=writes, dma=True)

    ident = A.alloc("ident", [128], BF16)
    identf = A.alloc("identf", [128], F32)
    ones = A.alloc("ones", [128], BF16)
    eps_t = A.alloc("eps", [1], F32)
    zero_t = A.alloc("zero", [520], BF16)
    lk2 = A.alloc("lk2", [2], F32)
    kx = A.alloc("kx", [768], BF16)
    qx = A.alloc("qx", [640], BF16)
    qxs = A.alloc("qxs", [640], F32)
    ctab = A.alloc("ctab", [CT_N], F32)
    lp = A.alloc("lp", [160], F32)
    gnext = A.alloc("gnext", [8], F32)
    gfin = A.alloc("gfin", [8], F32)
    gmix_ap = lp[:, 0:8]
    gffn_ap = lp[:, 8:16]
    convw_v = lp[:, 16:148].rearrange("p (k c) -> p k c", c=44)
    lg_v = lp[:, 148:156]

    def setup():
        m = A.mark()
        stg = A.alloc("stg0", [1024], F32)
        stg2 = A.alloc("stg1", [128], F32)
        gmix = A.alloc("gmix", [L + 1, 8], F32)
        gffn = A.alloc("gffn", [L, 8], F32)
        convw = A.alloc("convw", [L, 3, 44], F32)
        lg = A.alloc("lg", [L * 8], F32)
        P.add("pool", lambda: nc.gpsimd.memset(identf[:], 1.0), writes=[identf])
        P.add("pool", lambda: nc.gpsimd.affine_select(out=identf[:], in_=identf[:], pattern=[[-1, 128]],
                                                       compare_op=ALU.is_equal, fill=0.0, base=0, channel_multiplier=1),
              reads=[identf], writes=[identf])
        P.add("dve", lambda: nc.vector.tensor_copy(ident[:], identf[:]), reads=[identf], writes=[ident])
        P.add("pool", lambda: nc.gpsimd.memset(ones[:], 1.0), writes=[ones])
        P.add("pool", lambda: nc.gpsimd.memset(eps_t[:], 1e-6), writes=[eps_t])
        P.add("pool", lambda: nc.gpsimd.memset(zero_t[:], 0.0), writes=[zero_t])
        dma(ctab[:], ctab_d[:, :], writes=[ctab])
        dma(stg[0:12, 0:768], kx_d[:, :], writes=[stg])
        P.add("dve", lambda: nc.vector.tensor_copy(kx[0:12, :], stg[0:12, 0:768]), reads=[stg], writes=[kx])

        def colvec(dst_ap, src_ap, n):
            dma(stg2[0:n, :], src_ap, writes=[stg2])
            bk, br = next_bank()
            P.add("pe", lambda: nc.tensor.transpose(bk[:, 0:n], stg2[0:n, :], identf[0:n, 0:n]),
                  reads=[stg2, identf], writes=[br])
            P.add("dve", lambda: nc.vector.tensor_copy(dst_ap, bk[:, 0:n]), reads=[br], writes=[gmix, gffn, convw, gfin])
        for l in range(L):
            colvec(gmix[:, l, :], gmix_d[l], 8)
            colvec(gffn[:, l, :], gffn_d[l], 8)
            for k in range(3):
                colvec(convw[:, l, k, :], convw_d[l, k], 44)
        colvec(gfin[:, :], gfin_d[:, :], 8)
        colvec(gmix[:, L, :], gfin_d[:, :], 8)
        dec = A.alloc("dec", [L * 8], F32)
        dv = dec[:].rearrange("p (l e) -> p l e", e=8)
        dma(dv[:, :, 0:4], decf_d.rearrange("(l h) -> l h", h=4).partition_broadcast(128), writes=[dec])
        dma(dv[:, :, 4:8], decb_d.rearrange("(l h) -> l h", h=4).partition_broadcast(128), writes=[dec])
        P.add("act", lambda: nc.scalar.activation(out=dec[:], in_=dec[:], func=AF.Exp, scale=-1.0), reads=[dec], writes=[dec])
        P.add("act", lambda: nc.scalar.activation(out=dec[:], in_=dec[:], func=AF.Ln, bias=1.0), reads=[dec], writes=[dec])
        P.add("dve", lambda: nc.vector.tensor_scalar(lg[:], dec[:], -1.0, None, ALU.mult), reads=[dec], writes=[lg])
        for c in range(4):
            dma(nakT[c, :, 0:256], zero_t[:, 0:256], reads=[zero_t])
            dma(nakT[c, :, T + 256:T + 512], zero_t[:, 0:256], reads=[zero_t])
        for r0 in (0, 128, T + 256, T + 384):
            dma(navs[r0:r0 + 128, :], zero_t[:, :], reads=[zero_t])
        for c in range(8):
            dma(h2Ts[c, :, 0:64], zero_t[:, 0:64], reads=[zero_t])
            dma(h2Ts[c, :, T + 64:T + 128], zero_t[:, 0:64], reads=[zero_t])
        for l in range(L + 1):
            dma(ptab[l, :, 0:8], gmix[:, l, :], reads=[gmix])
            if l < L:
                dma(ptab[l, :, 8:16], gffn[:, l, :], reads=[gffn])
                dma(ptab[l, :, 16:148], convw[:, l, :, :].rearrange("p k c -> p (k c)"), reads=[convw])
                dma(ptab[l, :, 148:156], lg[:, l * 8:(l + 1) * 8], reads=[lg])
        P.emit()
        A.reset(m)

    def load_unit_params(u):
        dma(lk2[:], links_d[u], writes=[lk2])

    def load_layer_params(l):
        dma(wcur.rearrange("(a b) -> a b", a=16), wall_d[l].rearrange("(a b) -> a b", a=16))
        dma(lp[:], ptab[l], writes=[lp])
        dma(gnext[:], ptab[l + 1][:, 0:8], writes=[gnext])
        P.emit()

    cast_rr = [0]

    def cast(out_ap, in_ap, reads, writes):
        e = ("dve", "act", "pool")[cast_rr[0] % 3]
        cast_rr[0] += 1
        if e == "dve":
            P.add("dve", lambda: nc.vector.tensor_copy(out_ap, in_ap), reads=reads, writes=writes)
        elif e == "act":
            P.add("act", lambda: nc.scalar.copy(out_ap, in_ap), reads=reads, writes=writes)
        else:
            P.add("pool", lambda: nc.gpsimd.tensor_copy(out_ap, in_ap), reads=reads, writes=writes)

    def load_w(dst, src2d, KC, c_lo, c_hi, stage, dst_c0=None, swap=None):
        if dst_c0 is None:
            dst_c0 = c_lo
        i = 0
        for kc in range(KC):
            for c0 in range(c_lo, c_hi, 2048):
                w = min(2048, c_hi - c0)
                st = stage[i % 2]
                i += 1
                dma(st[:, 0:w], src2d[kc * 128:(kc + 1) * 128, c0:c0 + w], writes=[st])
                d0 = dst_c0 + (c0 - c_lo)
                cast(dst[:, kc, d0:d0 + w], st[:, 0:w], [st], [dst])
                if swap is not None:
                    sv = st[:, 0:w].rearrange("p (h t e) -> p h t e", t=2, e=64)
                    ov = swap[:, kc, d0:d0 + w].rearrange("p (h t e) -> p h t e", t=2, e=64)
                    cast(ov[:, :, 0, :], sv[:, :, 1, :], [st], [swap])
                    cast(ov[:, :, 1, :], sv[:, :, 0, :], [st], [swap])

    def rms_fm(xT, g_ap, g_tile, out, N, sq, tmp, rstd):
        P.add("act", lambda: nc.scalar.activation(out=sq[:].rearrange("p a b -> p (a b)"),
                                                  in_=xT[:].rearrange("p a b -> p (a b)"), func=AF.Square),
              reads=[xT], writes=[sq])
        bk, br = next_bank()
        P.add("pe", partial(mm_group, bk[:, 0:N], [(ones[:], sq[:, k, :]) for k in range(8)]), reads=[sq, ones], writes=[br])
        P.add("act", lambda: nc.scalar.activation(out=tmp[:], in_=bk[:, 0:N], func=AF.Sqrt, bias=eps_t[:, 0:1], scale=1.0 / D),
              reads=[br, eps_t], writes=[tmp])
        P.add("dve", lambda: nc.vector.reciprocal(rstd[:], tmp[:]), reads=[tmp], writes=[rstd])
        for k in range(8):
            P.add("dve", lambda k=k: nc.vector.scalar_tensor_tensor(out=out[:, k, :], in0=xT[:, k, :], scalar=g_ap[:, k:k + 1],
                                                                    in1=rstd[:], op0=ALU.mult, op1=ALU.mult),
                  reads=[xT, rstd, g_tile], writes=[out])

    def phase0():
        m = A.mark()
        xin = [A.alloc("xin%d" % i, [1024], F32) for i in range(2)]
        xT = [A.alloc("xT%d" % i, [8, 512], F32) for i in range(2)]
        hT = [A.alloc("hT%d" % i, [8, 512], BF16) for i in range(2)]
        sq = A.alloc("sq", [8, 512], BF16)
        tmp = A.alloc("tmp", [512], F32)
        rstd = A.alloc("rstd", [512], F32)
        for u in range(NU):
            for b in range(8):
                xt = xT[b % 2]
                ht = hT[b % 2]
                for s in range(4):
                    xi = xin[s % 2]
                    r0 = b * 512 + s * 128
                    dma(xi[:], xU[u][r0:r0 + 128, :], writes=[xi])
                    j = (b * 4 + s) % 4
                    db = dbank(j)
                    P.add("pe", lambda db=db, xi=xi: [nc.tensor.transpose(db[:, c * 128:(c + 1) * 128], xi[:, c * 128:(c + 1) * 128], identf[:])
                                                       for c in range(8)][-1],
                          reads=[xi, identf], writes=[psr[2 * j], psr[2 * j + 1]])
                    e = "act" if s % 2 else "dve"
                    if e == "dve":
                        P.add("dve", lambda db=db, xt=xt, s=s: nc.vector.tensor_copy(xt[:, :, s * 128:(s + 1) * 128], db.rearrange("p (c t) -> p c t", t=128)),
                              reads=[psr[2 * j], psr[2 * j + 1]], writes=[xt])
                    else:
                        P.add("act", lambda db=db, xt=xt, s=s: nc.scalar.copy(xt[:, :, s * 128:(s + 1) * 128], db.rearrange("p (c t) -> p c t", t=128)),
                              reads=[psr[2 * j], psr[2 * j + 1]], writes=[xt])
                rms_fm(xt, gmix_ap, lp, ht, 512, sq, tmp, rstd)
                dma(xTsU[u][:, :, SL(b * 512, 512)].rearrange("c p t -> p c t"), xt[:], reads=[xt])
                dma(hTsU[u][:, :, SL(b * 512, 512)].rearrange("c p t -> p c t"), ht[:], reads=[ht])
            P.emit()
        A.reset(m)

    def phase1(l):
        m = A.mark()
        W = A.alloc("w_in", [8, DIN], BF16)
        Wsw = A.alloc("w_sw", [8, 1024], BF16)
        m2 = A.mark()
        stage = [A.alloc("wst%d" % i, [2048], F32) for i in range(2)]
        wsrc = w_in_c
        load_w(W, wsrc, 8, 0, DIN, stage)
        i = 0
        for kc in range(8):
            st = stage[i % 2]
            i += 1
            dma(st[:, 0:1024], wsrc[kc * 128:(kc + 1) * 128, C_RQ:C_RV], writes=[st])
            sv = st[:, 0:1024].rearrange("p (h t e) -> p h t e", t=2, e=64)
            ov = Wsw[:, kc, :].rearrange("p (h t e) -> p h t e", t=2, e=64)
            cast(ov[:, :, 0, :], sv[:, :, 1, :], [st], [Wsw])
            cast(ov[:, :, 1, :], sv[:, :, 0, :], [st], [Wsw])
        P.emit()
        A.reset(m2)
        hT = [A.alloc("hT%d" % i, [8, 512], BF16) for i in range(2)]
        cs = [A.alloc("cs%d" % i, [2, 512], F32) for i in range(2)]
        G = [A.alloc("G%d" % i, [4, 512], BF16) for i in range(6)]
        TM = [A.alloc("TM%d" % i, [2568], BF16) for i in range(2)]
        rt = [A.alloc("rt%d" % i, [512], F32) for i in range(4)]
        gi = [0]
        for u in range(NU):
            for tm in TM:
                P.add("pool", lambda tm=tm: nc.gpsimd.memset(tm[:, 0:520].rearrange("p (h e) -> p h e", e=65)[:, :, 64:65], 1.0), writes=[tm])

            def load(b):
                dma(hT[b % 2][:], hTsU[u][:, :, SL(b * 512, 512)].rearrange("c p t -> p c t"), writes=[hT[b % 2]])
                dma(cs[b % 2][:], rope_dU[u][:, :, SL(b * 512, 512)].rearrange("c p t -> p c t"), writes=[cs[b % 2]])

            load(0)
            for b in range(8):
                if b + 1 < 8:
                    load(b + 1)
                h = hT[b % 2]
                c_s = cs[b % 2]
                t0 = b * 512

                def fm_chunk(col, Wt=W):
                    bk, br = next_bank()
                    P.add("pe", partial(mm_group, bk, [(Wt[:, k, col:col + 128], h[:, k, :]) for k in range(8)]),
                          reads=[Wt, h], writes=[br])
                    return bk, br

                for (c0, dst, nchunk, func) in ((C_NAQ, naqT, 4, AF.Copy), (C_NAK, nakT, 4, AF.Copy),
                                               (C_GA, gTs, 8, AF.Sigmoid), (C_GR, gTs, 8, AF.Sigmoid)):
                    for g0 in range(0, nchunk, 4):
                        gt = G[gi[0] % 6]
                        gi[0] += 1
                        for j in range(4):
                            bk, br = fm_chunk(c0 + (g0 + j) * 128)
                            P.add("act", lambda bk=bk, gt=gt, j=j, func=func: nc.scalar.activation(out=gt[:, j, :], in_=bk, func=func),
                                  reads=[br], writes=[gt])
                        if dst is nakT:
                            dap = nakTU[u][:, :, SL(256 + t0, 512)]
                        elif dst is gTs:
                            cb = (0 if c0 == C_GA else 8) + g0
                            dap = gTsU[u][cb:cb + 4, :, SL(t0, 512)]
                        else:
                            dap = naqTU[u][:, :, SL(t0, 512)]
                        dma(dap.rearrange("c p t -> p c t"), gt[:], reads=[gt])
                for (c0, dst) in ((C_RQ, rqTU), (C_RK, rkTU)):
                    gt = G[gi[0] % 6]
                    gi[0] += 1
                    for j in range(4):
                        bk, br = fm_chunk(c0 + j * 128)
                        bk2, br2 = fm_chunk(c0 - C_RQ + j * 128, Wt=Wsw)
                        r1 = rt[(2 * j) % 4]
                        r2 = rt[(2 * j + 1) % 4]
                        P.add("dve", lambda bk=bk, r1=r1, c_s=c_s: nc.vector.tensor_tensor(out=r1[:], in0=bk, in1=c_s[:, 0, :], op=ALU.mult),
                              reads=[br, c_s], writes=[r1])
                        P.add("dve", lambda bk2=bk2, r2=r2, c_s=c_s: nc.vector.tensor_tensor(out=r2[:], in0=bk2, in1=c_s[:, 1, :], op=ALU.mult),
                              reads=[br2, c_s], writes=[r2])
                        P.add("pool", lambda r1=r1, r2=r2, gt=gt, j=j: nc.gpsimd.tensor_tensor(out=gt[:, j, :], in0=r1[:], in1=r2[:], op=ALU.add),
                              reads=[r1, r2], writes=[gt])
                    dma(dst[u][:, :, SL(t0, 512)].rearrange("c p t -> p c t"), gt[:], reads=[gt])
                for s in range(4):
                    tm = TM[s % 2]
                    for gidx, col in enumerate((C_NAV, C_RV, C_RV + 512, C_RG, C_RG + 512)):
                        bk, br = next_bank()
                        P.add("pe", partial(mm_group, bk, [(h[:, k, s * 128:(s + 1) * 128], W[:, k, col:col + 512]) for k in range(8)]),
                              reads=[W, h], writes=[br])
                        if gidx == 0:
                            o = tm[:, 0:520].rearrange("p (h e) -> p h e", e=65)[:, :, 0:64]
                            P.add("dve", lambda bk=bk, o=o: nc.vector.tensor_copy(o, bk.rearrange("p (h e) -> p h e", e=64)), reads=[br], writes=[tm])
                            continue
                        o = tm[:, 8 + gidx * 512:8 + (gidx + 1) * 512]
                        if gidx >= 3:
                            P.add("act", lambda bk=bk, o=o: nc.scalar.activation(out=o, in_=bk, func=AF.Silu), reads=[br], writes=[tm])
                        else:
                            P.add("dve", lambda bk=bk, o=o: nc.vector.tensor_copy(o, bk), reads=[br], writes=[tm])
                    r0 = t0 + s * 128
                    dma(navsU[u][SL(256 + r0, 128), :], tm[:, 0:520], reads=[tm])
                    dma(rvsU[u][SL(r0, 128), :], tm[:, 520:1544], reads=[tm])
                    dma(rgsU[u][SL(r0, 128), :], tm[:, 1544:2568], reads=[tm])
            P.emit()
        A.reset(m)

    def phase_na(l):
        m = A.mark()
        kT = A.alloc("kT", [4, 4608], BF16)
        qT = A.alloc("qT", [4, 4096], BF16)
        V = A.alloc("V", [36, 8, 65], BF16)
        TA = A.alloc("TA", [8, 768], F32)
        TB = A.alloc("TB", [8, 768], F32)
        tmp = [A.alloc("natmp%d" % i, [768], F32) for i in range(2)]
        E = [A.alloc("naE%d" % i, [768], BF16) for i in range(2)]
        atok = [A.alloc("atok%d" % i, [8, 64], BF16) for i in range(2)]
        rden = [A.alloc("rden%d" % i, [8], F32) for i in range(2)]
        aTb = [A.alloc("aTb%d" % i, [4, 512], BF16) for i in range(2)]
        dma(TA[:], ta_c[0], writes=[TA])
        dma(TB[:], ta_c[1], writes=[TB])
        P.emit()
        cnt = 0
        S_res = [[psr[0], psr[1]], [psr[2], psr[3]]]
        PV_res = [psr[4], psr[5]]
        PV = dbank(2).rearrange("p (h e) -> p h e", e=128)
        psT = bank(6).bitcast(BF16)
        for u in range(NU):
            cnt = 0
            dma(qxs[0:12, :], qxU[u], writes=[qxs])
            P.add("dve", lambda: nc.vector.tensor_copy(qx[0:12, :], qxs[0:12, :]), reads=[qxs], writes=[qx])
            for c in range(4):
                dma(kT[:, c, :], nakTU[u][c, :, :], writes=[kT])
                dma(qT[:, c, :], naqTU[u][c, :, :], writes=[qT])
            for g0 in range(0, 36, 9):
                dma(V[:, g0:g0 + 9, :, :].rearrange("p g h e -> p g (h e)"), navsU[u][SL(g0 * 128, 9 * 128), :].rearrange("(g p) e -> p g e", p=128),
                    writes=[V])
            for p in range(32):
                if p == 0:
                    tp, Tt, nch, kt0 = 1, TA, 6, 0
                elif p == 1:
                    tp, Tt, nch, kt0 = 2, TB, 6, 0
                elif p == 30:
                    tp, Tt, nch, kt0 = 3, TA, 6, 30
                elif p == 31:
                    tp, Tt, nch, kt0 = 4, TB, 6, 30
                else:
                    tp, Tt, nch, kt0 = 0, TA, 5, p
                qxo = tp * 128
                for h in range(8):
                    hc, hp = h // 2, (h % 2) * 64
                    si = cnt % 2
                    cnt += 1
                    S = dbank(si)

                    def s_mm(S=S, hc=hc, hp=hp, nch=nch, kt0=kt0, p=p, qxo=qxo):
                        ins = None
                        for mth in range(nch):
                            o = S[:, mth * 128:(mth + 1) * 128]
                            nc.tensor.matmul(o, kT[hp:hp + 64, hc, (kt0 + mth) * 128:(kt0 + mth + 1) * 128],
                                             qT[hp:hp + 64, hc, p * 128:(p + 1) * 128], start=True, stop=False)
                            ins = nc.tensor.matmul(o, kx[0:12, mth * 128:(mth + 1) * 128], qx[0:12, qxo:qxo + 128], start=False, stop=True)
                        return ins
                    P.add("pe", s_mm, reads=[kT, qT, kx, qx], writes=S_res[si])
                    tm_, e_ = tmp[si], E[si]
                    w = nch * 128
                    P.add("dve", lambda S=S, tm_=tm_, Tt=Tt, h=h, w=w: nc.vector.scalar_tensor_tensor(
                        out=tm_[:, 0:w], in0=S[:, 0:w], scalar=0.125, in1=Tt[:, h, 0:w], op0=ALU.mult, op1=ALU.add),
                        reads=S_res[si] + [Tt.res], writes=[tm_])
                    P.add("act", lambda tm_=tm_, e_=e_, w=w: nc.scalar.activation(out=e_[:, 0:w], in_=tm_[:, 0:w], func=AF.Exp),
                          reads=[tm_], writes=[e_])
                    P.add("pe", partial(mm_group, PV[:, h, 0:65],
                                        [(e_[:, mth * 128:(mth + 1) * 128], V[:, kt0 + mth, h, :]) for mth in range(nch)]),
                          reads=[e_, V], writes=PV_res)
                at, rd = atok[p % 2], rden[p % 2]
                P.add("dve", lambda rd=rd: nc.vector.reciprocal(rd[:], PV[:, :, 64]), reads=PV_res, writes=[rd])
                P.add("dve", lambda at=at, rd=rd: nc.vector.tensor_tensor(out=at[:], in0=PV[:, :, 0:64],
                                                                           in1=rd[:].unsqueeze(2).to_broadcast([128, 8, 64]), op=ALU.mult),
                      reads=PV_res + [rd.res], writes=[at])
                atf = at[:].rearrange("p h e -> p (h e)")
                P.add("pe", lambda atf=atf: [nc.tensor.transpose(psT[:, c * 128:(c + 1) * 128], atf[:, c * 128:(c + 1) * 128], ident[:])
                                             for c in range(4)][-1], reads=[at, ident], writes=[psr[6]])
                blk = p // 4
                ab = aTb[blk % 2]
                P.add("act", lambda ab=ab, p=p: nc.scalar.copy(ab[:, :, (p % 4) * 128:(p % 4 + 1) * 128],
                                                                psT[:, 0:512].rearrange("p (c t) -> p c t", t=128)),
                      reads=[psr[6]], writes=[ab])
                if p % 4 == 3:
                    dma(aTsU[u][:, :, SL(blk * 512, 512)].rearrange("c p t -> p c t"), ab[:], reads=[ab])
            P.emit()
        A.reset(m)

    def phase_ret(l):
        m = A.mark()
        DT = A.alloc("DT", [4, 128], F32)
        QDF = A.alloc("QDF", [4, 128], F32)
        QDB = A.alloc("QDB", [4, 128], F32)
        kdf = A.alloc("kdf", [4], F32)
        kdb = A.alloc("kdb", [4], F32)
        gcf = A.alloc("gcf", [4], F32)
        gcb = A.alloc("gcb", [4], F32)
        gbc = A.alloc("gbc", [1024], F32)
        St = A.alloc("St", [4, 256], F32)
        Sbf = [A.alloc("Sbf%d" % i, [4, 256], BF16) for i in range(2)]
        arg = A.alloc("arg", [128], F32)
        lg = lp
        lgf = lambda h: lp[:, 148 + h:148 + h + 1]
        lgb = lambda h: lp[:, 152 + h:152 + h + 1]
        tabs = [DT, QDF, QDB, kdf, kdb, gcf, gcb]
        for h in range(4):
            P.add("dve", lambda h=h: nc.vector.tensor_scalar(arg[:], ctab[:, CT_M1:CT_M1 + 128], lgf(h), None, ALU.mult),
                  reads=[ctab, lg], writes=[arg])
            P.add("dve", lambda h=h: nc.vector.scalar_tensor_tensor(out=arg[:], in0=ctab[:, CT_M2:CT_M2 + 128], scalar=lgb(h), in1=arg[:],
                                                                    op0=ALU.mult, op1=ALU.add), reads=[ctab, lg, arg], writes=[arg])
            P.add("act", lambda h=h: nc.scalar.activation(out=DT[:, h, :], in_=arg[:], func=AF.Exp), reads=[arg], writes=[DT])
            P.add("act", lambda h=h: nc.scalar.activation(out=QDF[:, h, :], in_=ctab[:, CT_I1:CT_I1 + 128], func=AF.Exp, scale=lgf(h)),
                  reads=[ctab, lg], writes=[QDF])
            P.add("act", lambda h=h: nc.scalar.activation(out=QDB[:, h, :], in_=ctab[:, CT_I2:CT_I2 + 128], func=AF.Exp, scale=lgb(h)),
                  reads=[ctab, lg], writes=[QDB])
            P.add("act", lambda h=h: nc.scalar.activation(out=kdf[:, h:h + 1], in_=ctab[:, CT_J1:CT_J1 + 1], func=AF.Exp, scale=lgf(h)),
                  reads=[ctab, lg], writes=[kdf])
            P.add("act", lambda h=h: nc.scalar.activation(out=kdb[:, h:h + 1], in_=ctab[:, CT_J2:CT_J2 + 1], func=AF.Exp, scale=lgb(h)),
                  reads=[ctab, lg], writes=[kdb])
            P.add("act", lambda h=h: nc.scalar.activation(out=gcf[:, h:h + 1], in_=lgf(h), func=AF.Exp, scale=128.0), reads=[lg], writes=[gcf])
            P.add("act", lambda h=h: nc.scalar.activation(out=gcb[:, h:h + 1], in_=lgb(h), func=AF.Exp, scale=128.0), reads=[lg], writes=[gcb])
        for t_ in (DT, kdf, kdb):
            P.add("dve", lambda t_=t_: nc.vector.tensor_scalar(t_[:], t_[:], RSCALE, None, ALU.mult), reads=[t_], writes=[t_])
        dma(gbc[:], rng_c.partition_broadcast(128), writes=[gbc])
        P.emit()

        kTc = [A.alloc("kTc%d" % i, [4, 128], BF16) for i in range(2)]
        qTc = [A.alloc("qTc%d" % i, [4, 128], BF16) for i in range(2)]
        vc = [A.alloc("vc%d" % i, [1024], BF16) for i in range(2)]
        gsc = [A.alloc("gsc%d" % i, [1024], BF16) for i in range(2)]
        Bc = [A.alloc("Bc%d" % i, [4, 256], BF16) for i in range(2)]
        kd = [A.alloc("kd%d" % i, [4, 128], BF16) for i in range(2)]
        psT = bank(0).bitcast(BF16)[:, 0:512].rearrange("p (h d) -> p h d", d=128)
        psT_r = [psr[0]]
        psS = bank(1).rearrange("p (h i) -> p h i", i=128)
        psS_r = [psr[1]]
        psO = dbank(1).rearrange("p (h v) -> p h v", v=256)
        psO_r = [psr[2], psr[3]]
        psF = dbank(2).rearrange("p (h v) -> p h v", v=256)
        psF_r = [psr[4], psr[5]]
        psR = bank(6).bitcast(BF16).rearrange("p (c t) -> p c t", t=128)
        psR_r = [psr[6]]

        def state_update(gc, k_d, v_):
            P.add("pe", lambda: [nc.tensor.matmul(psF[:, h, :], k_d[:, h, :], v_[:, h * 256:(h + 1) * 256], start=True, stop=True)
                                 for h in range(4)][-1], reads=[k_d, v_], writes=psF_r)
            P.add("dve", lambda: nc.vector.tensor_tensor(out=St[:], in0=St[:], in1=gc[:].unsqueeze(2).to_broadcast([128, 4, 256]), op=ALU.mult),
                  reads=[St, gc], writes=[St])
            P.add("dve", lambda: nc.vector.tensor_tensor(out=St[:], in0=St[:], in1=psF, op=ALU.add), reads=[St] + psF_r, writes=[St])

        def k_tokmajor(kt, kdec, out):
            P.add("pe", lambda: [nc.tensor.transpose(psT[:, h, :], kt[:, h, :], ident[:]) for h in range(4)][-1],
                  reads=[kt, ident], writes=psT_r)
            P.add("dve", lambda: nc.vector.tensor_tensor(out=out[:], in0=psT, in1=kdec[:].unsqueeze(2).to_broadcast([128, 4, 128]), op=ALU.mult),
                  reads=psT_r + [kdec.res], writes=[out])

        P.add("pool", lambda: nc.gpsimd.memset(St[:], 0.0), writes=[St])
        P.emit()
        for ui in range(NU):
            u = (NU - 1) - ui
            load_unit_params(u)

            def bload(c):
                dma(kTc[c % 2][:], rkTU[u][:, :, SL(c * 128, 128)].rearrange("h p t -> p h t"), writes=[kTc[c % 2]])
                dma(vc[c % 2][:], rvsU[u][SL(c * 128, 128), :], writes=[vc[c % 2]])
            bload(31)
            for c in range(31, -1, -1):
                if c - 1 >= 0:
                    bload(c - 1)
                if c == 31:
                    P.add("dve", lambda: nc.vector.tensor_scalar(St[:], St[:], lk2[:, 1:2], None, ALU.mult),
                          reads=[St, lk2], writes=[St])
                sb = Sbf[c % 2]
                P.add("act", lambda sb=sb: nc.scalar.copy(sb[:], St[:]), reads=[St], writes=[sb])
                dma(BstU[u][c].rearrange("p (h v) -> p h v", v=256), sb[:], reads=[sb])
                k_tokmajor(kTc[c % 2], kdb, kd[c % 2])
                state_update(gcb, kd[c % 2], vc[c % 2])
            P.emit()

        AT = [A.alloc("AT%d" % i, [4, 128], BF16) for i in range(2)]
        qdf = [A.alloc("qdf%d" % i, [4, 128], BF16) for i in range(2)]
        qdb = [A.alloc("qdb%d" % i, [4, 128], BF16) for i in range(2)]
        stt = A.alloc("bnst", [4, 6], F32)
        mv = A.alloc("bnmv", [4, 2], F32)
        sd = A.alloc("bnsd", [4], F32)
        rs_ = A.alloc("bnrs", [4], F32)
        yn = A.alloc("yn", [1024], F32)
        gs = A.alloc("gs", [1024], F32)
        ro = [A.alloc("ro%d" % i, [1024], BF16) for i in range(2)]
        roT = [A.alloc("roT%d" % i, [8, 512], BF16) for i in range(2)]
        P.add("pool", lambda: nc.gpsimd.memset(St[:], 0.0), writes=[St])
        P.emit()
        Fb = Sbf[0]
        for u in range(NU):
            load_unit_params(u)

            def fload(c):
                i = c % 2
                dma(kTc[i][:], rkTU[u][:, :, SL(c * 128, 128)].rearrange("h p t -> p h t"), writes=[kTc[i]])
                dma(qTc[i][:], rqTU[u][:, :, SL(c * 128, 128)].rearrange("h p t -> p h t"), writes=[qTc[i]])
                dma(vc[i][:], rvsU[u][SL(c * 128, 128), :], writes=[vc[i]])
                dma(gsc[i][:], rgsU[u][SL(c * 128, 128), :], writes=[gsc[i]])
                dma(Bc[i][:], BstU[u][c].rearrange("p (h v) -> p h v", v=256), writes=[Bc[i]])
            fload(0)
            for c in range(32):
                if c + 1 < 32:
                    fload(c + 1)
                i = c % 2
                kt, qt, v_, g_, bc = kTc[i], qTc[i], vc[i], gsc[i], Bc[i]
                if c == 0:
                    P.add("dve", lambda: nc.vector.tensor_scalar(St[:], St[:], lk2[:, 0:1], None, ALU.mult),
                          reads=[St, lk2], writes=[St])
                P.add("act", lambda: nc.scalar.copy(Fb[:], St[:]), reads=[St], writes=[Fb])
                P.add("pe", lambda kt=kt, qt=qt: [nc.tensor.matmul(psS[:, h, :], kt[:, h, :], qt[:, h, :], start=True, stop=True) for h in range(4)][-1],
                      reads=[kt, qt], writes=psS_r)
                at = AT[i]
                P.add("dve", lambda at=at: nc.vector.tensor_tensor(out=at[:], in0=psS, in1=DT[:], op=ALU.mult), reads=psS_r + [DT.res], writes=[at])
                qf, qb = qdf[i], qdb[i]
                P.add("pool", lambda qt=qt, qf=qf: nc.gpsimd.tensor_tensor(out=qf[:], in0=qt[:], in1=QDF[:], op=ALU.mult), reads=[qt, QDF], writes=[qf])
                P.add("pool", lambda qt=qt, qb=qb: nc.gpsimd.tensor_tensor(out=qb[:], in0=qt[:], in1=QDB[:], op=ALU.mult), reads=[qt, QDB], writes=[qb])

                def o_mm(at=at, qf=qf, qb=qb, v_=v_, bc=bc):
                    ins = None
                    for h in range(4):
                        nc.tensor.matmul(psO[:, h, :], at[:, h, :], v_[:, h * 256:(h + 1) * 256], start=True, stop=False)
                        nc.tensor.matmul(psO[:, h, :], qf[:, h, :], Fb[:, h, :], start=False, stop=False)
                        ins = nc.tensor.matmul(psO[:, h, :], qb[:, h, :], bc[:, h, :], start=False, stop=True)
                    return ins
                P.add("pe", o_mm, reads=[at, qf, qb, v_, bc, Fb], writes=psO_r)
                k_tokmajor(kt, kdf, kd[i])
                state_update(gcf, kd[i], v_)
                def bn():
                    for h in range(4):
                        nc.vector.bn_stats(stt[:, h, :], psO[:, h, :])
                    ins = None
                    for h in range(4):
                        ins = nc.vector.bn_aggr(mv[:, h, :], stt[:, h, :])
                    return ins
                P.add("dve", bn, reads=psO_r, writes=[stt, mv])
                P.add("act", lambda: nc.scalar.activation(out=sd[:], in_=mv[:, :, 1], func=AF.Sqrt, bias=eps_t[:, 0:1], scale=1.0),
                      reads=[mv, eps_t], writes=[sd])
                P.add("dve", lambda: nc.vector.reciprocal(rs_[:], sd[:]), reads=[sd], writes=[rs_])
                for h in range(4):
                    P.add("dve", lambda h=h: nc.vector.tensor_scalar(yn[:, h * 256:(h + 1) * 256], psO[:, h, :], mv[:, h, 0:1], rs_[:, h:h + 1],
                                                                     ALU.subtract, ALU.mult), reads=psO_r + [mv.res, rs_.res], writes=[yn])
                P.add("pool", lambda g_=g_: nc.gpsimd.tensor_tensor(out=gs[:], in0=g_[:], in1=gbc[:], op=ALU.mult), reads=[g_, gbc], writes=[gs])
                r_ = ro[i]
                P.add("pool", lambda r_=r_: nc.gpsimd.tensor_tensor(out=r_[:], in0=yn[:], in1=gs[:], op=ALU.mult), reads=[yn, gs], writes=[r_])
                P.add("pe", lambda r_=r_: [nc.tensor.transpose(psR[:, c8, :], r_[:, c8 * 128:(c8 + 1) * 128], ident[:]) for c8 in range(8)][-1],
                      reads=[r_, ident], writes=psR_r)
                blk = c // 4
                rT = roT[blk % 2]
                P.add("act", lambda rT=rT, c=c: nc.scalar.copy(rT[:, :, (c % 4) * 128:(c % 4 + 1) * 128], psR), reads=psR_r, writes=[rT])
                if c % 4 == 3:
                    dma(roTsU[u][:, :, SL(blk * 512, 512)].rearrange("c p t -> p c t"), rT[:], reads=[rT])
            P.emit()
        A.reset(m)
    def phase3a(l):
        m = A.mark()
        Wa = A.alloc("w_ba", [4, D], BF16)
        Wr = A.alloc("w_br", [8, D], BF16)
        Wo = A.alloc("w_o", [8, D], BF16)
        m2 = A.mark()
        stage = [A.alloc("wst%d" % i, [2048], F32) for i in range(2)]
        load_w(Wa, w_ba_c, 4, 0, D, stage)
        load_w(Wr, w_br_c, 8, 0, D, stage)
        load_w(Wo, w_out_c, 8, 0, D, stage)
        P.emit()
        A.reset(m2)
        xT = [A.alloc("xT%d" % i, [8, 512], F32) for i in range(2)]
        aT = [A.alloc("aT%d" % i, [4, 512], BF16) for i in range(2)]
        rT = [A.alloc("rT%d" % i, [8, 512], BF16) for i in range(2)]
        gT = [A.alloc("gT%d" % i, [16, 512], BF16) for i in range(2)]
        mixed = A.alloc("mixed", [8, 512], BF16)
        t1 = [A.alloc("t1_%d" % i, [512], F32) for i in range(2)]
        t2 = [A.alloc("t2_%d" % i, [512], F32) for i in range(2)]
        sq = A.alloc("sq", [8, 512], BF16)
        h2 = [A.alloc("h2_%d" % i, [8, 512], BF16) for i in range(2)]
        tmp = A.alloc("tmp", [512], F32)
        rstd = A.alloc("rstd", [512], F32)

        for u in range(NU):
            def load(b):
                i = b % 2
                sl = SL(b * 512, 512)
                dma(aT[i][:], aTsU[u][:, :, sl].rearrange("c p t -> p c t"), writes=[aT[i]])
                dma(rT[i][:], roTsU[u][:, :, sl].rearrange("c p t -> p c t"), writes=[rT[i]])
                dma(gT[i][:], gTsU[u][:, :, sl].rearrange("c p t -> p c t"), writes=[gT[i]])
                dma(xT[i][:], xTsU[u][:, :, sl].rearrange("c p t -> p c t"), writes=[xT[i]])
            load(0)
            for b in range(8):
                if b + 1 < 8:
                    load(b + 1)
                i = b % 2
                a_, r_, g_, x_ = aT[i], rT[i], gT[i], xT[i]
                for oc in range(8):
                    bka, bra = next_bank()
                    P.add("pe", partial(mm_group, bka, [(Wa[:, k, oc * 128:(oc + 1) * 128], a_[:, k, :]) for k in range(4)]), reads=[Wa, a_], writes=[bra])
                    bkr, brr = next_bank()
                    P.add("pe", partial(mm_group, bkr, [(Wr[:, k, oc * 128:(oc + 1) * 128], r_[:, k, :]) for k in range(8)]), reads=[Wr, r_], writes=[brr])
                    u1, u2 = t1[oc % 2], t2[oc % 2]
                    P.add("dve", lambda bka=bka, u1=u1, oc=oc, g_=g_: nc.vector.tensor_tensor(out=u1[:], in0=bka, in1=g_[:, oc, :], op=ALU.mult),
                          reads=[bra, g_], writes=[u1])
                    P.add("dve", lambda bkr=bkr, u2=u2, oc=oc, g_=g_: nc.vector.tensor_tensor(out=u2[:], in0=bkr, in1=g_[:, 8 + oc, :], op=ALU.mult),
                          reads=[brr, g_], writes=[u2])
                    P.add("pool", lambda u1=u1, u2=u2, oc=oc: nc.gpsimd.tensor_tensor(out=mixed[:, oc, :], in0=u1[:], in1=u2[:], op=ALU.add),
                          reads=[u1, u2], writes=[mixed])
                for oc in range(8):
                    bk, br = next_bank()
                    P.add("pe", partial(mm_group, bk, [(Wo[:, k, oc * 128:(oc + 1) * 128], mixed[:, k, :]) for k in range(8)]), reads=[Wo, mixed], writes=[br])
                    P.add("dve", lambda bk=bk, oc=oc, x_=x_: nc.vector.tensor_tensor(out=x_[:, oc, :], in0=x_[:, oc, :], in1=bk, op=ALU.add),
                          reads=[br, x_], writes=[x_])
                rms_fm(x_, gffn_ap, lp, h2[i], 512, sq, tmp, rstd)
                dma(xTsU[u][:, :, SL(b * 512, 512)].rearrange("c p t -> p c t"), x_[:], reads=[x_])
                dma(h2TsU[u][:, :, SL(64 + b * 512, 512)].rearrange("c p t -> p c t"), h2[i][:], reads=[h2[i]])
            P.emit()
        A.reset(m)

    def phase3b(l):
        m = A.mark()
        Wu = A.alloc("w_up", [8, 2 * DFF], BF16)
        Wd = A.alloc("w_dn", [22, D], BF16)
        m2 = A.mark()
        stage = [A.alloc("wst%d" % i, [2048], F32) for i in range(2)]
        load_w(Wu, w_up_c, 8, 0, 2 * DFF, stage)
        load_w(Wd, w_dn_c, 22, 0, D, stage)
        P.emit()
        A.reset(m2)
        N = 256
        xT = [A.alloc("xT%d" % i, [8, N], F32) for i in range(2)]
        h2 = [A.alloc("h2_%d" % i, [8, N + 2], BF16) for i in range(2)]
        act_ = A.alloc("act", [22, N], BF16)
        cg = [A.alloc("cg%d" % i, [N], F32) for i in range(2)]
        cv = [A.alloc("cv%d" % i, [N], F32) for i in range(2)]
        tt = [A.alloc("tt%d" % i, [N], F32) for i in range(2)]
        sg = [A.alloc("sg%d" % i, [N], F32) for i in range(2)]
        sq = A.alloc("sq", [8, N], BF16)
        tmp = A.alloc("tmp", [N], F32)
        rstd = A.alloc("rstd", [N], F32)
        hT = [A.view(act_, [8, N], BF16)]
        convw = convw_v
        for u in range(NU):
            load_unit_params(u)

            def load(fb):
                i = fb % 2
                dma(h2[i][:], h2TsU[u][:, :, SL(63 + fb * N, N + 2)].rearrange("c p t -> p c t"), writes=[h2[i]])
                dma(xT[i][:], xTsU[u][:, :, SL(fb * N, N)].rearrange("c p t -> p c t"), writes=[xT[i]])
            load(0)
            for fb in range(UT // N):
                if fb + 1 < UT // N:
                    load(fb + 1)
                i = fb % 2
                h_, x_ = h2[i], xT[i]
                t0 = fb * N
                if fb == 0:
                    P.add("dve", lambda h_=h_: nc.vector.tensor_scalar(h_[:, :, 0], h_[:, :, 0], lk2[:, 0:1], None, ALU.mult),
                          reads=[h_, lk2], writes=[h_])
                if fb == UT // N - 1:
                    P.add("dve", lambda h_=h_: nc.vector.tensor_scalar(h_[:, :, N + 1], h_[:, :, N + 1], lk2[:, 1:2], None, ALU.mult),
                          reads=[h_, lk2], writes=[h_])
                for fc in range(22):
                    res = []
                    for (col, dst) in ((fc * 128, cg[fc % 2]), (DFF + fc * 128, cv[fc % 2])):
                        bk, br = next_bank()
                        P.add("pe", partial(mm_group, bk[:, 0:N + 2], [(Wu[:, k, col:col + 128], h_[:, k, :]) for k in range(8)]), reads=[Wu, h_], writes=[br])
                        cc = col // 128
                        P.add("act", lambda bk=bk, dst=dst, cc=cc: nc.scalar.activation(out=dst[:], in_=bk[:, 1:N + 1], func=AF.Copy, scale=convw[:, 1, cc:cc + 1]),
                              reads=[br, lp], writes=[dst])
                        P.add("dve", lambda bk=bk, dst=dst, cc=cc: nc.vector.scalar_tensor_tensor(out=dst[:], in0=bk[:, 0:N], scalar=convw[:, 0, cc:cc + 1],
                                                                                                  in1=dst[:], op0=ALU.mult, op1=ALU.add),
                              reads=[br, lp, dst], writes=[dst])
                        P.add("dve", lambda bk=bk, dst=dst, cc=cc: nc.vector.scalar_tensor_tensor(out=dst[:], in0=bk[:, 2:N + 2], scalar=convw[:, 2, cc:cc + 1],
                                                                                                  in1=dst[:], op0=ALU.mult, op1=ALU.add),
                              reads=[br, lp, dst], writes=[dst])
                    g_, v_, t_, s_ = cg[fc % 2], cv[fc % 2], tt[fc % 2], sg[fc % 2]
                    P.add("pool", lambda g_=g_, t_=t_: nc.gpsimd.tensor_tensor(out=t_[:], in0=g_[:], in1=g_[:], op=ALU.mult), reads=[g_], writes=[t_])
                    P.add("pool", lambda t_=t_: nc.gpsimd.tensor_scalar(t_[:], t_[:], 0.044715, 1.0, ALU.mult, ALU.add), reads=[t_], writes=[t_])
                    P.add("pool", lambda g_=g_, t_=t_: nc.gpsimd.tensor_tensor(out=t_[:], in0=t_[:], in1=g_[:], op=ALU.mult), reads=[g_, t_], writes=[t_])
                    P.add("act", lambda t_=t_, s_=s_: nc.scalar.activation(out=s_[:], in_=t_[:], func=AF.Sigmoid, scale=GELU_C), reads=[t_], writes=[s_])
                    P.add("pool", lambda g_=g_, v_=v_: nc.gpsimd.tensor_tensor(out=v_[:], in0=g_[:], in1=v_[:], op=ALU.mult), reads=[g_, v_], writes=[v_])
                    P.add("dve", lambda s_=s_, v_=v_, fc=fc: nc.vector.tensor_tensor(out=act_[:, fc, :], in0=s_[:], in1=v_[:], op=ALU.mult),
                          reads=[s_, v_], writes=[act_])
                for oc in range(8):
                    bk, br = next_bank()
                    P.add("pe", partial(mm_group, bk[:, 0:N], [(Wd[:, k, oc * 128:(oc + 1) * 128], act_[:, k, :]) for k in range(22)]), reads=[Wd, act_], writes=[br])
                    P.add("dve", lambda bk=bk, oc=oc, x_=x_: nc.vector.tensor_tensor(out=x_[:, oc, :], in0=x_[:, oc, :], in1=bk[:, 0:N], op=ALU.add),
                          reads=[br, x_], writes=[x_])
                ho = hT[0]
                rms_fm(x_, gnext[:, :], gnext, ho, N, sq, tmp, rstd)
                dma(xTsU[u][:, :, SL(t0, N)].rearrange("c p t -> p c t"), x_[:], reads=[x_])
                dma(hTsU[u][:, :, SL(t0, N)].rearrange("c p t -> p c t"), ho[:], reads=[ho])
            P.emit()
        A.reset(m)

    def phase_final():
        m = A.mark()
        N = 512
        xT = [A.alloc("xT%d" % i, [8, N], F32) for i in range(2)]
        yT = A.alloc("yT", [8, N], F32)
        ytok = [A.alloc("ytok%d" % i, [1024], F32) for i in range(2)]
        sq = A.alloc("sq", [8, N], BF16)
        tmp = A.alloc("tmp", [N], F32)
        rstd = A.alloc("rstd", [N], F32)
        for u in range(NU):
            def load(fb):
                dma(xT[fb % 2][:], xTsU[u][:, :, SL(fb * N, N)].rearrange("c p t -> p c t"), writes=[xT[fb % 2]])
            load(0)
            for fb in range(UT // N):
                if fb + 1 < UT // N:
                    load(fb + 1)
                x_ = xT[fb % 2]
                t0 = fb * N
                rms_fm(x_, gfin[:, :], gfin, yT, N, sq, tmp, rstd)
                for s in range(N // 128):
                    j = s % 2
                    db = dbank(j)
                    P.add("pe", lambda db=db, s=s: [nc.tensor.transpose(db[:, c * 128:(c + 1) * 128], yT[:, c, s * 128:(s + 1) * 128], identf[:])
                                                     for c in range(8)][-1], reads=[yT, identf], writes=[psr[2 * j], psr[2 * j + 1]])
                    yt = ytok[s % 2]
                    P.add("act", lambda db=db, yt=yt: nc.scalar.copy(yt[:], db), reads=[psr[2 * j], psr[2 * j + 1]], writes=[yt])
                    dma(yU[u][SL(t0 + s * 128, 128), :], yt[:], reads=[yt])
            P.emit()
        A.reset(m)

    setup()
    load_layer_params(0)
    phase0()
    with nc.Fori(0, L) as l:
        load_layer_params(l)
        phase1(l)
        phase_na(l)
        phase_ret(l)
        phase3a(l)
        phase3b(l)
    phase_final()
    stats = dict(P.tot)
    stats["sbuf_peak"] = A.peak
    return nc, stats


def _rope_tables(pos):
    f32 = np.float32
    inv_freq = (f32(10000.0) ** (-(np.arange(0, 128, 2, dtype=f32) / f32(128)))).astype(f32)
    ang = (pos.astype(f32)[None, :] * inv_freq[:, None]).astype(f32)
    c = np.cos(ang).astype(f32)
    s = np.sin(ang).astype(f32)
    return np.stack([np.concatenate([c, c], 0), np.concatenate([-s, s], 0)], 0)


def _ctab():
    f32 = np.float32
    j = np.arange(128)[:, None]
    i = np.arange(128)[None, :]
    t = np.zeros((128, CT_N), f32)
    t[:, CT_M1:CT_M1 + 128] = np.maximum(i - j, 0)
    t[:, CT_M2:CT_M2 + 128] = np.maximum(j - i, 0)
    t[:, CT_I1:CT_I1 + 128] = np.broadcast_to(i + 1, (128, 128))
    t[:, CT_I2:CT_I2 + 128] = np.broadcast_to(128 - i, (128, 128))
    t[:, CT_J1] = 127 - np.arange(128)
    t[:, CT_J2] = np.arange(128)
    return t


def _kx():
    k = np.zeros((12, 768), np.float32)
    for m in range(6):
        for a in range(2):
            k[2 * m + a, m * 128 + a * 64:m * 128 + a * 64 + 64] = 1.0
    return k


def _qx(link):
    NU = len(link) - 1
    BIG = 8.0 * NEG
    q = np.full((12, NU, 5, 2, 64), BIG, np.float32)
    for u in range(NU):
        lp, ln = link[u] > 0, link[u + 1] > 0
        for a in range(2):
            q[a:a + 8, u, 0, a, :] = 0.0
            rng = (a, a + 8) if lp else (4, 12)
            q[rng[0]:rng[1], u, 1, a, :] = 0.0
            rng = (a + 2, a + 10) if lp else (4, 12)
            q[rng[0]:rng[1], u, 2, a, :] = 0.0
            rng = (a, a + 8) if ln else (0, 8)
            q[rng[0]:rng[1], u, 3, a, :] = 0.0
            rng = (a + 2, a + 10) if ln else (0, 8)
            q[rng[0]:rng[1], u, 4, a, :] = 0.0
    return q.reshape(12, NU * 5 * 128)


def _ta(rel_bias):
    L = rel_bias.shape[0]
    ap, cp, mm, a, c = np.meshgrid(np.arange(2), np.arange(64), np.arange(6), np.arange(2), np.arange(64), indexing="ij")
    cs = np.clip(c - 8, 0, 48)
    colok = (cp >= cs) & (cp < cs + 16)
    ci = np.clip(cp - c + 15, 0, 30)
    out = np.empty((L, 2, 128, 8, 768), np.float32)
    for t, off in enumerate((4, 6)):
        j = (2 * mm + ap) - (off + a) + 7
        assert j.min() >= 0 and j.max() <= 14
        g = rel_bias[:, :, j, ci]
        g = np.where(colok[None, None], g, np.float32(NEG))
        out[:, t] = np.transpose(g.reshape(L, 8, 128, 768), (0, 2, 1, 3))
    return out


_CACHE = {}


def _get_prog(NU, L):
    key = (NU, L)
    if key not in _CACHE:
        _CACHE[key] = build(NU, L)
    return _CACHE[key]


def make_in_maps(inputs, core_tokens, core_pos, core_links, L):
    f32 = np.float32
    shared = {
        "kx": _kx(), "ctab": _ctab(),
        "wall": np.concatenate([np.asarray(inputs[k][:L], f32).reshape(L, -1) for k in
                                ("w_in", "w_branch_attn", "w_branch_ret", "w_out", "w_up", "w_down")]
                               + [_ta(np.asarray(inputs["na_rel_bias"], f32)[:L]).reshape(L, -1),
                                  np.asarray(inputs["ret_norm_g"][:L], f32).reshape(L, -1)], axis=1),
        "norm_mix_g": np.ascontiguousarray(inputs["norm_mix_g"][:L]).reshape(L, 8, 128),
        "norm_ffn_g": np.ascontiguousarray(inputs["norm_ffn_g"][:L]).reshape(L, 8, 128),
        "norm_final_g": np.ascontiguousarray(inputs["norm_final_g"]).reshape(8, 128),
        "ffn_conv_w": np.ascontiguousarray(inputs["ffn_conv_w"][:L]).reshape(L, 3, 44, 128),
        "ret_decay_fwd": np.ascontiguousarray(inputs["ret_decay_fwd"][:L]).reshape(-1),
        "ret_decay_bwd": np.ascontiguousarray(inputs["ret_decay_bwd"][:L]).reshape(-1),
    }
    maps = []
    for xt, pos, link in zip(core_tokens, core_pos, core_links):
        NU_ = len(link) - 1
        lk = np.zeros((NU_, 128, 2), f32)
        for u_ in range(NU_):
            lk[u_, :, 0] = link[u_]
            lk[u_, :, 1] = link[u_ + 1]
        d = dict(shared)
        d.update({"x": np.ascontiguousarray(xt, dtype=f32), "rope": _rope_tables(pos), "links": lk, "qx": _qx(link)})
        maps.append(d)
    return maps


def kernel(**inputs):
    L = 4
    NU = 4
    inputs = {k: np.asarray(v) for k, v in inputs.items()}
    xp = inputs["x_prompt"]
    xs = inputs["x_sample"]
    nc, _ = _get_prog(NU, L)
    toks, poss, lnks = [], [], []
    ppos = np.tile(np.arange(UT), NU)
    for c in range(4):
        toks.append(xp[4 * c:4 * c + 4].reshape(NU * UT, D))
        poss.append(ppos)
        lnks.append([0, 0, 0, 0, 0])
    toks.append(xs[0])
    poss.append(np.arange(NU * UT))
    lnks.append([0, 1, 1, 1, 0])
    for c in range(5, 8):
        toks.append(toks[0])
        poss.append(ppos)
        lnks.append([0, 0, 0, 0, 0])
    maps = make_in_maps(inputs, toks, poss, lnks, L)
    res = run_bass_kernel_spmd(nc, maps, core_ids=list(range(8)))
    outs = [r["y"] for r in res.results]
    y_prompt = np.stack([outs[c].reshape(4, UT, D) for c in range(4)], 0).reshape(16, UT, D).astype(np.float32)
    y_sample = outs[4].reshape(1, NU * UT, D).astype(np.float32)
    return (y_prompt, y_sample)
```

```python
import numpy as np
import concourse.bass as bass
import concourse.mybir as mybir

F32 = mybir.dt.float32
BF16 = mybir.dt.bfloat16
U8 = mybir.dt.uint8
AF = mybir.ActivationFunctionType
ALU = mybir.AluOpType
AX = mybir.AxisListType

DT_SIZE = {F32: 4, BF16: 2, U8: 1}


class Res:
    __slots__ = ("w", "rc", "rd", "name")

    def __init__(self, name=""):
        self.w = None
        self.rc = {}
        self.rd = []
        self.name = name


class Tile:
    __slots__ = ("ap", "res", "off")

    def __init__(self, ap, res, off=None):
        self.ap = ap
        self.res = res
        self.off = off

    def __getitem__(self, k):
        return self.ap[k]


class Op:
    __slots__ = ("eng", "fn", "reads", "writes", "dma", "deps", "need_inc", "inc", "key", "snap", "barrier")


ENGS = ("pe", "act", "dve", "pool", "sp")


class Prog:
    def __init__(self, nc, n_dma_sems=40):
        self.nc = nc
        self.ops = []
        self.eobj = {"pe": nc.tensor, "act": nc.scalar, "dve": nc.vector, "pool": nc.gpsimd, "sp": nc.sync}
        self.n_dma_sems = n_dma_sems
        self.dma_q = ("sp", "pool", "act")
        self.dma_cnt = {q: 0 for q in self.dma_q}
        self.touched = []
        self.sems = {}
        for e in ENGS:
            self.sems[("c", e)] = nc.alloc_semaphore("c_" + e)
        for i in range(n_dma_sems):
            self.sems[("d", "sp", i)] = nc.alloc_semaphore("d_sp_%d" % i)
        self.tot = dict(n_ops=0, n_wait=0, n_seg=0)

    def add(self, eng, fn, reads=(), writes=(), dma=False):
        op = Op()
        op.eng = eng
        op.fn = fn
        op.reads = [r.res if isinstance(r, Tile) else r for r in reads]
        op.writes = [r.res if isinstance(r, Tile) else r for r in writes]
        op.dma = dma
        op.need_inc = dma
        op.barrier = False
        op.deps = None
        op.snap = None
        self.ops.append(op)
        self.touched.extend(op.reads)
        self.touched.extend(op.writes)
        return op

    def barrier(self):
        self._bar = getattr(self, "_bar", 0) + 1
        for e in ENGS:
            op = self.add(e, None)
            op.barrier = self._bar

    def finalize(self):
        nc = self.nc
        ops = self.ops
        nsem = self.n_dma_sems
        dma_sem_last = {}
        dma_sem_cnt = {}
        qcount = {q: 0 for q in self.dma_q}
        last_on_eng = {e: None for e in ENGS}
        dma_since_barrier = []
        bar_snap = {}
        for i, op in enumerate(ops):
            deps = set()
            if op.barrier:
                if bar_snap.get("id") != op.barrier:
                    bar_snap = dict(id=op.barrier, last=dict(last_on_eng), dmas=list(dma_since_barrier))
                    dma_since_barrier = []
                for e in ENGS:
                    if e != op.eng and bar_snap["last"][e] is not None:
                        deps.add(bar_snap["last"][e])
                deps.update(bar_snap["dmas"])
                op.deps = sorted(deps, reverse=True)
                for d in op.deps:
                    ops[d].need_inc = True
                continue
            for r in op.reads:
                if r.w is not None:
                    deps.add(r.w)
            for r in op.writes:
                if r.w is not None:
                    deps.add(r.w)
                deps.update(r.rc.values())
                deps.update(r.rd)
            if op.dma:
                slot = (op.eng, qcount[op.eng] % nsem)
                qcount[op.eng] += 1
                if slot in dma_sem_last:
                    deps.add(dma_sem_last[slot])
                dma_sem_last[slot] = i
                dma_sem_cnt[slot] = dma_sem_cnt.get(slot, 0) + 16
                op.key = ("d",) + slot
                op.inc = dma_sem_cnt[slot]
                dma_since_barrier.append(i)
            deps.discard(i)
            out = []
            rset = set(id(r) for r in op.reads)
            for d in deps:
                po = ops[d]
                if po.eng == op.eng and not po.dma and not op.dma:
                    if op.eng == "pe":
                        continue
                    raw = False
                    for w in po.writes:
                        if id(w) in rset:
                            raw = True
                            break
                    if not raw:
                        continue
                out.append(d)
            op.deps = sorted(out, reverse=True)
            for d in op.deps:
                ops[d].need_inc = True
            if op.dma:
                for r in op.reads:
                    r.rd.append(i)
            else:
                for r in op.reads:
                    r.rc[op.eng] = i
            for r in op.writes:
                r.w = i
                r.rc = {}
                r.rd = []
            last_on_eng[op.eng] = i
        ccount = {e: 0 for e in ENGS}
        for op in ops:
            if not op.dma and op.need_inc:
                ccount[op.eng] += 1
                op.inc = ccount[op.eng]
                op.key = ("c", op.eng)
        sems = self.sems
        known = {e: {} for e in ENGS}
        nwait = 0
        for op in ops:
            e = op.eng
            k = known[e]
            eo = self.eobj[e]
            for d in op.deps:
                po = ops[d]
                key, val = po.key, po.inc
                if k.get(key, 0) >= val:
                    continue
                eo.wait_ge(sems[key], val)
                nwait += 1
                k[key] = val
                if po.snap is not None:
                    for kk, vv in po.snap.items():
                        if k.get(kk, 0) < vv:
                            k[kk] = vv
            ins = None
            if op.fn is not None:
                ins = op.fn()
            if op.need_inc:
                if ins is None:
                    ins = eo.nop()
                ins.then_inc(sems[op.key], 16 if op.dma else 1)
                snap = dict(k)
                snap[op.key] = op.inc
                op.snap = snap
        sp = self.eobj["sp"]
        ksp = known["sp"]
        for slot, cnt in dma_sem_cnt.items():
            key = ("d",) + slot
            if ksp.get(key, 0) < cnt:
                sp.wait_ge(sems[key], cnt)
        self.stats = dict(n_ops=len(ops), n_wait=nwait)
        self._used = [("d",) + slot for slot in dma_sem_cnt] + [("c", e) for e in ENGS if ccount[e] > 0]
        return self.stats

    def emit(self):
        nc = self.nc
        st = self.finalize()
        nc.all_engine_barrier()
        for key in self._used:
            nc.gpsimd.sem_clear(self.sems[key])
        nc.all_engine_barrier()
        for r in self.touched:
            r.w = None
            r.rc = {}
            r.rd = []
        self.touched = []
        self.ops = []
        self.tot["n_ops"] += st["n_ops"]
        self.tot["n_wait"] += st["n_wait"]
        self.tot["n_seg"] += 1


class Arena:
    def __init__(self, nc, nbytes):
        self.nc = nc
        self.t = nc.alloc_sbuf_tensor("arena", [128, nbytes], U8)
        self.n = nbytes
        self.off = 0
        self.peak = 0

    def mark(self):
        return self.off

    def reset(self, m):
        self.off = m

    def view(self, tile, free_shape, dtype):
        return self.alloc("view", free_shape, dtype, res=tile.res, at=tile.off)

    def alloc(self, name, free_shape, dtype, parts=128, res=None, at=None):
        sz = DT_SIZE[dtype]
        n = int(np.prod(free_shape)) * sz
        if at is None:
            off = (self.off + 31) // 32 * 32
            assert off + n <= self.n, f"SBUF arena overflow allocating {name}: {off}+{n} > {self.n}"
            self.off = off + n
            self.peak = max(self.peak, self.off)
        else:
            off = at
        ap = self.t[0:parts, off:off + n].bitcast(dtype)
        if len(free_shape) == 2:
            ap = ap.rearrange("p (a b) -> p a b", b=free_shape[1])
        elif len(free_shape) == 3:
            ap = ap.rearrange("p (a b c) -> p a b c", b=free_shape[1], c=free_shape[2])
        elif len(free_shape) == 4:
            ap = ap.rearrange("p (a b c d) -> p a b c d", b=free_shape[1], c=free_shape[2], d=free_shape[3])
        return Tile(ap, res if res is not None else Res(name), off)


from functools import partial
from concourse.bass_utils import run_bass_kernel_spmd

D = 1024
DIN = 6656
DFF = 2816
UT = 4096
NEG = -30000.0
C_NAQ, C_NAK, C_NAV, C_RQ, C_RK, C_RV, C_RG, C_GA, C_GR = 0, 512, 1024, 1536, 2048, 2560, 3584, 4608, 5632
CT_M1, CT_M2, CT_I1, CT_I2, CT_J1, CT_J2, CT_N = 0, 128, 256, 384, 512, 513, 514
RSCALE = 128 ** -0.5
GELU_C = 1.5957691216057308
SKIP = set()


def build(NU, L, dbg=False):
    nc = bass.Bass("TRN2", target_bir_lowering=False)
    T = NU * UT
    NB = T // 512
    NCH = T // 128
    NFB = T // 256
    P = Prog(nc)
    A = Arena(nc, 207000)

    def dram(name, shape, dtype, kind="Internal"):
        if dbg and kind == "Internal":
            kind = "ExternalOutput"
        return nc.dram_tensor(name, shape, dtype, kind=kind).ap()

    def SL(a, n):
        return slice(a, a + n)

    def uview(apx, tok_axis, win):
        dims = [list(d) for d in apx.ap]
        s = dims[tok_axis][0]
        new = [[UT * s, NU]] + dims[:tok_axis] + [[s, win]] + dims[tok_axis + 1:]
        return bass.AP(apx.tensor, apx.offset, new)

    x_d = dram("x", [T, D], F32, "ExternalInput")
    rope_d = dram("rope", [2, 128, T], F32, "ExternalInput")
    links_d = dram("links", [NU, 128, 2], F32, "ExternalInput")
    qx_d = dram("qx", [12, NU * 5 * 128], F32, "ExternalInput")
    kx_d = dram("kx", [12, 768], F32, "ExternalInput")
    ctab_d = dram("ctab", [128, CT_N], F32, "ExternalInput")
    gmix_d = dram("norm_mix_g", [L, 8, 128], F32, "ExternalInput")
    gffn_d = dram("norm_ffn_g", [L, 8, 128], F32, "ExternalInput")
    gfin_d = dram("norm_final_g", [8, 128], F32, "ExternalInput")
    convw_d = dram("ffn_conv_w", [L, 3, 44, 128], F32, "ExternalInput")
    decf_d = dram("ret_decay_fwd", [L * 4], F32, "ExternalInput")
    decb_d = dram("ret_decay_bwd", [L * 4], F32, "ExternalInput")
    y_d = dram("y", [T, D], F32, "ExternalOutput")

    xTs = dram("xTs", [8, 128, T], F32)
    hTs = dram("hTs", [8, 128, T], BF16)
    naqT = dram("naqT", [4, 128, T], BF16)
    nakT = dram("nakT", [4, 128, T + 512], BF16)
    navs = dram("navs", [T + 512, 520], BF16)
    rqT = dram("rqT", [4, 128, T], BF16)
    rkT = dram("rkT", [4, 128, T], BF16)
    rvs = dram("rvs", [T, 1024], BF16)
    rgs = dram("rgs", [T, 1024], BF16)
    gTs = dram("gTs", [16, 128, T], BF16)
    Bst = dram("Bst", [NCH, 128, 1024], BF16)
    aTs = dram("aTs", [4, 128, T], BF16)
    roTs = dram("roTs", [8, 128, T], BF16)
    h2Ts = dram("h2Ts", [8, 128, T + 128], BF16)
    ptab = dram("ptab", [L + 1, 128, 160], F32)
    NWL = 1024 * DIN + 512 * D + D * D + D * D + D * 2 * DFF + DFF * D + 2 * 128 * 8 * 768 + D
    wall_d = dram("wall", [L, NWL], F32, "ExternalInput")
    wcur = dram("wcur", [NWL], F32)
    _wo = [0]

    def wview(rows, cols):
        v = bass.AP(wcur.tensor, _wo[0], [[cols, rows], [1, cols]])
        _wo[0] += rows * cols
        return v
    w_in_c = wview(D, DIN)
    w_ba_c = wview(512, D)
    w_br_c = wview(D, D)
    w_out_c = wview(D, D)
    w_up_c = wview(D, 2 * DFF)
    w_dn_c = wview(DFF, D)
    ta_c = bass.AP(wcur.tensor, _wo[0], [[128 * 8 * 768, 2], [8 * 768, 128], [768, 8], [1, 768]])
    _wo[0] += 2 * 128 * 8 * 768
    rng_c = bass.AP(wcur.tensor, _wo[0], [[1, D]])
    xU = uview(x_d, 0, UT)
    yU = uview(y_d, 0, UT)
    rope_dU = uview(rope_d, 2, UT)
    qxU = uview(qx_d, 1, 640) if False else bass.AP(qx_d.tensor, qx_d.offset, [[640, NU], [NU * 640, 12], [1, 640]])
    xTsU = uview(xTs, 2, UT)
    hTsU = uview(hTs, 2, UT)
    naqTU = uview(naqT, 2, UT)
    nakTU = uview(nakT, 2, UT + 512)
    navsU = uview(navs, 0, UT + 512)
    rqTU = uview(rqT, 2, UT)
    rkTU = uview(rkT, 2, UT)
    rvsU = uview(rvs, 0, UT)
    rgsU = uview(rgs, 0, UT)
    gTsU = uview(gTs, 2, UT)
    BstU = Bst.rearrange("(u c) p f -> u c p f", u=NU)
    aTsU = uview(aTs, 2, UT)
    roTsU = uview(roTs, 2, UT)
    h2TsU = uview(h2Ts, 2, UT + 128)

    pp = [nc.alloc_psum_tensor("pp%d" % i, [128, 1024], F32) for i in range(4)]
    psr = [Res("ps%d" % i) for i in range(8)]

    def bank(i):
        return pp[i // 2][:, (i % 2) * 512:(i % 2) * 512 + 512]

    def dbank(j):
        return pp[j][:, :]

    rot = [0]

    def next_bank():
        i = rot[0] % 8
        rot[0] += 1
        return bank(i), psr[i]

    def mm_group(out_ap, pairs):
        n = len(pairs)
        ins = None
        for i, (a, b) in enumerate(pairs):
            ins = nc.tensor.matmul(out_ap, a, b, start=(i == 0), stop=(i == n - 1))
        return ins

    def dma(out, in_, reads=(), writes=(), **kw):
        return P.add("sp", lambda: nc.sync.dma_start(out=out, in_=in_, **kw), reads=reads, writes=writes, dma=True)

    ident = A.alloc("ident", [128], BF16)
    identf = A.alloc("identf", [128], F32)
    ones = A.alloc("ones", [128], BF16)
    eps_t = A.alloc("eps", [1], F32)
    zero_t = A.alloc("zero", [520], BF16)
    lk2 = A.alloc("lk2", [2], F32)
    kx = A.alloc("kx", [768], BF16)
    qx = A.alloc("qx", [640], BF16)
    qxs = A.alloc("qxs", [640], F32)
    ctab = A.alloc("ctab", [CT_N], F32)
    lp = A.alloc("lp", [160], F32)
    gnext = A.alloc("gnext", [8], F32)
    gfin = A.alloc("gfin", [8], F32)
    gmix_ap = lp[:, 0:8]
    gffn_ap = lp[:, 8:16]
    convw_v = lp[:, 16:148].rearrange("p (k c) -> p k c", c=44)
    lg_v = lp[:, 148:156]

    def setup():
        m = A.mark()
        stg = A.alloc("stg0", [1024], F32)
        stg2 = A.alloc("stg1", [128], F32)
        gmix = A.alloc("gmix", [L + 1, 8], F32)
        gffn = A.alloc("gffn", [L, 8], F32)
        convw = A.alloc("convw", [L, 3, 44], F32)
        lg = A.alloc("lg", [L * 8], F32)
        P.add("pool", lambda: nc.gpsimd.memset(identf[:], 1.0), writes=[identf])
        P.add("pool", lambda: nc.gpsimd.affine_select(out=identf[:], in_=identf[:], pattern=[[-1, 128]],
                                                       compare_op=ALU.is_equal, fill=0.0, base=0, channel_multiplier=1),
              reads=[identf], writes=[identf])
        P.add("dve", lambda: nc.vector.tensor_copy(ident[:], identf[:]), reads=[identf], writes=[ident])
        P.add("pool", lambda: nc.gpsimd.memset(ones[:], 1.0), writes=[ones])
        P.add("pool", lambda: nc.gpsimd.memset(eps_t[:], 1e-6), writes=[eps_t])
        P.add("pool", lambda: nc.gpsimd.memset(zero_t[:], 0.0), writes=[zero_t])
        dma(ctab[:], ctab_d[:, :], writes=[ctab])
        dma(stg[0:12, 0:768], kx_d[:, :], writes=[stg])
        P.add("dve", lambda: nc.vector.tensor_copy(kx[0:12, :], stg[0:12, 0:768]), reads=[stg], writes=[kx])

        def colvec(dst_ap, src_ap, n):
            dma(stg2[0:n, :], src_ap, writes=[stg2])
            bk, br = next_bank()
            P.add("pe", lambda: nc.tensor.transpose(bk[:, 0:n], stg2[0:n, :], identf[0:n, 0:n]),
                  reads=[stg2, identf], writes=[br])
            P.add("dve", lambda: nc.vector.tensor_copy(dst_ap, bk[:, 0:n]), reads=[br], writes=[gmix, gffn, convw, gfin])
        for l in range(L):
            colvec(gmix[:, l, :], gmix_d[l], 8)
            colvec(gffn[:, l, :], gffn_d[l], 8)
            for k in range(3):
                colvec(convw[:, l, k, :], convw_d[l, k], 44)
        colvec(gfin[:, :], gfin_d[:, :], 8)
        colvec(gmix[:, L, :], gfin_d[:, :], 8)
        dec = A.alloc("dec", [L * 8], F32)
        dv = dec[:].rearrange("p (l e) -> p l e", e=8)
        dma(dv[:, :, 0:4], decf_d.rearrange("(l h) -> l h", h=4).partition_broadcast(128), writes=[dec])
        dma(dv[:, :, 4:8], decb_d.rearrange("(l h) -> l h", h=4).partition_broadcast(128), writes=[dec])
        P.add("act", lambda: nc.scalar.activation(out=dec[:], in_=dec[:], func=AF.Exp, scale=-1.0), reads=[dec], writes=[dec])
        P.add("act", lambda: nc.scalar.activation(out=dec[:], in_=dec[:], func=AF.Ln, bias=1.0), reads=[dec], writes=[dec])
        P.add("dve", lambda: nc.vector.tensor_scalar(lg[:], dec[:], -1.0, None, ALU.mult), reads=[dec], writes=[lg])
        for c in range(4):
            dma(nakT[c, :, 0:256], zero_t[:, 0:256], reads=[zero_t])
            dma(nakT[c, :, T + 256:T + 512], zero_t[:, 0:256], reads=[zero_t])
        for r0 in (0, 128, T + 256, T + 384):
            dma(navs[r0:r0 + 128, :], zero_t[:, :], reads=[zero_t])
        for c in range(8):
            dma(h2Ts[c, :, 0:64], zero_t[:, 0:64], reads=[zero_t])
            dma(h2Ts[c, :, T + 64:T + 128], zero_t[:, 0:64], reads=[zero_t])
        for l in range(L + 1):
            dma(ptab[l, :, 0:8], gmix[:, l, :], reads=[gmix])
            if l < L:
                dma(ptab[l, :, 8:16], gffn[:, l, :], reads=[gffn])
                dma(ptab[l, :, 16:148], convw[:, l, :, :].rearrange("p k c -> p (k c)"), reads=[convw])
                dma(ptab[l, :, 148:156], lg[:, l * 8:(l + 1) * 8], reads=[lg])
        P.emit()
        A.reset(m)

    def load_unit_params(u):
        dma(lk2[:], links_d[u], writes=[lk2])

    def load_layer_params(l):
        dma(wcur.rearrange("(a b) -> a b", a=16), wall_d[l].rearrange("(a b) -> a b", a=16))
        dma(lp[:], ptab[l], writes=[lp])
        dma(gnext[:], ptab[l + 1][:, 0:8], writes=[gnext])
        P.emit()

    cast_rr = [0]

    def cast(out_ap, in_ap, reads, writes):
        e = ("dve", "act", "pool")[cast_rr[0] % 3]
        cast_rr[0] += 1
        if e == "dve":
            P.add("dve", lambda: nc.vector.tensor_copy(out_ap, in_ap), reads=reads, writes=writes)
        elif e == "act":
            P.add("act", lambda: nc.scalar.copy(out_ap, in_ap), reads=reads, writes=writes)
        else:
            P.add("pool", lambda: nc.gpsimd.tensor_copy(out_ap, in_ap), reads=reads, writes=writes)

    def load_w(dst, src2d, KC, c_lo, c_hi, stage, dst_c0=None, swap=None):
        if dst_c0 is None:
            dst_c0 = c_lo
        i = 0
        for kc in range(KC):
            for c0 in range(c_lo, c_hi, 2048):
                w = min(2048, c_hi - c0)
                st = stage[i % 2]
                i += 1
                dma(st[:, 0:w], src2d[kc * 128:(kc + 1) * 128, c0:c0 + w], writes=[st])
                d0 = dst_c0 + (c0 - c_lo)
                cast(dst[:, kc, d0:d0 + w], st[:, 0:w], [st], [dst])
                if swap is not None:
                    sv = st[:, 0:w].rearrange("p (h t e) -> p h t e", t=2, e=64)
                    ov = swap[:, kc, d0:d0 + w].rearrange("p (h t e) -> p h t e", t=2, e=64)
                    cast(ov[:, :, 0, :], sv[:, :, 1, :], [st], [swap])
                    cast(ov[:, :, 1, :], sv[:, :, 0, :], [st], [swap])

    def rms_fm(xT, g_ap, g_tile, out, N, sq, tmp, rstd):
        P.add("act", lambda: nc.scalar.activation(out=sq[:].rearrange("p a b -> p (a b)"),
                                                  in_=xT[:].rearrange("p a b -> p (a b)"), func=AF.Square),
              reads=[xT], writes=[sq])
        bk, br = next_bank()
        P.add("pe", partial(mm_group, bk[:, 0:N], [(ones[:], sq[:, k, :]) for k in range(8)]), reads=[sq, ones], writes=[br])
        P.add("act", lambda: nc.scalar.activation(out=tmp[:], in_=bk[:, 0:N], func=AF.Sqrt, bias=eps_t[:, 0:1], scale=1.0 / D),
              reads=[br, eps_t], writes=[tmp])
        P.add("dve", lambda: nc.vector.reciprocal(rstd[:], tmp[:]), reads=[tmp], writes=[rstd])
        for k in range(8):
            P.add("dve", lambda k=k: nc.vector.scalar_tensor_tensor(out=out[:, k, :], in0=xT[:, k, :], scalar=g_ap[:, k:k + 1],
                                                                    in1=rstd[:], op0=ALU.mult, op1=ALU.mult),
                  reads=[xT, rstd, g_tile], writes=[out])

    def phase0():
        m = A.mark()
        xin = [A.alloc("xin%d" % i, [1024], F32) for i in range(2)]
        xT = [A.alloc("xT%d" % i, [8, 512], F32) for i in range(2)]
        hT = [A.alloc("hT%d" % i, [8, 512], BF16) for i in range(2)]
        sq = A.alloc("sq", [8, 512], BF16)
        tmp = A.alloc("tmp", [512], F32)
        rstd = A.alloc("rstd", [512], F32)
        for u in range(NU):
            for b in range(8):
                xt = xT[b % 2]
                ht = hT[b % 2]
                for s in range(4):
                    xi = xin[s % 2]
                    r0 = b * 512 + s * 128
                    dma(xi[:], xU[u][r0:r0 + 128, :], writes=[xi])
                    j = (b * 4 + s) % 4
                    db = dbank(j)
                    P.add("pe", lambda db=db, xi=xi: [nc.tensor.transpose(db[:, c * 128:(c + 1) * 128], xi[:, c * 128:(c + 1) * 128], identf[:])
                                                       for c in range(8)][-1],
                          reads=[xi, identf], writes=[psr[2 * j], psr[2 * j + 1]])
                    e = "act" if s % 2 else "dve"
                    if e == "dve":
                        P.add("dve", lambda db=db, xt=xt, s=s: nc.vector.tensor_copy(xt[:, :, s * 128:(s + 1) * 128], db.rearrange("p (c t) -> p c t", t=128)),
                              reads=[psr[2 * j], psr[2 * j + 1]], writes=[xt])
                    else:
                        P.add("act", lambda db=db, xt=xt, s=s: nc.scalar.copy(xt[:, :, s * 128:(s + 1) * 128], db.rearrange("p (c t) -> p c t", t=128)),
                              reads=[psr[2 * j], psr[2 * j + 1]], writes=[xt])
                rms_fm(xt, gmix_ap, lp, ht, 512, sq, tmp, rstd)
                dma(xTsU[u][:, :, SL(b * 512, 512)].rearrange("c p t -> p c t"), xt[:], reads=[xt])
                dma(hTsU[u][:, :, SL(b * 512, 512)].rearrange("c p t -> p c t"), ht[:], reads=[ht])
            P.emit()
        A.reset(m)

    def phase1(l):
        m = A.mark()
        W = A.alloc("w_in", [8, DIN], BF16)
        Wsw = A.alloc("w_sw", [8, 1024], BF16)
        m2 = A.mark()
        stage = [A.alloc("wst%d" % i, [2048], F32) for i in range(2)]
        wsrc = w_in_c
        load_w(W, wsrc, 8, 0, DIN, stage)
        i = 0
        for kc in range(8):
            st = stage[i % 2]
            i += 1
            dma(st[:, 0:1024], wsrc[kc * 128:(kc + 1) * 128, C_RQ:C_RV], writes=[st])
            sv = st[:, 0:1024].rearrange("p (h t e) -> p h t e", t=2, e=64)
            ov = Wsw[:, kc, :].rearrange("p (h t e) -> p h t e", t=2, e=64)
            cast(ov[:, :, 0, :], sv[:, :, 1, :], [st], [Wsw])
            cast(ov[:, :, 1, :], sv[:, :, 0, :], [st], [Wsw])
        P.emit()
        A.reset(m2)
        hT = [A.alloc("hT%d" % i, [8, 512], BF16) for i in range(2)]
        cs = [A.alloc("cs%d" % i, [2, 512], F32) for i in range(2)]
        G = [A.alloc("G%d" % i, [4, 512], BF16) for i in range(6)]
        TM = [A.alloc("TM%d" % i, [2568], BF16) for i in range(2)]
        rt = [A.alloc("rt%d" % i, [512], F32) for i in range(4)]
        gi = [0]
        for u in range(NU):
            for tm in TM:
                P.add("pool", lambda tm=tm: nc.gpsimd.memset(tm[:, 0:520].rearrange("p (h e) -> p h e", e=65)[:, :, 64:65], 1.0), writes=[tm])

            def load(b):
                dma(hT[b % 2][:], hTsU[u][:, :, SL(b * 512, 512)].rearrange("c p t -> p c t"), writes=[hT[b % 2]])
                dma(cs[b % 2][:], rope_dU[u][:, :, SL(b * 512, 512)].rearrange("c p t -> p c t"), writes=[cs[b % 2]])

            load(0)
            for b in range(8):
                if b + 1 < 8:
                    load(b + 1)
                h = hT[b % 2]
                c_s = cs[b % 2]
                t0 = b * 512

                def fm_chunk(col, Wt=W):
                    bk, br = next_bank()
                    P.add("pe", partial(mm_group, bk, [(Wt[:, k, col:col + 128], h[:, k, :]) for k in range(8)]),
                          reads=[Wt, h], writes=[br])
                    return bk, br

                for (c0, dst, nchunk, func) in ((C_NAQ, naqT, 4, AF.Copy), (C_NAK, nakT, 4, AF.Copy),
                                               (C_GA, gTs, 8, AF.Sigmoid), (C_GR, gTs, 8, AF.Sigmoid)):
                    for g0 in range(0, nchunk, 4):
                        gt = G[gi[0] % 6]
                        gi[0] += 1
                        for j in range(4):
                            bk, br = fm_chunk(c0 + (g0 + j) * 128)
                            P.add("act", lambda bk=bk, gt=gt, j=j, func=func: nc.scalar.activation(out=gt[:, j, :], in_=bk, func=func),
                                  reads=[br], writes=[gt])
                        if dst is nakT:
                            dap = nakTU[u][:, :, SL(256 + t0, 512)]
                        elif dst is gTs:
                            cb = (0 if c0 == C_GA else 8) + g0
                            dap = gTsU[u][cb:cb + 4, :, SL(t0, 512)]
                        else:
                            dap = naqTU[u][:, :, SL(t0, 512)]
                        dma(dap.rearrange("c p t -> p c t"), gt[:], reads=[gt])
                for (c0, dst) in ((C_RQ, rqTU), (C_RK, rkTU)):
                    gt = G[gi[0] % 6]
                    gi[0] += 1
                    for j in range(4):
                        bk, br = fm_chunk(c0 + j * 128)
                        bk2, br2 = fm_chunk(c0 - C_RQ + j * 128, Wt=Wsw)
                        r1 = rt[(2 * j) % 4]
                        r2 = rt[(2 * j + 1) % 4]
                        P.add("dve", lambda bk=bk, r1=r1, c_s=c_s: nc.vector.tensor_tensor(out=r1[:], in0=bk, in1=c_s[:, 0, :], op=ALU.mult),
                              reads=[br, c_s], writes=[r1])
                        P.add("dve", lambda bk2=bk2, r2=r2, c_s=c_s: nc.vector.tensor_tensor(out=r2[:], in0=bk2, in1=c_s[:, 1, :], op=ALU.mult),
                              reads=[br2, c_s], writes=[r2])
                        P.add("pool", lambda r1=r1, r2=r2, gt=gt, j=j: nc.gpsimd.tensor_tensor(out=gt[:, j, :], in0=r1[:], in1=r2[:], op=ALU.add),
                              reads=[r1, r2], writes=[gt])
                    dma(dst[u][:, :, SL(t0, 512)].rearrange("c p t -> p c t"), gt[:], reads=[gt])
                for s in range(4):
                    tm = TM[s % 2]
                    for gidx, col in enumerate((C_NAV, C_RV, C_RV + 512, C_RG, C_RG + 512)):
                        bk, br = next_bank()
                        P.add("pe", partial(mm_group, bk, [(h[:, k, s * 128:(s + 1) * 128], W[:, k, col:col + 512]) for k in range(8)]),
                              reads=[W, h], writes=[br])
                        if gidx == 0:
                            o = tm[:, 0:520].rearrange("p (h e) -> p h e", e=65)[:, :, 0:64]
                            P.add("dve", lambda bk=bk, o=o: nc.vector.tensor_copy(o, bk.rearrange("p (h e) -> p h e", e=64)), reads=[br], writes=[tm])
                            continue
                        o = tm[:, 8 + gidx * 512:8 + (gidx + 1) * 512]
                        if gidx >= 3:
                            P.add("act", lambda bk=bk, o=o: nc.scalar.activation(out=o, in_=bk, func=AF.Silu), reads=[br], writes=[tm])
                        else:
                            P.add("dve", lambda bk=bk, o=o: nc.vector.tensor_copy(o, bk), reads=[br], writes=[tm])
                    r0 = t0 + s * 128
                    dma(navsU[u][SL(256 + r0, 128), :], tm[:, 0:520], reads=[tm])
                    dma(rvsU[u][SL(r0, 128), :], tm[:, 520:1544], reads=[tm])
                    dma(rgsU[u][SL(r0, 128), :], tm[:, 1544:2568], reads=[tm])
            P.emit()
        A.reset(m)

    def phase_na(l):
        m = A.mark()
        kT = A.alloc("kT", [4, 4608], BF16)
        qT = A.alloc("qT", [4, 4096], BF16)
        V = A.alloc("V", [36, 8, 65], BF16)
        TA = A.alloc("TA", [8, 768], F32)
        TB = A.alloc("TB", [8, 768], F32)
        tmp = [A.alloc("natmp%d" % i, [768], F32) for i in range(2)]
        E = [A.alloc("naE%d" % i, [768], BF16) for i in range(2)]
        atok = [A.alloc("atok%d" % i, [8, 64], BF16) for i in range(2)]
        rden = [A.alloc("rden%d" % i, [8], F32) for i in range(2)]
        aTb = [A.alloc("aTb%d" % i, [4, 512], BF16) for i in range(2)]
        dma(TA[:], ta_c[0], writes=[TA])
        dma(TB[:], ta_c[1], writes=[TB])
        P.emit()
        cnt = 0
        S_res = [[psr[0], psr[1]], [psr[2], psr[3]]]
        PV_res = [psr[4], psr[5]]
        PV = dbank(2).rearrange("p (h e) -> p h e", e=128)
        psT = bank(6).bitcast(BF16)
        for u in range(NU):
            cnt = 0
            dma(qxs[0:12, :], qxU[u], writes=[qxs])
            P.add("dve", lambda: nc.vector.tensor_copy(qx[0:12, :], qxs[0:12, :]), reads=[qxs], writes=[qx])
            for c in range(4):
                dma(kT[:, c, :], nakTU[u][c, :, :], writes=[kT])
                dma(qT[:, c, :], naqTU[u][c, :, :], writes=[qT])
            for g0 in range(0, 36, 9):
                dma(V[:, g0:g0 + 9, :, :].rearrange("p g h e -> p g (h e)"), navsU[u][SL(g0 * 128, 9 * 128), :].rearrange("(g p) e -> p g e", p=128),
                    writes=[V])
            def pinfo(p):
                if p == 0:
                    return 1, TA, 6, 0
                if p == 1:
                    return 2, TB, 6, 0
                if p == 30:
                    return 3, TA, 6, 30
                if p == 31:
                    return 4, TB, 6, 30
                return 0, TA, 5, p

            def emit_s(p, h, si):
                tp, Tt, nch, kt0 = pinfo(p)
                qxo = tp * 128
                hc, hp = h // 2, (h % 2) * 64
                S = dbank(si)

                def s_mm():
                    ins = None
                    for mth in range(nch):
                        o = S[:, mth * 128:(mth + 1) * 128]
                        nc.tensor.matmul(o, kT[hp:hp + 64, hc, (kt0 + mth) * 128:(kt0 + mth + 1) * 128],
                                         qT[hp:hp + 64, hc, p * 128:(p + 1) * 128], start=True, stop=False)
                        ins = nc.tensor.matmul(o, kx[0:12, mth * 128:(mth + 1) * 128], qx[0:12, qxo:qxo + 128], start=False, stop=True)
                    return ins
                P.add("pe", s_mm, reads=[kT, qT, kx, qx], writes=S_res[si])

            def emit_rest(p, h, si):
                tp, Tt, nch, kt0 = pinfo(p)
                S = dbank(si)
                tm_, e_ = tmp[si], E[si]
                w = nch * 128
                P.add("dve", lambda: nc.vector.scalar_tensor_tensor(
                    out=tm_[:, 0:w], in0=S[:, 0:w], scalar=0.125, in1=Tt[:, h, 0:w], op0=ALU.mult, op1=ALU.add),
                    reads=S_res[si] + [Tt.res], writes=[tm_])
                P.add("act", lambda: nc.scalar.activation(out=e_[:, 0:w], in_=tm_[:, 0:w], func=AF.Exp),
                      reads=[tm_], writes=[e_])
                P.add("pe", partial(mm_group, PV[:, h, 0:65],
                                    [(e_[:, mth * 128:(mth + 1) * 128], V[:, kt0 + mth, h, :]) for mth in range(nch)]),
                      reads=[e_, V], writes=PV_res)

            def emit_pair_end(p):
                at, rd = atok[p % 2], rden[p % 2]
                P.add("dve", lambda: nc.vector.reciprocal(rd[:], PV[:, :, 64]), reads=PV_res, writes=[rd])
                P.add("dve", lambda: nc.vector.tensor_tensor(out=at[:], in0=PV[:, :, 0:64],
                                                             in1=rd[:].unsqueeze(2).to_broadcast([128, 8, 64]), op=ALU.mult),
                      reads=PV_res + [rd.res], writes=[at])
                atf = at[:].rearrange("p h e -> p (h e)")
                P.add("pe", lambda: [nc.tensor.transpose(psT[:, c * 128:(c + 1) * 128], atf[:, c * 128:(c + 1) * 128], ident[:])
                                     for c in range(4)][-1], reads=[at, ident], writes=[psr[6]])
                blk = p // 4
                ab = aTb[blk % 2]
                P.add("act", lambda: nc.scalar.copy(ab[:, :, (p % 4) * 128:(p % 4 + 1) * 128],
                                                    psT[:, 0:512].rearrange("p (c t) -> p c t", t=128)),
                      reads=[psr[6]], writes=[ab])
                if p % 4 == 3:
                    dma(aTsU[u][:, :, SL(blk * 512, 512)].rearrange("c p t -> p c t"), ab[:], reads=[ab])

            items = [(p, h) for p in range(32) for h in range(8)]
            emit_s(items[0][0], items[0][1], 0)
            for i, (p, h) in enumerate(items):
                if i + 1 < len(items):
                    emit_s(items[i + 1][0], items[i + 1][1], (i + 1) % 2)
                emit_rest(p, h, i % 2)
                if h == 7:
                    emit_pair_end(p)
            P.emit()
        A.reset(m)

    def phase_ret(l):
        m = A.mark()
        DT = A.alloc("DT", [4, 128], F32)
        QDF = A.alloc("QDF", [4, 128], F32)
        QDB = A.alloc("QDB", [4, 128], F32)
        kdf = A.alloc("kdf", [4], F32)
        kdb = A.alloc("kdb", [4], F32)
        gcf = A.alloc("gcf", [4], F32)
        gcb = A.alloc("gcb", [4], F32)
        gbc = A.alloc("gbc", [1024], F32)
        St = A.alloc("St", [4, 256], F32)
        Sbf = [A.alloc("Sbf%d" % i, [4, 256], BF16) for i in range(2)]
        arg = A.alloc("arg", [128], F32)
        lg = lp
        lgf = lambda h: lp[:, 148 + h:148 + h + 1]
        lgb = lambda h: lp[:, 152 + h:152 + h + 1]
        tabs = [DT, QDF, QDB, kdf, kdb, gcf, gcb]
        for h in range(4):
            P.add("dve", lambda h=h: nc.vector.tensor_scalar(arg[:], ctab[:, CT_M1:CT_M1 + 128], lgf(h), None, ALU.mult),
                  reads=[ctab, lg], writes=[arg])
            P.add("dve", lambda h=h: nc.vector.scalar_tensor_tensor(out=arg[:], in0=ctab[:, CT_M2:CT_M2 + 128], scalar=lgb(h), in1=arg[:],
                                                                    op0=ALU.mult, op1=ALU.add), reads=[ctab, lg, arg], writes=[arg])
            P.add("act", lambda h=h: nc.scalar.activation(out=DT[:, h, :], in_=arg[:], func=AF.Exp), reads=[arg], writes=[DT])
            P.add("act", lambda h=h: nc.scalar.activation(out=QDF[:, h, :], in_=ctab[:, CT_I1:CT_I1 + 128], func=AF.Exp, scale=lgf(h)),
                  reads=[ctab, lg], writes=[QDF])
            P.add("act", lambda h=h: nc.scalar.activation(out=QDB[:, h, :], in_=ctab[:, CT_I2:CT_I2 + 128], func=AF.Exp, scale=lgb(h)),
                  reads=[ctab, lg], writes=[QDB])
            P.add("act", lambda h=h: nc.scalar.activation(out=kdf[:, h:h + 1], in_=ctab[:, CT_J1:CT_J1 + 1], func=AF.Exp, scale=lgf(h)),
                  reads=[ctab, lg], writes=[kdf])
            P.add("act", lambda h=h: nc.scalar.activation(out=kdb[:, h:h + 1], in_=ctab[:, CT_J2:CT_J2 + 1], func=AF.Exp, scale=lgb(h)),
                  reads=[ctab, lg], writes=[kdb])
            P.add("act", lambda h=h: nc.scalar.activation(out=gcf[:, h:h + 1], in_=lgf(h), func=AF.Exp, scale=128.0), reads=[lg], writes=[gcf])
            P.add("act", lambda h=h: nc.scalar.activation(out=gcb[:, h:h + 1], in_=lgb(h), func=AF.Exp, scale=128.0), reads=[lg], writes=[gcb])
        for t_ in (DT, kdf, kdb):
            P.add("dve", lambda t_=t_: nc.vector.tensor_scalar(t_[:], t_[:], RSCALE, None, ALU.mult), reads=[t_], writes=[t_])
        dma(gbc[:], rng_c.partition_broadcast(128), writes=[gbc])
        P.emit()

        kTc = [A.alloc("kTc%d" % i, [4, 128], BF16) for i in range(2)]
        qTc = [A.alloc("qTc%d" % i, [4, 128], BF16) for i in range(2)]
        vc = [A.alloc("vc%d" % i, [1024], BF16) for i in range(2)]
        gsc = [A.alloc("gsc%d" % i, [1024], BF16) for i in range(2)]
        Bc = [A.alloc("Bc%d" % i, [4, 256], BF16) for i in range(2)]
        kd = [A.alloc("kd%d" % i, [4, 128], BF16) for i in range(2)]
        psT = bank(0).bitcast(BF16)[:, 0:512].rearrange("p (h d) -> p h d", d=128)
        psT_r = [psr[0]]
        psS = bank(1).rearrange("p (h i) -> p h i", i=128)
        psS_r = [psr[1]]
        psO = dbank(1).rearrange("p (h v) -> p h v", v=256)
        psO_r = [psr[2], psr[3]]
        psF = dbank(2).rearrange("p (h v) -> p h v", v=256)
        psF_r = [psr[4], psr[5]]
        psR = bank(6).bitcast(BF16).rearrange("p (c t) -> p c t", t=128)
        psR_r = [psr[6]]

        def state_update(gc, k_d, v_):
            P.add("pe", lambda: [nc.tensor.matmul(psF[:, h, :], k_d[:, h, :], v_[:, h * 256:(h + 1) * 256], start=True, stop=True)
                                 for h in range(4)][-1], reads=[k_d, v_], writes=psF_r)
            P.add("dve", lambda: nc.vector.tensor_tensor(out=St[:], in0=St[:], in1=gc[:].unsqueeze(2).to_broadcast([128, 4, 256]), op=ALU.mult),
                  reads=[St, gc], writes=[St])
            P.add("dve", lambda: nc.vector.tensor_tensor(out=St[:], in0=St[:], in1=psF, op=ALU.add), reads=[St] + psF_r, writes=[St])

        def k_tokmajor(kt, kdec, out):
            P.add("pe", lambda: [nc.tensor.transpose(psT[:, h, :], kt[:, h, :], ident[:]) for h in range(4)][-1],
                  reads=[kt, ident], writes=psT_r)
            P.add("dve", lambda: nc.vector.tensor_tensor(out=out[:], in0=psT, in1=kdec[:].unsqueeze(2).to_broadcast([128, 4, 128]), op=ALU.mult),
                  reads=psT_r + [kdec.res], writes=[out])

        P.add("pool", lambda: nc.gpsimd.memset(St[:], 0.0), writes=[St])
        P.emit()
        for ui in range(NU):
            u = (NU - 1) - ui
            load_unit_params(u)

            def bload(c):
                dma(kTc[c % 2][:], rkTU[u][:, :, SL(c * 128, 128)].rearrange("h p t -> p h t"), writes=[kTc[c % 2]])
                dma(vc[c % 2][:], rvsU[u][SL(c * 128, 128), :], writes=[vc[c % 2]])
            bload(31)
            for c in range(31, -1, -1):
                if c - 1 >= 0:
                    bload(c - 1)
                if c == 31:
                    P.add("dve", lambda: nc.vector.tensor_scalar(St[:], St[:], lk2[:, 1:2], None, ALU.mult),
                          reads=[St, lk2], writes=[St])
                sb = Sbf[c % 2]
                P.add("act", lambda sb=sb: nc.scalar.copy(sb[:], St[:]), reads=[St], writes=[sb])
                dma(BstU[u][c].rearrange("p (h v) -> p h v", v=256), sb[:], reads=[sb])
                k_tokmajor(kTc[c % 2], kdb, kd[c % 2])
                state_update(gcb, kd[c % 2], vc[c % 2])
            P.emit()

        AT = [A.alloc("AT%d" % i, [4, 128], BF16) for i in range(2)]
        qdf = [A.alloc("qdf%d" % i, [4, 128], BF16) for i in range(2)]
        qdb = [A.alloc("qdb%d" % i, [4, 128], BF16) for i in range(2)]
        stt = A.alloc("bnst", [4, 6], F32)
        mv = A.alloc("bnmv", [4, 2], F32)
        sd = A.alloc("bnsd", [4], F32)
        rs_ = A.alloc("bnrs", [4], F32)
        yn = A.alloc("yn", [1024], F32)
        gs = A.alloc("gs", [1024], F32)
        ro = [A.alloc("ro%d" % i, [1024], BF16) for i in range(2)]
        roT = [A.alloc("roT%d" % i, [8, 512], BF16) for i in range(2)]
        P.add("pool", lambda: nc.gpsimd.memset(St[:], 0.0), writes=[St])
        P.emit()
        Fb = Sbf[0]
        for u in range(NU):
            load_unit_params(u)

            def fload(c):
                i = c % 2
                dma(kTc[i][:], rkTU[u][:, :, SL(c * 128, 128)].rearrange("h p t -> p h t"), writes=[kTc[i]])
                dma(qTc[i][:], rqTU[u][:, :, SL(c * 128, 128)].rearrange("h p t -> p h t"), writes=[qTc[i]])
                dma(vc[i][:], rvsU[u][SL(c * 128, 128), :], writes=[vc[i]])
                dma(gsc[i][:], rgsU[u][SL(c * 128, 128), :], writes=[gsc[i]])
                dma(Bc[i][:], BstU[u][c].rearrange("p (h v) -> p h v", v=256), writes=[Bc[i]])
            fload(0)
            for c in range(32):
                if c + 1 < 32:
                    fload(c + 1)
                i = c % 2
                kt, qt, v_, g_, bc = kTc[i], qTc[i], vc[i], gsc[i], Bc[i]
                if c == 0:
                    P.add("dve", lambda: nc.vector.tensor_scalar(St[:], St[:], lk2[:, 0:1], None, ALU.mult),
                          reads=[St, lk2], writes=[St])
                P.add("act", lambda: nc.scalar.copy(Fb[:], St[:]), reads=[St], writes=[Fb])
                P.add("pe", lambda kt=kt, qt=qt: [nc.tensor.matmul(psS[:, h, :], kt[:, h, :], qt[:, h, :], start=True, stop=True) for h in range(4)][-1],
                      reads=[kt, qt], writes=psS_r)
                at = AT[i]
                P.add("dve", lambda at=at: nc.vector.tensor_tensor(out=at[:], in0=psS, in1=DT[:], op=ALU.mult), reads=psS_r + [DT.res], writes=[at])
                qf, qb = qdf[i], qdb[i]
                P.add("pool", lambda qt=qt, qf=qf: nc.gpsimd.tensor_tensor(out=qf[:], in0=qt[:], in1=QDF[:], op=ALU.mult), reads=[qt, QDF], writes=[qf])
                P.add("pool", lambda qt=qt, qb=qb: nc.gpsimd.tensor_tensor(out=qb[:], in0=qt[:], in1=QDB[:], op=ALU.mult), reads=[qt, QDB], writes=[qb])

                def o_mm(at=at, qf=qf, qb=qb, v_=v_, bc=bc):
                    ins = None
                    for h in range(4):
                        nc.tensor.matmul(psO[:, h, :], at[:, h, :], v_[:, h * 256:(h + 1) * 256], start=True, stop=False)
                        nc.tensor.matmul(psO[:, h, :], qf[:, h, :], Fb[:, h, :], start=False, stop=False)
                        ins = nc.tensor.matmul(psO[:, h, :], qb[:, h, :], bc[:, h, :], start=False, stop=True)
                    return ins
                P.add("pe", o_mm, reads=[at, qf, qb, v_, bc, Fb], writes=psO_r)
                k_tokmajor(kt, kdf, kd[i])
                state_update(gcf, kd[i], v_)
                def bn():
                    for h in range(4):
                        nc.vector.bn_stats(stt[:, h, :], psO[:, h, :])
                    ins = None
                    for h in range(4):
                        ins = nc.vector.bn_aggr(mv[:, h, :], stt[:, h, :])
                    return ins
                P.add("dve", bn, reads=psO_r, writes=[stt, mv])
                P.add("act", lambda: nc.scalar.activation(out=sd[:], in_=mv[:, :, 1], func=AF.Sqrt, bias=eps_t[:, 0:1], scale=1.0),
                      reads=[mv, eps_t], writes=[sd])
                P.add("dve", lambda: nc.vector.reciprocal(rs_[:], sd[:]), reads=[sd], writes=[rs_])
                for h in range(4):
                    P.add("dve", lambda h=h: nc.vector.tensor_scalar(yn[:, h * 256:(h + 1) * 256], psO[:, h, :], mv[:, h, 0:1], rs_[:, h:h + 1],
                                                                     ALU.subtract, ALU.mult), reads=psO_r + [mv.res, rs_.res], writes=[yn])
                P.add("pool", lambda g_=g_: nc.gpsimd.tensor_tensor(out=gs[:], in0=g_[:], in1=gbc[:], op=ALU.mult), reads=[g_, gbc], writes=[gs])
                r_ = ro[i]
                P.add("pool", lambda r_=r_: nc.gpsimd.tensor_tensor(out=r_[:], in0=yn[:], in1=gs[:], op=ALU.mult), reads=[yn, gs], writes=[r_])
                P.add("pe", lambda r_=r_: [nc.tensor.transpose(psR[:, c8, :], r_[:, c8 * 128:(c8 + 1) * 128], ident[:]) for c8 in range(8)][-1],
                      reads=[r_, ident], writes=psR_r)
                blk = c // 4
                rT = roT[blk % 2]
                P.add("act", lambda rT=rT, c=c: nc.scalar.copy(rT[:, :, (c % 4) * 128:(c % 4 + 1) * 128], psR), reads=psR_r, writes=[rT])
                if c % 4 == 3:
                    dma(roTsU[u][:, :, SL(blk * 512, 512)].rearrange("c p t -> p c t"), rT[:], reads=[rT])
            P.emit()
        A.reset(m)
    def phase3a(l):
        m = A.mark()
        Wa = A.alloc("w_ba", [4, D], BF16)
        Wr = A.alloc("w_br", [8, D], BF16)
        Wo = A.alloc("w_o", [8, D], BF16)
        m2 = A.mark()
        stage = [A.alloc("wst%d" % i, [2048], F32) for i in range(2)]
        load_w(Wa, w_ba_c, 4, 0, D, stage)
        load_w(Wr, w_br_c, 8, 0, D, stage)
        load_w(Wo, w_out_c, 8, 0, D, stage)
        P.emit()
        A.reset(m2)
        xT = [A.alloc("xT%d" % i, [8, 512], F32) for i in range(2)]
        aT = [A.alloc("aT%d" % i, [4, 512], BF16) for i in range(2)]
        rT = [A.alloc("rT%d" % i, [8, 512], BF16) for i in range(2)]
        gT = [A.alloc("gT%d" % i, [16, 512], BF16) for i in range(2)]
        mixed = A.alloc("mixed", [8, 512], BF16)
        t1 = [A.alloc("t1_%d" % i, [512], F32) for i in range(2)]
        t2 = [A.alloc("t2_%d" % i, [512], F32) for i in range(2)]
        sq = A.alloc("sq", [8, 512], BF16)
        h2 = [A.alloc("h2_%d" % i, [8, 512], BF16) for i in range(2)]
        tmp = A.alloc("tmp", [512], F32)
        rstd = A.alloc("rstd", [512], F32)

        for u in range(NU):
            def load(b):
                i = b % 2
                sl = SL(b * 512, 512)
                dma(aT[i][:], aTsU[u][:, :, sl].rearrange("c p t -> p c t"), writes=[aT[i]])
                dma(rT[i][:], roTsU[u][:, :, sl].rearrange("c p t -> p c t"), writes=[rT[i]])
                dma(gT[i][:], gTsU[u][:, :, sl].rearrange("c p t -> p c t"), writes=[gT[i]])
                dma(xT[i][:], xTsU[u][:, :, sl].rearrange("c p t -> p c t"), writes=[xT[i]])
            load(0)
            for b in range(8):
                if b + 1 < 8:
                    load(b + 1)
                i = b % 2
                a_, r_, g_, x_ = aT[i], rT[i], gT[i], xT[i]
                for oc in range(8):
                    bka, bra = next_bank()
                    P.add("pe", partial(mm_group, bka, [(Wa[:, k, oc * 128:(oc + 1) * 128], a_[:, k, :]) for k in range(4)]), reads=[Wa, a_], writes=[bra])
                    bkr, brr = next_bank()
                    P.add("pe", partial(mm_group, bkr, [(Wr[:, k, oc * 128:(oc + 1) * 128], r_[:, k, :]) for k in range(8)]), reads=[Wr, r_], writes=[brr])
                    u1, u2 = t1[oc % 2], t2[oc % 2]
                    P.add("dve", lambda bka=bka, u1=u1, oc=oc, g_=g_: nc.vector.tensor_tensor(out=u1[:], in0=bka, in1=g_[:, oc, :], op=ALU.mult),
                          reads=[bra, g_], writes=[u1])
                    P.add("dve", lambda bkr=bkr, u2=u2, oc=oc, g_=g_: nc.vector.tensor_tensor(out=u2[:], in0=bkr, in1=g_[:, 8 + oc, :], op=ALU.mult),
                          reads=[brr, g_], writes=[u2])
                    P.add("pool", lambda u1=u1, u2=u2, oc=oc: nc.gpsimd.tensor_tensor(out=mixed[:, oc, :], in0=u1[:], in1=u2[:], op=ALU.add),
                          reads=[u1, u2], writes=[mixed])
                for oc in range(8):
                    bk, br = next_bank()
                    P.add("pe", partial(mm_group, bk, [(Wo[:, k, oc * 128:(oc + 1) * 128], mixed[:, k, :]) for k in range(8)]), reads=[Wo, mixed], writes=[br])
                    P.add("dve", lambda bk=bk, oc=oc, x_=x_: nc.vector.tensor_tensor(out=x_[:, oc, :], in0=x_[:, oc, :], in1=bk, op=ALU.add),
                          reads=[br, x_], writes=[x_])
                rms_fm(x_, gffn_ap, lp, h2[i], 512, sq, tmp, rstd)
                dma(xTsU[u][:, :, SL(b * 512, 512)].rearrange("c p t -> p c t"), x_[:], reads=[x_])
                dma(h2TsU[u][:, :, SL(64 + b * 512, 512)].rearrange("c p t -> p c t"), h2[i][:], reads=[h2[i]])
            P.emit()
        A.reset(m)

    def phase3b(l):
        m = A.mark()
        Wu = A.alloc("w_up", [8, 2 * DFF], BF16)
        Wd = A.alloc("w_dn", [22, D], BF16)
        m2 = A.mark()
        stage = [A.alloc("wst%d" % i, [2048], F32) for i in range(2)]
        load_w(Wu, w_up_c, 8, 0, 2 * DFF, stage)
        load_w(Wd, w_dn_c, 22, 0, D, stage)
        P.emit()
        A.reset(m2)
        N = 256
        xT = [A.alloc("xT%d" % i, [8, N], F32) for i in range(2)]
        h2 = [A.alloc("h2_%d" % i, [8, N + 2], BF16) for i in range(2)]
        act_ = A.alloc("act", [22, N], BF16)
        cg = [A.alloc("cg%d" % i, [N], F32) for i in range(2)]
        cv = [A.alloc("cv%d" % i, [N], F32) for i in range(2)]
        tt = [A.alloc("tt%d" % i, [N], F32) for i in range(2)]
        sg = [A.alloc("sg%d" % i, [N], F32) for i in range(2)]
        sq = A.alloc("sq", [8, N], BF16)
        tmp = A.alloc("tmp", [N], F32)
        rstd = A.alloc("rstd", [N], F32)
        hT = [A.view(act_, [8, N], BF16)]
        convw = convw_v
        for u in range(NU):
            load_unit_params(u)

            def load(fb):
                i = fb % 2
                dma(h2[i][:], h2TsU[u][:, :, SL(63 + fb * N, N + 2)].rearrange("c p t -> p c t"), writes=[h2[i]])
                dma(xT[i][:], xTsU[u][:, :, SL(fb * N, N)].rearrange("c p t -> p c t"), writes=[xT[i]])
            load(0)
            for fb in range(UT // N):
                if fb + 1 < UT // N:
                    load(fb + 1)
                i = fb % 2
                h_, x_ = h2[i], xT[i]
                t0 = fb * N
                if fb == 0:
                    P.add("dve", lambda h_=h_: nc.vector.tensor_scalar(h_[:, :, 0], h_[:, :, 0], lk2[:, 0:1], None, ALU.mult),
                          reads=[h_, lk2], writes=[h_])
                if fb == UT // N - 1:
                    P.add("dve", lambda h_=h_: nc.vector.tensor_scalar(h_[:, :, N + 1], h_[:, :, N + 1], lk2[:, 1:2], None, ALU.mult),
                          reads=[h_, lk2], writes=[h_])
                for fc in range(22):
                    res = []
                    for (col, dst) in ((fc * 128, cg[fc % 2]), (DFF + fc * 128, cv[fc % 2])):
                        bk, br = next_bank()
                        P.add("pe", partial(mm_group, bk[:, 0:N + 2], [(Wu[:, k, col:col + 128], h_[:, k, :]) for k in range(8)]), reads=[Wu, h_], writes=[br])
                        cc = col // 128
                        P.add("act", lambda bk=bk, dst=dst, cc=cc: nc.scalar.activation(out=dst[:], in_=bk[:, 1:N + 1], func=AF.Copy, scale=convw[:, 1, cc:cc + 1]),
                              reads=[br, lp], writes=[dst])
                        P.add("dve", lambda bk=bk, dst=dst, cc=cc: nc.vector.scalar_tensor_tensor(out=dst[:], in0=bk[:, 0:N], scalar=convw[:, 0, cc:cc + 1],
                                                                                                  in1=dst[:], op0=ALU.mult, op1=ALU.add),
                              reads=[br, lp, dst], writes=[dst])
                        P.add("dve", lambda bk=bk, dst=dst, cc=cc: nc.vector.scalar_tensor_tensor(out=dst[:], in0=bk[:, 2:N + 2], scalar=convw[:, 2, cc:cc + 1],
                                                                                                  in1=dst[:], op0=ALU.mult, op1=ALU.add),
                              reads=[br, lp, dst], writes=[dst])
                    g_, v_, t_, s_ = cg[fc % 2], cv[fc % 2], tt[fc % 2], sg[fc % 2]
                    P.add("act", lambda g_=g_, t_=t_: nc.scalar.activation(out=t_[:], in_=g_[:], func=AF.Square, scale=0.044715 ** 0.5), reads=[g_], writes=[t_])
                    P.add("dve", lambda g_=g_, t_=t_, s_=s_: nc.vector.scalar_tensor_tensor(out=s_[:], in0=t_[:], scalar=1.0, in1=g_[:], op0=ALU.add, op1=ALU.mult),
                          reads=[g_, t_], writes=[s_])
                    P.add("act", lambda s_=s_: nc.scalar.activation(out=s_[:], in_=s_[:], func=AF.Sigmoid, scale=GELU_C), reads=[s_], writes=[s_])
                    P.add("pool", lambda g_=g_, v_=v_: nc.gpsimd.tensor_tensor(out=v_[:], in0=g_[:], in1=v_[:], op=ALU.mult), reads=[g_, v_], writes=[v_])
                    P.add("dve", lambda s_=s_, v_=v_, fc=fc: nc.vector.tensor_tensor(out=act_[:, fc, :], in0=s_[:], in1=v_[:], op=ALU.mult),
                          reads=[s_, v_], writes=[act_])
                for oc in range(8):
                    bk, br = next_bank()
                    P.add("pe", partial(mm_group, bk[:, 0:N], [(Wd[:, k, oc * 128:(oc + 1) * 128], act_[:, k, :]) for k in range(22)]), reads=[Wd, act_], writes=[br])
                    P.add("dve", lambda bk=bk, oc=oc, x_=x_: nc.vector.tensor_tensor(out=x_[:, oc, :], in0=x_[:, oc, :], in1=bk[:, 0:N], op=ALU.add),
                          reads=[br, x_], writes=[x_])
                ho = hT[0]
                rms_fm(x_, gnext[:, :], gnext, ho, N, sq, tmp, rstd)
                dma(xTsU[u][:, :, SL(t0, N)].rearrange("c p t -> p c t"), x_[:], reads=[x_])
                dma(hTsU[u][:, :, SL(t0, N)].rearrange("c p t -> p c t"), ho[:], reads=[ho])
            P.emit()
        A.reset(m)

    def phase_final():
        m = A.mark()
        N = 512
        xT = [A.alloc("xT%d" % i, [8, N], F32) for i in range(2)]
        yT = A.alloc("yT", [8, N], F32)
        ytok = [A.alloc("ytok%d" % i, [1024], F32) for i in range(2)]
        sq = A.alloc("sq", [8, N], BF16)
        tmp = A.alloc("tmp", [N], F32)
        rstd = A.alloc("rstd", [N], F32)
        for u in range(NU):
            def load(fb):
                dma(xT[fb % 2][:], xTsU[u][:, :, SL(fb * N, N)].rearrange("c p t -> p c t"), writes=[xT[fb % 2]])
            load(0)
            for fb in range(UT // N):
                if fb + 1 < UT // N:
                    load(fb + 1)
                x_ = xT[fb % 2]
                t0 = fb * N
                rms_fm(x_, gfin[:, :], gfin, yT, N, sq, tmp, rstd)
                for s in range(N // 128):
                    j = s % 2
                    db = dbank(j)
                    P.add("pe", lambda db=db, s=s: [nc.tensor.transpose(db[:, c * 128:(c + 1) * 128], yT[:, c, s * 128:(s + 1) * 128], identf[:])
                                                     for c in range(8)][-1], reads=[yT, identf], writes=[psr[2 * j], psr[2 * j + 1]])
                    yt = ytok[s % 2]
                    P.add("act", lambda db=db, yt=yt: nc.scalar.copy(yt[:], db), reads=[psr[2 * j], psr[2 * j + 1]], writes=[yt])
                    dma(yU[u][SL(t0 + s * 128, 128), :], yt[:], reads=[yt])
            P.emit()
        A.reset(m)

    setup()
    load_layer_params(0)
    phase0()
    with nc.Fori(0, L) as l:
        load_layer_params(l)
        if "p1" not in SKIP:
            phase1(l)
        if "na" not in SKIP:
            phase_na(l)
        if "ret" not in SKIP:
            phase_ret(l)
        if "p3a" not in SKIP:
            phase3a(l)
        if "p3b" not in SKIP:
            phase3b(l)
    phase_final()
    stats = dict(P.tot)
    stats["sbuf_peak"] = A.peak
    return nc, stats


def _rope_tables(pos):
    f32 = np.float32
    inv_freq = (f32(10000.0) ** (-(np.arange(0, 128, 2, dtype=f32) / f32(128)))).astype(f32)
    ang = (pos.astype(f32)[None, :] * inv_freq[:, None]).astype(f32)
    c = np.cos(ang).astype(f32)
    s = np.sin(ang).astype(f32)
    return np.stack([np.concatenate([c, c], 0), np.concatenate([-s, s], 0)], 0)


def _ctab():
    f32 = np.float32
    j = np.arange(128)[:, None]
    i = np.arange(128)[None, :]
    t = np.zeros((128, CT_N), f32)
    t[:, CT_M1:CT_M1 + 128] = np.maximum(i - j, 0)
    t[:, CT_M2:CT_M2 + 128] = np.maximum(j - i, 0)
    t[:, CT_I1:CT_I1 + 128] = np.broadcast_to(i + 1, (128, 128))
    t[:, CT_I2:CT_I2 + 128] = np.broadcast_to(128 - i, (128, 128))
    t[:, CT_J1] = 127 - np.arange(128)
    t[:, CT_J2] = np.arange(128)
    return t


def _kx():
    k = np.zeros((12, 768), np.float32)
    for m in range(6):
        for a in range(2):
            k[2 * m + a, m * 128 + a * 64:m * 128 + a * 64 + 64] = 1.0
    return k


def _qx(link):
    NU = len(link) - 1
    BIG = 8.0 * NEG
    q = np.full((12, NU, 5, 2, 64), BIG, np.float32)
    for u in range(NU):
        lp, ln = link[u] > 0, link[u + 1] > 0
        for a in range(2):
            q[a:a + 8, u, 0, a, :] = 0.0
            rng = (a, a + 8) if lp else (4, 12)
            q[rng[0]:rng[1], u, 1, a, :] = 0.0
            rng = (a + 2, a + 10) if lp else (4, 12)
            q[rng[0]:rng[1], u, 2, a, :] = 0.0
            rng = (a, a + 8) if ln else (0, 8)
            q[rng[0]:rng[1], u, 3, a, :] = 0.0
            rng = (a + 2, a + 10) if ln else (0, 8)
            q[rng[0]:rng[1], u, 4, a, :] = 0.0
    return q.reshape(12, NU * 5 * 128)


def _ta(rel_bias):
    L = rel_bias.shape[0]
    ap, cp, mm, a, c = np.meshgrid(np.arange(2), np.arange(64), np.arange(6), np.arange(2), np.arange(64), indexing="ij")
    cs = np.clip(c - 8, 0, 48)
    colok = (cp >= cs) & (cp < cs + 16)
    ci = np.clip(cp - c + 15, 0, 30)
    out = np.empty((L, 2, 128, 8, 768), np.float32)
    for t, off in enumerate((4, 6)):
        j = (2 * mm + ap) - (off + a) + 7
        assert j.min() >= 0 and j.max() <= 14
        g = rel_bias[:, :, j, ci]
        g = np.where(colok[None, None], g, np.float32(NEG))
        out[:, t] = np.transpose(g.reshape(L, 8, 128, 768), (0, 2, 1, 3))
    return out


_CACHE = {}


def _get_prog(NU, L):
    key = (NU, L)
    if key not in _CACHE:
        _CACHE[key] = build(NU, L)
    return _CACHE[key]


def make_in_maps(inputs, core_tokens, core_pos, core_links, L):
    f32 = np.float32
    shared = {
        "kx": _kx(), "ctab": _ctab(),
        "wall": np.concatenate([np.asarray(inputs[k][:L], f32).reshape(L, -1) for k in
                                ("w_in", "w_branch_attn", "w_branch_ret", "w_out", "w_up", "w_down")]
                               + [_ta(np.asarray(inputs["na_rel_bias"], f32)[:L]).reshape(L, -1),
                                  np.asarray(inputs["ret_norm_g"][:L], f32).reshape(L, -1)], axis=1),
        "norm_mix_g": np.ascontiguousarray(inputs["norm_mix_g"][:L]).reshape(L, 8, 128),
        "norm_ffn_g": np.ascontiguousarray(inputs["norm_ffn_g"][:L]).reshape(L, 8, 128),
        "norm_final_g": np.ascontiguousarray(inputs["norm_final_g"]).reshape(8, 128),
        "ffn_conv_w": np.ascontiguousarray(inputs["ffn_conv_w"][:L]).reshape(L, 3, 44, 128),
        "ret_decay_fwd": np.ascontiguousarray(inputs["ret_decay_fwd"][:L]).reshape(-1),
        "ret_decay_bwd": np.ascontiguousarray(inputs["ret_decay_bwd"][:L]).reshape(-1),
    }
    maps = []
    for xt, pos, link in zip(core_tokens, core_pos, core_links):
        NU_ = len(link) - 1
        lk = np.zeros((NU_, 128, 2), f32)
        for u_ in range(NU_):
            lk[u_, :, 0] = link[u_]
            lk[u_, :, 1] = link[u_ + 1]
        d = dict(shared)
        d.update({"x": np.ascontiguousarray(xt, dtype=f32), "rope": _rope_tables(pos), "links": lk, "qx": _qx(link)})
        maps.append(d)
    return maps


def kernel(**inputs):
    L = 4
    NU = 4
    inputs = {k: np.asarray(v) for k, v in inputs.items()}
    xp = inputs["x_prompt"]
    xs = inputs["x_sample"]
    nc, _ = _get_prog(NU, L)
    toks, poss, lnks = [], [], []
    ppos = np.tile(np.arange(UT), NU)
    for c in range(4):
        toks.append(xp[4 * c:4 * c + 4].reshape(NU * UT, D))
        poss.append(ppos)
        lnks.append([0, 0, 0, 0, 0])
    toks.append(xs[0])
    poss.append(np.arange(NU * UT))
    lnks.append([0, 1, 1, 1, 0])
    for c in range(5, 8):
        toks.append(toks[0])
        poss.append(ppos)
        lnks.append([0, 0, 0, 0, 0])
    maps = make_in_maps(inputs, toks, poss, lnks, L)
    res = run_bass_kernel_spmd(nc, maps, core_ids=list(range(8)))
    outs = [r["y"] for r in res.results]
    y_prompt = np.stack([outs[c].reshape(4, UT, D) for c in range(4)], 0).reshape(16, UT, D).astype(np.float32)
    y_sample = outs[4].reshape(1, NU * UT, D).astype(np.float32)
    return (y_prompt, y_sample)
```

```python
import numpy as np
import concourse.bass as bass
import concourse.mybir as mybir

F32 = mybir.dt.float32
BF16 = mybir.dt.bfloat16
U8 = mybir.dt.uint8
AF = mybir.ActivationFunctionType
ALU = mybir.AluOpType
AX = mybir.AxisListType

DT_SIZE = {F32: 4, BF16: 2, U8: 1}


class Res:
    __slots__ = ("w", "rc", "rd", "name")

    def __init__(self, name=""):
        self.w = None
        self.rc = {}
        self.rd = []
        self.name = name


class Tile:
    __slots__ = ("ap", "res", "off")

    def __init__(self, ap, res, off=None):
        self.ap = ap
        self.res = res
        self.off = off

    def __getitem__(self, k):
        return self.ap[k]


class Op:
    __slots__ = ("eng", "fn", "reads", "writes", "dma", "deps", "need_inc", "inc", "key", "snap", "barrier")


ENGS = ("pe", "act", "dve", "pool", "sp")


class Prog:
    def __init__(self, nc, n_dma_sems=40):
        self.nc = nc
        self.ops = []
        self.eobj = {"pe": nc.tensor, "act": nc.scalar, "dve": nc.vector, "pool": nc.gpsimd, "sp": nc.sync}
        self.n_dma_sems = n_dma_sems
        self.dma_q = ("sp", "pool", "act")
        self.dma_cnt = {q: 0 for q in self.dma_q}
        self.touched = []
        self.sems = {}
        for e in ENGS:
            self.sems[("c", e)] = nc.alloc_semaphore("c_" + e)
        for i in range(n_dma_sems):
            self.sems[("d", "sp", i)] = nc.alloc_semaphore("d_sp_%d" % i)
        self.tot = dict(n_ops=0, n_wait=0, n_seg=0)

    def add(self, eng, fn, reads=(), writes=(), dma=False):
        op = Op()
        op.eng = eng
        op.fn = fn
        op.reads = [r.res if isinstance(r, Tile) else r for r in reads]
        op.writes = [r.res if isinstance(r, Tile) else r for r in writes]
        op.dma = dma
        op.need_inc = dma
        op.barrier = False
        op.deps = None
        op.snap = None
        self.ops.append(op)
        self.touched.extend(op.reads)
        self.touched.extend(op.writes)
        return op

    def barrier(self):
        self._bar = getattr(self, "_bar", 0) + 1
        for e in ENGS:
            op = self.add(e, None)
            op.barrier = self._bar

    def finalize(self):
        nc = self.nc
        ops = self.ops
        nsem = self.n_dma_sems
        dma_sem_last = {}
        dma_sem_cnt = {}
        qcount = {q: 0 for q in self.dma_q}
        last_on_eng = {e: None for e in ENGS}
        dma_since_barrier = []
        bar_snap = {}
        for i, op in enumerate(ops):
            deps = set()
            if op.barrier:
                if bar_snap.get("id") != op.barrier:
                    bar_snap = dict(id=op.barrier, last=dict(last_on_eng), dmas=list(dma_since_barrier))
                    dma_since_barrier = []
                for e in ENGS:
                    if e != op.eng and bar_snap["last"][e] is not None:
                        deps.add(bar_snap["last"][e])
                deps.update(bar_snap["dmas"])
                op.deps = sorted(deps, reverse=True)
                for d in op.deps:
                    ops[d].need_inc = True
                continue
            for r in op.reads:
                if r.w is not None:
                    deps.add(r.w)
            for r in op.writes:
                if r.w is not None:
                    deps.add(r.w)
                deps.update(r.rc.values())
                deps.update(r.rd)
            if op.dma:
                slot = (op.eng, qcount[op.eng] % nsem)
                qcount[op.eng] += 1
                if slot in dma_sem_last:
                    deps.add(dma_sem_last[slot])
                dma_sem_last[slot] = i
                dma_sem_cnt[slot] = dma_sem_cnt.get(slot, 0) + 16
                op.key = ("d",) + slot
                op.inc = dma_sem_cnt[slot]
                dma_since_barrier.append(i)
            deps.discard(i)
            out = []
            rset = set(id(r) for r in op.reads)
            for d in deps:
                po = ops[d]
                if po.eng == op.eng and not po.dma and not op.dma:
                    if op.eng == "pe":
                        continue
                    raw = False
                    for w in po.writes:
                        if id(w) in rset:
                            raw = True
                            break
                    if not raw:
                        continue
                out.append(d)
            op.deps = sorted(out, reverse=True)
            for d in op.deps:
                ops[d].need_inc = True
            if op.dma:
                for r in op.reads:
                    r.rd.append(i)
            else:
                for r in op.reads:
                    r.rc[op.eng] = i
            for r in op.writes:
                r.w = i
                r.rc = {}
                r.rd = []
            last_on_eng[op.eng] = i
        ccount = {e: 0 for e in ENGS}
        for op in ops:
            if not op.dma and op.need_inc:
                ccount[op.eng] += 1
                op.inc = ccount[op.eng]
                op.key = ("c", op.eng)
        sems = self.sems
        known = {e: {} for e in ENGS}
        nwait = 0
        for op in ops:
            e = op.eng
            k = known[e]
            eo = self.eobj[e]
            for d in op.deps:
                po = ops[d]
                key, val = po.key, po.inc
                if k.get(key, 0) >= val:
                    continue
                eo.wait_ge(sems[key], val)
                nwait += 1
                k[key] = val
                if po.snap is not None:
                    for kk, vv in po.snap.items():
                        if k.get(kk, 0) < vv:
                            k[kk] = vv
            ins = None
            if op.fn is not None:
                ins = op.fn()
            if op.need_inc:
                if ins is None:
                    ins = eo.nop()
                ins.then_inc(sems[op.key], 16 if op.dma else 1)
                snap = dict(k)
                snap[op.key] = op.inc
                op.snap = snap
        sp = self.eobj["sp"]
        ksp = known["sp"]
        for slot, cnt in dma_sem_cnt.items():
            key = ("d",) + slot
            if ksp.get(key, 0) < cnt:
                sp.wait_ge(sems[key], cnt)
        self.stats = dict(n_ops=len(ops), n_wait=nwait)
        self._used = [("d",) + slot for slot in dma_sem_cnt] + [("c", e) for e in ENGS if ccount[e] > 0]
        return self.stats

    def emit(self):
        nc = self.nc
        st = self.finalize()
        nc.all_engine_barrier()
        for key in self._used:
            nc.gpsimd.sem_clear(self.sems[key])
        nc.all_engine_barrier()
        for r in self.touched:
            r.w = None
            r.rc = {}
            r.rd = []
        self.touched = []
        self.ops = []
        self.tot["n_ops"] += st["n_ops"]
        self.tot["n_wait"] += st["n_wait"]
        self.tot["n_seg"] += 1


class Arena:
    def __init__(self, nc, nbytes):
        self.nc = nc
        self.t = nc.alloc_sbuf_tensor("arena", [128, nbytes], U8)
        self.n = nbytes
        self.off = 0
        self.peak = 0

    def mark(self):
        return self.off

    def reset(self, m):
        self.off = m

    def view(self, tile, free_shape, dtype):
        return self.alloc("view", free_shape, dtype, res=tile.res, at=tile.off)

    def alloc(self, name, free_shape, dtype, parts=128, res=None, at=None):
        sz = DT_SIZE[dtype]
        n = int(np.prod(free_shape)) * sz
        if at is None:
            off = (self.off + 31) // 32 * 32
            assert off + n <= self.n, f"SBUF arena overflow allocating {name}: {off}+{n} > {self.n}"
            self.off = off + n
            self.peak = max(self.peak, self.off)
        else:
            off = at
        ap = self.t[0:parts, off:off + n].bitcast(dtype)
        if len(free_shape) == 2:
            ap = ap.rearrange("p (a b) -> p a b", b=free_shape[1])
        elif len(free_shape) == 3:
            ap = ap.rearrange("p (a b c) -> p a b c", b=free_shape[1], c=free_shape[2])
        elif len(free_shape) == 4:
            ap = ap.rearrange("p (a b c d) -> p a b c d", b=free_shape[1], c=free_shape[2], d=free_shape[3])
        return Tile(ap, res if res is not None else Res(name), off)


from functools import partial
from concourse.bass_utils import run_bass_kernel_spmd

D = 1024
DIN = 6656
DFF = 2816
UT = 4096
NEG = -30000.0
C_NAQ, C_NAK, C_NAV, C_RQ, C_RK, C_RV, C_RG, C_GA, C_GR = 0, 512, 1024, 1536, 2048, 2560, 3584, 4608, 5632
CT_M1, CT_M2, CT_I1, CT_I2, CT_J1, CT_J2, CT_N = 0, 128, 256, 384, 512, 513, 514
RSCALE = 128 ** -0.5
GELU_C = 1.5957691216057308
SKIP = set()


def build(NU, L, dbg=False):
    nc = bass.Bass("TRN2", target_bir_lowering=False)
    T = NU * UT
    NB = T // 512
    NCH = T // 128
    NFB = T // 256
    P = Prog(nc)
    A = Arena(nc, 207000)

    def dram(name, shape, dtype, kind="Internal"):
        if dbg and kind == "Internal":
            kind = "ExternalOutput"
        return nc.dram_tensor(name, shape, dtype, kind=kind).ap()

    def SL(a, n):
        return slice(a, a + n)

    def uview(apx, tok_axis, win):
        dims = [list(d) for d in apx.ap]
        s = dims[tok_axis][0]
        new = [[UT * s, NU]] + dims[:tok_axis] + [[s, win]] + dims[tok_axis + 1:]
        return bass.AP(apx.tensor, apx.offset, new)

    x_d = dram("x", [T, D], F32, "ExternalInput")
    rope_d = dram("rope", [2, 128, T], F32, "ExternalInput")
    links_d = dram("links", [NU, 128, 2], F32, "ExternalInput")
    qx_d = dram("qx", [12, NU * 5 * 128], F32, "ExternalInput")
    kx_d = dram("kx", [12, 768], F32, "ExternalInput")
    ctab_d = dram("ctab", [128, CT_N], F32, "ExternalInput")
    gmix_d = dram("norm_mix_g", [L, 8, 128], F32, "ExternalInput")
    gffn_d = dram("norm_ffn_g", [L, 8, 128], F32, "ExternalInput")
    gfin_d = dram("norm_final_g", [8, 128], F32, "ExternalInput")
    convw_d = dram("ffn_conv_w", [L, 3, 44, 128], F32, "ExternalInput")
    decf_d = dram("ret_decay_fwd", [L * 4], F32, "ExternalInput")
    decb_d = dram("ret_decay_bwd", [L * 4], F32, "ExternalInput")
    y_d = dram("y", [T, D], F32, "ExternalOutput")

    xTs = dram("xTs", [8, 128, T], F32)
    hTs = dram("hTs", [8, 128, T], BF16)
    naqT = dram("naqT", [4, 128, T], BF16)
    nakT = dram("nakT", [4, 128, T + 512], BF16)
    navs = dram("navs", [T + 512, 520], BF16)
    rqT = dram("rqT", [4, 128, T], BF16)
    rkT = dram("rkT", [4, 128, T], BF16)
    rvs = dram("rvs", [T, 1024], BF16)
    rgs = dram("rgs", [T, 1024], BF16)
    gTs = dram("gTs", [16, 128, T], BF16)
    Bst = dram("Bst", [NCH, 128, 1024], BF16)
    aTs = dram("aTs", [4, 128, T], BF16)
    roTs = dram("roTs", [8, 128, T], BF16)
    h2Ts = dram("h2Ts", [8, 128, T + 128], BF16)
    ptab = dram("ptab", [L + 1, 128, 160], F32)
    NWL = 1024 * DIN + 512 * D + D * D + D * D + D * 2 * DFF + DFF * D + 2 * 128 * 8 * 768 + D
    wall_d = dram("wall", [L, NWL], F32, "ExternalInput")
    wcur = dram("wcur", [NWL], F32)
    _wo = [0]

    def wview(rows, cols):
        v = bass.AP(wcur.tensor, _wo[0], [[cols, rows], [1, cols]])
        _wo[0] += rows * cols
        return v
    w_in_c = wview(D, DIN)
    w_ba_c = wview(512, D)
    w_br_c = wview(D, D)
    w_out_c = wview(D, D)
    w_up_c = wview(D, 2 * DFF)
    w_dn_c = wview(DFF, D)
    ta_c = bass.AP(wcur.tensor, _wo[0], [[128 * 8 * 768, 2], [8 * 768, 128], [768, 8], [1, 768]])
    _wo[0] += 2 * 128 * 8 * 768
    rng_c = bass.AP(wcur.tensor, _wo[0], [[1, D]])
    xU = uview(x_d, 0, UT)
    yU = uview(y_d, 0, UT)
    rope_dU = uview(rope_d, 2, UT)
    qxU = uview(qx_d, 1, 640) if False else bass.AP(qx_d.tensor, qx_d.offset, [[640, NU], [NU * 640, 12], [1, 640]])
    xTsU = uview(xTs, 2, UT)
    hTsU = uview(hTs, 2, UT)
    naqTU = uview(naqT, 2, UT)
    nakTU = uview(nakT, 2, UT + 512)
    navsU = uview(navs, 0, UT + 512)
    rqTU = uview(rqT, 2, UT)
    rkTU = uview(rkT, 2, UT)
    rvsU = uview(rvs, 0, UT)
    rgsU = uview(rgs, 0, UT)
    gTsU = uview(gTs, 2, UT)
    BstU = Bst.rearrange("(u c) p f -> u c p f", u=NU)
    aTsU = uview(aTs, 2, UT)
    roTsU = uview(roTs, 2, UT)
    h2TsU = uview(h2Ts, 2, UT + 128)

    pp = [nc.alloc_psum_tensor("pp%d" % i, [128, 1024], F32) for i in range(4)]
    psr = [Res("ps%d" % i) for i in range(8)]

    def bank(i):
        return pp[i // 2][:, (i % 2) * 512:(i % 2) * 512 + 512]

    def dbank(j):
        return pp[j][:, :]

    rot = [0]

    def next_bank():
        i = rot[0] % 8
        rot[0] += 1
        return bank(i), psr[i]

    def mm_group(out_ap, pairs):
        n = len(pairs)
        ins = None
        for i, (a, b) in enumerate(pairs):
            ins = nc.tensor.matmul(out_ap, a, b, start=(i == 0), stop=(i == n - 1))
        return ins

    def dma(out, in_, reads=(), writes=(), **kw):
        return P.add("sp", lambda: nc.sync.dma_start(out=out, in_=in_, **kw), reads=reads, writes=writes, dma=True)

    ident = A.alloc("ident", [128], BF16)
    identf = A.alloc("identf", [128], F32)
    ones = A.alloc("ones", [128], BF16)
    eps_t = A.alloc("eps", [1], F32)
    zero_t = A.alloc("zero", [520], BF16)
    lk2 = A.alloc("lk2", [2], F32)
    kx = A.alloc("kx", [768], BF16)
    qx = A.alloc("qx", [640], BF16)
    qxs = A.alloc("qxs", [640], F32)
    ctab = A.alloc("ctab", [CT_N], F32)
    lp = A.alloc("lp", [160], F32)
    gnext = A.alloc("gnext", [8], F32)
    gfin = A.alloc("gfin", [8], F32)
    gmix_ap = lp[:, 0:8]
    gffn_ap = lp[:, 8:16]
    convw_v = lp[:, 16:148].rearrange("p (k c) -> p k c", c=44)
    lg_v = lp[:, 148:156]

    def setup():
        m = A.mark()
        stg = A.alloc("stg0", [1024], F32)
        stg2 = A.alloc("stg1", [128], F32)
        gmix = A.alloc("gmix", [L + 1, 8], F32)
        gffn = A.alloc("gffn", [L, 8], F32)
        convw = A.alloc("convw", [L, 3, 44], F32)
        lg = A.alloc("lg", [L * 8], F32)
        P.add("pool", lambda: nc.gpsimd.memset(identf[:], 1.0), writes=[identf])
        P.add("pool", lambda: nc.gpsimd.affine_select(out=identf[:], in_=identf[:], pattern=[[-1, 128]],
                                                       compare_op=ALU.is_equal, fill=0.0, base=0, channel_multiplier=1),
              reads=[identf], writes=[identf])
        P.add("dve", lambda: nc.vector.tensor_copy(ident[:], identf[:]), reads=[identf], writes=[ident])
        P.add("pool", lambda: nc.gpsimd.memset(ones[:], 1.0), writes=[ones])
        P.add("pool", lambda: nc.gpsimd.memset(eps_t[:], 1e-6), writes=[eps_t])
        P.add("pool", lambda: nc.gpsimd.memset(zero_t[:], 0.0), writes=[zero_t])
        dma(ctab[:], ctab_d[:, :], writes=[ctab])
        dma(stg[0:12, 0:768], kx_d[:, :], writes=[stg])
        P.add("dve", lambda: nc.vector.tensor_copy(kx[0:12, :], stg[0:12, 0:768]), reads=[stg], writes=[kx])

        def colvec(dst_ap, src_ap, n):
            dma(stg2[0:n, :], src_ap, writes=[stg2])
            bk, br = next_bank()
            P.add("pe", lambda: nc.tensor.transpose(bk[:, 0:n], stg2[0:n, :], identf[0:n, 0:n]),
                  reads=[stg2, identf], writes=[br])
            P.add("dve", lambda: nc.vector.tensor_copy(dst_ap, bk[:, 0:n]), reads=[br], writes=[gmix, gffn, convw, gfin])
        for l in range(L):
            colvec(gmix[:, l, :], gmix_d[l], 8)
            colvec(gffn[:, l, :], gffn_d[l], 8)
            for k in range(3):
                colvec(convw[:, l, k, :], convw_d[l, k], 44)
        colvec(gfin[:, :], gfin_d[:, :], 8)
        colvec(gmix[:, L, :], gfin_d[:, :], 8)
        dec = A.alloc("dec", [L * 8], F32)
        dv = dec[:].rearrange("p (l e) -> p l e", e=8)
        dma(dv[:, :, 0:4], decf_d.rearrange("(l h) -> l h", h=4).partition_broadcast(128), writes=[dec])
        dma(dv[:, :, 4:8], decb_d.rearrange("(l h) -> l h", h=4).partition_broadcast(128), writes=[dec])
        P.add("act", lambda: nc.scalar.activation(out=dec[:], in_=dec[:], func=AF.Exp, scale=-1.0), reads=[dec], writes=[dec])
        P.add("act", lambda: nc.scalar.activation(out=dec[:], in_=dec[:], func=AF.Ln, bias=1.0), reads=[dec], writes=[dec])
        P.add("dve", lambda: nc.vector.tensor_scalar(lg[:], dec[:], -1.0, None, ALU.mult), reads=[dec], writes=[lg])
        for c in range(4):
            dma(nakT[c, :, 0:256], zero_t[:, 0:256], reads=[zero_t])
            dma(nakT[c, :, T + 256:T + 512], zero_t[:, 0:256], reads=[zero_t])
        for r0 in (0, 128, T + 256, T + 384):
            dma(navs[r0:r0 + 128, :], zero_t[:, :], reads=[zero_t])
        for c in range(8):
            dma(h2Ts[c, :, 0:64], zero_t[:, 0:64], reads=[zero_t])
            dma(h2Ts[c, :, T + 64:T + 128], zero_t[:, 0:64], reads=[zero_t])
        for l in range(L + 1):
            dma(ptab[l, :, 0:8], gmix[:, l, :], reads=[gmix])
            if l < L:
                dma(ptab[l, :, 8:16], gffn[:, l, :], reads=[gffn])
                dma(ptab[l, :, 16:148], convw[:, l, :, :].rearrange("p k c -> p (k c)"), reads=[convw])
                dma(ptab[l, :, 148:156], lg[:, l * 8:(l + 1) * 8], reads=[lg])
        P.emit()
        A.reset(m)

    def load_unit_params(u):
        dma(lk2[:], links_d[u], writes=[lk2])

    def load_layer_params(l):
        dma(wcur.rearrange("(a b) -> a b", a=16), wall_d[l].rearrange("(a b) -> a b", a=16))
        dma(lp[:], ptab[l], writes=[lp])
        dma(gnext[:], ptab[l + 1][:, 0:8], writes=[gnext])
        P.emit()

    cast_rr = [0]

    def cast(out_ap, in_ap, reads, writes):
        e = ("dve", "act", "pool")[cast_rr[0] % 3]
        cast_rr[0] += 1
        if e == "dve":
            P.add("dve", lambda: nc.vector.tensor_copy(out_ap, in_ap), reads=reads, writes=writes)
        elif e == "act":
            P.add("act", lambda: nc.scalar.copy(out_ap, in_ap), reads=reads, writes=writes)
        else:
            P.add("pool", lambda: nc.gpsimd.tensor_copy(out_ap, in_ap), reads=reads, writes=writes)

    def load_w(dst, src2d, KC, c_lo, c_hi, stage, dst_c0=None, swap=None):
        if dst_c0 is None:
            dst_c0 = c_lo
        i = 0
        for kc in range(KC):
            for c0 in range(c_lo, c_hi, 2048):
                w = min(2048, c_hi - c0)
                st = stage[i % 2]
                i += 1
                dma(st[:, 0:w], src2d[kc * 128:(kc + 1) * 128, c0:c0 + w], writes=[st])
                d0 = dst_c0 + (c0 - c_lo)
                cast(dst[:, kc, d0:d0 + w], st[:, 0:w], [st], [dst])
                if swap is not None:
                    sv = st[:, 0:w].rearrange("p (h t e) -> p h t e", t=2, e=64)
                    ov = swap[:, kc, d0:d0 + w].rearrange("p (h t e) -> p h t e", t=2, e=64)
                    cast(ov[:, :, 0, :], sv[:, :, 1, :], [st], [swap])
                    cast(ov[:, :, 1, :], sv[:, :, 0, :], [st], [swap])

    def rms_fm(xT, g_ap, g_tile, out, N, sq, tmp, rstd):
        P.add("act", lambda: nc.scalar.activation(out=sq[:].rearrange("p a b -> p (a b)"),
                                                  in_=xT[:].rearrange("p a b -> p (a b)"), func=AF.Square),
              reads=[xT], writes=[sq])
        bk, br = next_bank()
        P.add("pe", partial(mm_group, bk[:, 0:N], [(ones[:], sq[:, k, :]) for k in range(8)]), reads=[sq, ones], writes=[br])
        P.add("act", lambda: nc.scalar.activation(out=tmp[:], in_=bk[:, 0:N], func=AF.Sqrt, bias=eps_t[:, 0:1], scale=1.0 / D),
              reads=[br, eps_t], writes=[tmp])
        P.add("dve", lambda: nc.vector.reciprocal(rstd[:], tmp[:]), reads=[tmp], writes=[rstd])
        for k in range(8):
            P.add("dve", lambda k=k: nc.vector.scalar_tensor_tensor(out=out[:, k, :], in0=xT[:, k, :], scalar=g_ap[:, k:k + 1],
                                                                    in1=rstd[:], op0=ALU.mult, op1=ALU.mult),
                  reads=[xT, rstd, g_tile], writes=[out])

    def phase0():
        m = A.mark()
        xin = [A.alloc("xin%d" % i, [1024], F32) for i in range(2)]
        xT = [A.alloc("xT%d" % i, [8, 512], F32) for i in range(2)]
        hT = [A.alloc("hT%d" % i, [8, 512], BF16) for i in range(2)]
        sq = A.alloc("sq", [8, 512], BF16)
        tmp = A.alloc("tmp", [512], F32)
        rstd = A.alloc("rstd", [512], F32)
        for u in range(NU):
            for b in range(8):
                xt = xT[b % 2]
                ht = hT[b % 2]
                for s in range(4):
                    xi = xin[s % 2]
                    r0 = b * 512 + s * 128
                    dma(xi[:], xU[u][r0:r0 + 128, :], writes=[xi])
                    j = (b * 4 + s) % 4
                    db = dbank(j)
                    P.add("pe", lambda db=db, xi=xi: [nc.tensor.transpose(db[:, c * 128:(c + 1) * 128], xi[:, c * 128:(c + 1) * 128], identf[:])
                                                       for c in range(8)][-1],
                          reads=[xi, identf], writes=[psr[2 * j], psr[2 * j + 1]])
                    e = "act" if s % 2 else "dve"
                    if e == "dve":
                        P.add("dve", lambda db=db, xt=xt, s=s: nc.vector.tensor_copy(xt[:, :, s * 128:(s + 1) * 128], db.rearrange("p (c t) -> p c t", t=128)),
                              reads=[psr[2 * j], psr[2 * j + 1]], writes=[xt])
                    else:
                        P.add("act", lambda db=db, xt=xt, s=s: nc.scalar.copy(xt[:, :, s * 128:(s + 1) * 128], db.rearrange("p (c t) -> p c t", t=128)),
                              reads=[psr[2 * j], psr[2 * j + 1]], writes=[xt])
                rms_fm(xt, gmix_ap, lp, ht, 512, sq, tmp, rstd)
                dma(xTsU[u][:, :, SL(b * 512, 512)].rearrange("c p t -> p c t"), xt[:], reads=[xt])
                dma(hTsU[u][:, :, SL(b * 512, 512)].rearrange("c p t -> p c t"), ht[:], reads=[ht])
            P.emit()
        A.reset(m)

    def phase1(l):
        m = A.mark()
        W = A.alloc("w_in", [8, DIN], BF16)
        Wsw = A.alloc("w_sw", [8, 1024], BF16)
        m2 = A.mark()
        stage = [A.alloc("wst%d" % i, [2048], F32) for i in range(2)]
        wsrc = w_in_c
        load_w(W, wsrc, 8, 0, DIN, stage)
        i = 0
        for kc in range(8):
            st = stage[i % 2]
            i += 1
            dma(st[:, 0:1024], wsrc[kc * 128:(kc + 1) * 128, C_RQ:C_RV], writes=[st])
            sv = st[:, 0:1024].rearrange("p (h t e) -> p h t e", t=2, e=64)
            ov = Wsw[:, kc, :].rearrange("p (h t e) -> p h t e", t=2, e=64)
            cast(ov[:, :, 0, :], sv[:, :, 1, :], [st], [Wsw])
            cast(ov[:, :, 1, :], sv[:, :, 0, :], [st], [Wsw])
        P.emit()
        A.reset(m2)
        hT = [A.alloc("hT%d" % i, [8, 512], BF16) for i in range(2)]
        cs = [A.alloc("cs%d" % i, [2, 512], F32) for i in range(2)]
        G = [A.alloc("G%d" % i, [4, 512], BF16) for i in range(6)]
        TM = [A.alloc("TM%d" % i, [2568], BF16) for i in range(2)]
        rt = [A.alloc("rt%d" % i, [512], F32) for i in range(4)]
        gi = [0]
        for u in range(NU):
            for tm in TM:
                P.add("pool", lambda tm=tm: nc.gpsimd.memset(tm[:, 0:520].rearrange("p (h e) -> p h e", e=65)[:, :, 64:65], 1.0), writes=[tm])

            def load(b):
                dma(hT[b % 2][:], hTsU[u][:, :, SL(b * 512, 512)].rearrange("c p t -> p c t"), writes=[hT[b % 2]])
                dma(cs[b % 2][:], rope_dU[u][:, :, SL(b * 512, 512)].rearrange("c p t -> p c t"), writes=[cs[b % 2]])

            load(0)
            for b in range(8):
                if b + 1 < 8:
                    load(b + 1)
                h = hT[b % 2]
                c_s = cs[b % 2]
                t0 = b * 512

                def fm_chunk(col, Wt=W):
                    bk, br = next_bank()
                    P.add("pe", partial(mm_group, bk, [(Wt[:, k, col:col + 128], h[:, k, :]) for k in range(8)]),
                          reads=[Wt, h], writes=[br])
                    return bk, br

                for (c0, dst, nchunk, func) in ((C_NAQ, naqT, 4, AF.Copy), (C_NAK, nakT, 4, AF.Copy),
                                               (C_GA, gTs, 8, AF.Sigmoid), (C_GR, gTs, 8, AF.Sigmoid)):
                    for g0 in range(0, nchunk, 4):
                        gt = G[gi[0] % 6]
                        gi[0] += 1
                        for j in range(4):
                            bk, br = fm_chunk(c0 + (g0 + j) * 128)
                            P.add("act", lambda bk=bk, gt=gt, j=j, func=func: nc.scalar.activation(out=gt[:, j, :], in_=bk, func=func),
                                  reads=[br], writes=[gt])
                        if dst is nakT:
                            dap = nakTU[u][:, :, SL(256 + t0, 512)]
                        elif dst is gTs:
                            cb = (0 if c0 == C_GA else 8) + g0
                            dap = gTsU[u][cb:cb + 4, :, SL(t0, 512)]
                        else:
                            dap = naqTU[u][:, :, SL(t0, 512)]
                        dma(dap.rearrange("c p t -> p c t"), gt[:], reads=[gt])
                for (c0, dst) in ((C_RQ, rqTU), (C_RK, rkTU)):
                    gt = G[gi[0] % 6]
                    gi[0] += 1
                    for j in range(4):
                        bk, br = fm_chunk(c0 + j * 128)
                        bk2, br2 = fm_chunk(c0 - C_RQ + j * 128, Wt=Wsw)
                        r1 = rt[(2 * j) % 4]
                        r2 = rt[(2 * j + 1) % 4]
                        P.add("dve", lambda bk=bk, r1=r1, c_s=c_s: nc.vector.tensor_tensor(out=r1[:], in0=bk, in1=c_s[:, 0, :], op=ALU.mult),
                              reads=[br, c_s], writes=[r1])
                        P.add("dve", lambda bk2=bk2, r2=r2, c_s=c_s: nc.vector.tensor_tensor(out=r2[:], in0=bk2, in1=c_s[:, 1, :], op=ALU.mult),
                              reads=[br2, c_s], writes=[r2])
                        P.add("pool", lambda r1=r1, r2=r2, gt=gt, j=j: nc.gpsimd.tensor_tensor(out=gt[:, j, :], in0=r1[:], in1=r2[:], op=ALU.add),
                              reads=[r1, r2], writes=[gt])
                    dma(dst[u][:, :, SL(t0, 512)].rearrange("c p t -> p c t"), gt[:], reads=[gt])
                for s in range(4):
                    tm = TM[s % 2]
                    for gidx, col in enumerate((C_NAV, C_RV, C_RV + 512, C_RG, C_RG + 512)):
                        bk, br = next_bank()
                        P.add("pe", partial(mm_group, bk, [(h[:, k, s * 128:(s + 1) * 128], W[:, k, col:col + 512]) for k in range(8)]),
                              reads=[W, h], writes=[br])
                        if gidx == 0:
                            o = tm[:, 0:520].rearrange("p (h e) -> p h e", e=65)[:, :, 0:64]
                            P.add("dve", lambda bk=bk, o=o: nc.vector.tensor_copy(o, bk.rearrange("p (h e) -> p h e", e=64)), reads=[br], writes=[tm])
                            continue
                        o = tm[:, 8 + gidx * 512:8 + (gidx + 1) * 512]
                        if gidx >= 3:
                            P.add("act", lambda bk=bk, o=o: nc.scalar.activation(out=o, in_=bk, func=AF.Silu), reads=[br], writes=[tm])
                        else:
                            P.add("dve", lambda bk=bk, o=o: nc.vector.tensor_copy(o, bk), reads=[br], writes=[tm])
                    r0 = t0 + s * 128
                    dma(navsU[u][SL(256 + r0, 128), :], tm[:, 0:520], reads=[tm])
                    dma(rvsU[u][SL(r0, 128), :], tm[:, 520:1544], reads=[tm])
                    dma(rgsU[u][SL(r0, 128), :], tm[:, 1544:2568], reads=[tm])
            P.emit()
        A.reset(m)

    def phase_na(l):
        m = A.mark()
        kT = A.alloc("kT", [4, 4608], BF16)
        qT = A.alloc("qT", [4, 4096], BF16)
        V = A.alloc("V", [36, 8, 65], BF16)
        TA = A.alloc("TA", [8, 768], F32)
        TB = A.alloc("TB", [8, 768], F32)
        tmp = [A.alloc("natmp%d" % i, [768], F32) for i in range(2)]
        E = [A.alloc("naE%d" % i, [768], BF16) for i in range(2)]
        atok = [A.alloc("atok%d" % i, [8, 64], BF16) for i in range(2)]
        rden = [A.alloc("rden%d" % i, [8], F32) for i in range(2)]
        aTb = [A.alloc("aTb%d" % i, [4, 512], BF16) for i in range(2)]
        dma(TA[:], ta_c[0], writes=[TA])
        dma(TB[:], ta_c[1], writes=[TB])
        P.emit()
        cnt = 0
        S_res = [[psr[0], psr[1]], [psr[2], psr[3]]]
        PV_res = [psr[4], psr[5]]
        PV = dbank(2).rearrange("p (h e) -> p h e", e=128)
        psT = bank(6).bitcast(BF16)
        for u in range(NU):
            cnt = 0
            dma(qxs[0:12, :], qxU[u], writes=[qxs])
            P.add("dve", lambda: nc.vector.tensor_copy(qx[0:12, :], qxs[0:12, :]), reads=[qxs], writes=[qx])
            for c in range(4):
                dma(kT[:, c, :], nakTU[u][c, :, :], writes=[kT])
                dma(qT[:, c, :], naqTU[u][c, :, :], writes=[qT])
            for g0 in range(0, 36, 9):
                dma(V[:, g0:g0 + 9, :, :].rearrange("p g h e -> p g (h e)"), navsU[u][SL(g0 * 128, 9 * 128), :].rearrange("(g p) e -> p g e", p=128),
                    writes=[V])
            def pinfo(p):
                if p == 0:
                    return 1, TA, 6, 0
                if p == 1:
                    return 2, TB, 6, 0
                if p == 30:
                    return 3, TA, 6, 30
                if p == 31:
                    return 4, TB, 6, 30
                return 0, TA, 5, p

            def emit_s(p, h, si):
                tp, Tt, nch, kt0 = pinfo(p)
                qxo = tp * 128
                hc, hp = h // 2, (h % 2) * 64
                S = dbank(si)

                def s_mm():
                    ins = None
                    for mth in range(nch):
                        o = S[:, mth * 128:(mth + 1) * 128]
                        nc.tensor.matmul(o, kT[hp:hp + 64, hc, (kt0 + mth) * 128:(kt0 + mth + 1) * 128],
                                         qT[hp:hp + 64, hc, p * 128:(p + 1) * 128], start=True, stop=False)
                        ins = nc.tensor.matmul(o, kx[0:12, mth * 128:(mth + 1) * 128], qx[0:12, qxo:qxo + 128], start=False, stop=True)
                    return ins
                P.add("pe", s_mm, reads=[kT, qT, kx, qx], writes=S_res[si])

            def emit_rest(p, h, si):
                tp, Tt, nch, kt0 = pinfo(p)
                S = dbank(si)
                tm_, e_ = tmp[si], E[si]
                w = nch * 128
                P.add("dve", lambda: nc.vector.scalar_tensor_tensor(
                    out=tm_[:, 0:w], in0=S[:, 0:w], scalar=0.125, in1=Tt[:, h, 0:w], op0=ALU.mult, op1=ALU.add),
                    reads=S_res[si] + [Tt.res], writes=[tm_])
                P.add("act", lambda: nc.scalar.activation(out=e_[:, 0:w], in_=tm_[:, 0:w], func=AF.Exp),
                      reads=[tm_], writes=[e_])
                P.add("pe", partial(mm_group, PV[:, h, 0:65],
                                    [(e_[:, mth * 128:(mth + 1) * 128], V[:, kt0 + mth, h, :]) for mth in range(nch)]),
                      reads=[e_, V], writes=PV_res)

            def emit_pair_end(p):
                at, rd = atok[p % 2], rden[p % 2]
                P.add("dve", lambda: nc.vector.reciprocal(rd[:], PV[:, :, 64]), reads=PV_res, writes=[rd])
                P.add("dve", lambda: nc.vector.tensor_tensor(out=at[:], in0=PV[:, :, 0:64],
                                                             in1=rd[:].unsqueeze(2).to_broadcast([128, 8, 64]), op=ALU.mult),
                      reads=PV_res + [rd.res], writes=[at])
                atf = at[:].rearrange("p h e -> p (h e)")
                P.add("pe", lambda: [nc.tensor.transpose(psT[:, c * 128:(c + 1) * 128], atf[:, c * 128:(c + 1) * 128], ident[:])
                                     for c in range(4)][-1], reads=[at, ident], writes=[psr[6]])
                blk = p // 4
                ab = aTb[blk % 2]
                P.add("act", lambda: nc.scalar.copy(ab[:, :, (p % 4) * 128:(p % 4 + 1) * 128],
                                                    psT[:, 0:512].rearrange("p (c t) -> p c t", t=128)),
                      reads=[psr[6]], writes=[ab])
                if p % 4 == 3:
                    dma(aTsU[u][:, :, SL(blk * 512, 512)].rearrange("c p t -> p c t"), ab[:], reads=[ab])

            items = [(p, h) for p in range(32) for h in range(8)]
            emit_s(items[0][0], items[0][1], 0)
            for i, (p, h) in enumerate(items):
                if i + 1 < len(items):
                    emit_s(items[i + 1][0], items[i + 1][1], (i + 1) % 2)
                emit_rest(p, h, i % 2)
                if h == 7:
                    emit_pair_end(p)
            P.emit()
        A.reset(m)

    def phase_ret(l):
        m = A.mark()
        DT = A.alloc("DT", [4, 128], F32)
        QDF = A.alloc("QDF", [4, 128], F32)
        QDB = A.alloc("QDB", [4, 128], F32)
        kdf = A.alloc("kdf", [4], F32)
        kdb = A.alloc("kdb", [4], F32)
        gcf = A.alloc("gcf", [4], F32)
        gcb = A.alloc("gcb", [4], F32)
        gbc = A.alloc("gbc", [1024], F32)
        St = A.alloc("St", [4, 256], F32)
        Sbf = [A.alloc("Sbf%d" % i, [4, 256], BF16) for i in range(2)]
        arg = A.alloc("arg", [128], F32)
        lg = lp
        lgf = lambda h: lp[:, 148 + h:148 + h + 1]
        lgb = lambda h: lp[:, 152 + h:152 + h + 1]
        tabs = [DT, QDF, QDB, kdf, kdb, gcf, gcb]
        for h in range(4):
            P.add("dve", lambda h=h: nc.vector.tensor_scalar(arg[:], ctab[:, CT_M1:CT_M1 + 128], lgf(h), None, ALU.mult),
                  reads=[ctab, lg], writes=[arg])
            P.add("dve", lambda h=h: nc.vector.scalar_tensor_tensor(out=arg[:], in0=ctab[:, CT_M2:CT_M2 + 128], scalar=lgb(h), in1=arg[:],
                                                                    op0=ALU.mult, op1=ALU.add), reads=[ctab, lg, arg], writes=[arg])
            P.add("act", lambda h=h: nc.scalar.activation(out=DT[:, h, :], in_=arg[:], func=AF.Exp), reads=[arg], writes=[DT])
            P.add("act", lambda h=h: nc.scalar.activation(out=QDF[:, h, :], in_=ctab[:, CT_I1:CT_I1 + 128], func=AF.Exp, scale=lgf(h)),
                  reads=[ctab, lg], writes=[QDF])
            P.add("act", lambda h=h: nc.scalar.activation(out=QDB[:, h, :], in_=ctab[:, CT_I2:CT_I2 + 128], func=AF.Exp, scale=lgb(h)),
                  reads=[ctab, lg], writes=[QDB])
            P.add("act", lambda h=h: nc.scalar.activation(out=kdf[:, h:h + 1], in_=ctab[:, CT_J1:CT_J1 + 1], func=AF.Exp, scale=lgf(h)),
                  reads=[ctab, lg], writes=[kdf])
            P.add("act", lambda h=h: nc.scalar.activation(out=kdb[:, h:h + 1], in_=ctab[:, CT_J2:CT_J2 + 1], func=AF.Exp, scale=lgb(h)),
                  reads=[ctab, lg], writes=[kdb])
            P.add("act", lambda h=h: nc.scalar.activation(out=gcf[:, h:h + 1], in_=lgf(h), func=AF.Exp, scale=128.0), reads=[lg], writes=[gcf])
            P.add("act", lambda h=h: nc.scalar.activation(out=gcb[:, h:h + 1], in_=lgb(h), func=AF.Exp, scale=128.0), reads=[lg], writes=[gcb])
        for t_ in (DT, kdf, kdb):
            P.add("dve", lambda t_=t_: nc.vector.tensor_scalar(t_[:], t_[:], RSCALE, None, ALU.mult), reads=[t_], writes=[t_])
        dma(gbc[:], rng_c.partition_broadcast(128), writes=[gbc])
        P.emit()

        kTc = [A.alloc("kTc%d" % i, [4, 128], BF16) for i in range(2)]
        qTc = [A.alloc("qTc%d" % i, [4, 128], BF16) for i in range(2)]
        vc = [A.alloc("vc%d" % i, [1024], BF16) for i in range(2)]
        gsc = [A.alloc("gsc%d" % i, [1024], BF16) for i in range(2)]
        Bc = [A.alloc("Bc%d" % i, [4, 256], BF16) for i in range(2)]
        kd = [A.alloc("kd%d" % i, [4, 128], BF16) for i in range(2)]
        psT = bank(0).bitcast(BF16)[:, 0:512].rearrange("p (h d) -> p h d", d=128)
        psT_r = [psr[0]]
        psS = bank(1).rearrange("p (h i) -> p h i", i=128)
        psS_r = [psr[1]]
        psO = dbank(1).rearrange("p (h v) -> p h v", v=256)
        psO_r = [psr[2], psr[3]]
        psF = dbank(2).rearrange("p (h v) -> p h v", v=256)
        psF_r = [psr[4], psr[5]]
        psR = bank(6).bitcast(BF16).rearrange("p (c t) -> p c t", t=128)
        psR_r = [psr[6]]

        def state_update(gc, k_d, v_):
            P.add("pe", lambda: [nc.tensor.matmul(psF[:, h, :], k_d[:, h, :], v_[:, h * 256:(h + 1) * 256], start=True, stop=True)
                                 for h in range(4)][-1], reads=[k_d, v_], writes=psF_r)
            P.add("dve", lambda: nc.vector.tensor_tensor(out=St[:], in0=St[:], in1=gc[:].unsqueeze(2).to_broadcast([128, 4, 256]), op=ALU.mult),
                  reads=[St, gc], writes=[St])
            P.add("dve", lambda: nc.vector.tensor_tensor(out=St[:], in0=St[:], in1=psF, op=ALU.add), reads=[St] + psF_r, writes=[St])

        def k_tokmajor(kt, kdec, out):
            P.add("pe", lambda: [nc.tensor.transpose(psT[:, h, :], kt[:, h, :], ident[:]) for h in range(4)][-1],
                  reads=[kt, ident], writes=psT_r)
            P.add("dve", lambda: nc.vector.tensor_tensor(out=out[:], in0=psT, in1=kdec[:].unsqueeze(2).to_broadcast([128, 4, 128]), op=ALU.mult),
                  reads=psT_r + [kdec.res], writes=[out])

        P.add("pool", lambda: nc.gpsimd.memset(St[:], 0.0), writes=[St])
        P.emit()
        for ui in range(NU):
            u = (NU - 1) - ui
            load_unit_params(u)

            def bload(c):
                dma(kTc[c % 2][:], rkTU[u][:, :, SL(c * 128, 128)].rearrange("h p t -> p h t"), writes=[kTc[c % 2]])
                dma(vc[c % 2][:], rvsU[u][SL(c * 128, 128), :], writes=[vc[c % 2]])
            bload(31)
            for c in range(31, -1, -1):
                if c - 1 >= 0:
                    bload(c - 1)
                if c == 31:
                    P.add("dve", lambda: nc.vector.tensor_scalar(St[:], St[:], lk2[:, 1:2], None, ALU.mult),
                          reads=[St, lk2], writes=[St])
                sb = Sbf[c % 2]
                P.add("act", lambda sb=sb: nc.scalar.copy(sb[:], St[:]), reads=[St], writes=[sb])
                dma(BstU[u][c].rearrange("p (h v) -> p h v", v=256), sb[:], reads=[sb])
                k_tokmajor(kTc[c % 2], kdb, kd[c % 2])
                state_update(gcb, kd[c % 2], vc[c % 2])
            P.emit()

        AT = [A.alloc("AT%d" % i, [4, 128], BF16) for i in range(2)]
        qdf = [A.alloc("qdf%d" % i, [4, 128], BF16) for i in range(2)]
        qdb = [A.alloc("qdb%d" % i, [4, 128], BF16) for i in range(2)]
        stt = A.alloc("bnst", [4, 6], F32)
        mv = A.alloc("bnmv", [4, 2], F32)
        sd = A.alloc("bnsd", [4], F32)
        rs_ = A.alloc("bnrs", [4], F32)
        yn = A.alloc("yn", [1024], F32)
        gs = A.alloc("gs", [1024], F32)
        ro = [A.alloc("ro%d" % i, [1024], BF16) for i in range(2)]
        roT = [A.alloc("roT%d" % i, [8, 512], BF16) for i in range(2)]
        P.add("pool", lambda: nc.gpsimd.memset(St[:], 0.0), writes=[St])
        P.emit()
        Fb = Sbf[0]
        for u in range(NU):
            load_unit_params(u)

            def fload(c):
                i = c % 2
                dma(kTc[i][:], rkTU[u][:, :, SL(c * 128, 128)].rearrange("h p t -> p h t"), writes=[kTc[i]])
                dma(qTc[i][:], rqTU[u][:, :, SL(c * 128, 128)].rearrange("h p t -> p h t"), writes=[qTc[i]])
                dma(vc[i][:], rvsU[u][SL(c * 128, 128), :], writes=[vc[i]])
                dma(gsc[i][:], rgsU[u][SL(c * 128, 128), :], writes=[gsc[i]])
                dma(Bc[i][:], BstU[u][c].rearrange("p (h v) -> p h v", v=256), writes=[Bc[i]])
            fload(0)
            for c in range(32):
                if c + 1 < 32:
                    fload(c + 1)
                i = c % 2
                kt, qt, v_, g_, bc = kTc[i], qTc[i], vc[i], gsc[i], Bc[i]
                if c == 0:
                    P.add("dve", lambda: nc.vector.tensor_scalar(St[:], St[:], lk2[:, 0:1], None, ALU.mult),
                          reads=[St, lk2], writes=[St])
                P.add("act", lambda: nc.scalar.copy(Fb[:], St[:]), reads=[St], writes=[Fb])
                P.add("pe", lambda kt=kt, qt=qt: [nc.tensor.matmul(psS[:, h, :], kt[:, h, :], qt[:, h, :], start=True, stop=True) for h in range(4)][-1],
                      reads=[kt, qt], writes=psS_r)
                at = AT[i]
                P.add("dve", lambda at=at: nc.vector.tensor_tensor(out=at[:], in0=psS, in1=DT[:], op=ALU.mult), reads=psS_r + [DT.res], writes=[at])
                qf, qb = qdf[i], qdb[i]
                P.add("pool", lambda qt=qt, qf=qf: nc.gpsimd.tensor_tensor(out=qf[:], in0=qt[:], in1=QDF[:], op=ALU.mult), reads=[qt, QDF], writes=[qf])
                P.add("pool", lambda qt=qt, qb=qb: nc.gpsimd.tensor_tensor(out=qb[:], in0=qt[:], in1=QDB[:], op=ALU.mult), reads=[qt, QDB], writes=[qb])

                def o_mm(at=at, qf=qf, qb=qb, v_=v_, bc=bc):
                    ins = None
                    for h in range(4):
                        nc.tensor.matmul(psO[:, h, :], at[:, h, :], v_[:, h * 256:(h + 1) * 256], start=True, stop=False)
                        nc.tensor.matmul(psO[:, h, :], qf[:, h, :], Fb[:, h, :], start=False, stop=False)
                        ins = nc.tensor.matmul(psO[:, h, :], qb[:, h, :], bc[:, h, :], start=False, stop=True)
                    return ins
                P.add("pe", o_mm, reads=[at, qf, qb, v_, bc, Fb], writes=psO_r)
                k_tokmajor(kt, kdf, kd[i])
                state_update(gcf, kd[i], v_)
                def bn():
                    for h in range(4):
                        nc.vector.bn_stats(stt[:, h, :], psO[:, h, :])
                    ins = None
                    for h in range(4):
                        ins = nc.vector.bn_aggr(mv[:, h, :], stt[:, h, :])
                    return ins
                P.add("dve", bn, reads=psO_r, writes=[stt, mv])
                P.add("act", lambda: nc.scalar.activation(out=sd[:], in_=mv[:, :, 1], func=AF.Sqrt, bias=eps_t[:, 0:1], scale=1.0),
                      reads=[mv, eps_t], writes=[sd])
                P.add("dve", lambda: nc.vector.reciprocal(rs_[:], sd[:]), reads=[sd], writes=[rs_])
                for h in range(4):
                    P.add("dve", lambda h=h: nc.vector.tensor_scalar(yn[:, h * 256:(h + 1) * 256], psO[:, h, :], mv[:, h, 0:1], rs_[:, h:h + 1],
                                                                     ALU.subtract, ALU.mult), reads=psO_r + [mv.res, rs_.res], writes=[yn])
                P.add("pool", lambda g_=g_: nc.gpsimd.tensor_tensor(out=gs[:], in0=g_[:], in1=gbc[:], op=ALU.mult), reads=[g_, gbc], writes=[gs])
                r_ = ro[i]
                P.add("pool", lambda r_=r_: nc.gpsimd.tensor_tensor(out=r_[:], in0=yn[:], in1=gs[:], op=ALU.mult), reads=[yn, gs], writes=[r_])
                P.add("pe", lambda r_=r_: [nc.tensor.transpose(psR[:, c8, :], r_[:, c8 * 128:(c8 + 1) * 128], ident[:]) for c8 in range(8)][-1],
                      reads=[r_, ident], writes=psR_r)
                blk = c // 4
                rT = roT[blk % 2]
                P.add("act", lambda rT=rT, c=c: nc.scalar.copy(rT[:, :, (c % 4) * 128:(c % 4 + 1) * 128], psR), reads=psR_r, writes=[rT])
                if c % 4 == 3:
                    dma(roTsU[u][:, :, SL(blk * 512, 512)].rearrange("c p t -> p c t"), rT[:], reads=[rT])
            P.emit()
        A.reset(m)
    def phase3a(l):
        m = A.mark()
        Wa = A.alloc("w_ba", [4, D], BF16)
        Wr = A.alloc("w_br", [8, D], BF16)
        Wo = A.alloc("w_o", [8, D], BF16)
        m2 = A.mark()
        stage = [A.alloc("wst%d" % i, [2048], F32) for i in range(2)]
        load_w(Wa, w_ba_c, 4, 0, D, stage)
        load_w(Wr, w_br_c, 8, 0, D, stage)
        load_w(Wo, w_out_c, 8, 0, D, stage)
        P.emit()
        A.reset(m2)
        xT = [A.alloc("xT%d" % i, [8, 512], F32) for i in range(2)]
        aT = [A.alloc("aT%d" % i, [4, 512], BF16) for i in range(2)]
        rT = [A.alloc("rT%d" % i, [8, 512], BF16) for i in range(2)]
        gT = [A.alloc("gT%d" % i, [16, 512], BF16) for i in range(2)]
        mixed = A.alloc("mixed", [8, 512], BF16)
        t1 = [A.alloc("t1_%d" % i, [512], F32) for i in range(2)]
        t2 = [A.alloc("t2_%d" % i, [512], F32) for i in range(2)]
        sq = A.alloc("sq", [8, 512], BF16)
        h2 = [A.alloc("h2_%d" % i, [8, 512], BF16) for i in range(2)]
        tmp = A.alloc("tmp", [512], F32)
        rstd = A.alloc("rstd", [512], F32)

        for u in range(NU):
            def load(b):
                i = b % 2
                sl = SL(b * 512, 512)
                dma(aT[i][:], aTsU[u][:, :, sl].rearrange("c p t -> p c t"), writes=[aT[i]])
                dma(rT[i][:], roTsU[u][:, :, sl].rearrange("c p t -> p c t"), writes=[rT[i]])
                dma(gT[i][:], gTsU[u][:, :, sl].rearrange("c p t -> p c t"), writes=[gT[i]])
                dma(xT[i][:], xTsU[u][:, :, sl].rearrange("c p t -> p c t"), writes=[xT[i]])
            load(0)
            for b in range(8):
                if b + 1 < 8:
                    load(b + 1)
                i = b % 2
                a_, r_, g_, x_ = aT[i], rT[i], gT[i], xT[i]
                for oc in range(8):
                    bka, bra = next_bank()
                    P.add("pe", partial(mm_group, bka, [(Wa[:, k, oc * 128:(oc + 1) * 128], a_[:, k, :]) for k in range(4)]), reads=[Wa, a_], writes=[bra])
                    bkr, brr = next_bank()
                    P.add("pe", partial(mm_group, bkr, [(Wr[:, k, oc * 128:(oc + 1) * 128], r_[:, k, :]) for k in range(8)]), reads=[Wr, r_], writes=[brr])
                    u1, u2 = t1[oc % 2], t2[oc % 2]
                    P.add("dve", lambda bka=bka, u1=u1, oc=oc, g_=g_: nc.vector.tensor_tensor(out=u1[:], in0=bka, in1=g_[:, oc, :], op=ALU.mult),
                          reads=[bra, g_], writes=[u1])
                    P.add("dve", lambda bkr=bkr, u2=u2, oc=oc, g_=g_: nc.vector.tensor_tensor(out=u2[:], in0=bkr, in1=g_[:, 8 + oc, :], op=ALU.mult),
                          reads=[brr, g_], writes=[u2])
                    P.add("pool", lambda u1=u1, u2=u2, oc=oc: nc.gpsimd.tensor_tensor(out=mixed[:, oc, :], in0=u1[:], in1=u2[:], op=ALU.add),
                          reads=[u1, u2], writes=[mixed])
                for oc in range(8):
                    bk, br = next_bank()
                    P.add("pe", partial(mm_group, bk, [(Wo[:, k, oc * 128:(oc + 1) * 128], mixed[:, k, :]) for k in range(8)]), reads=[Wo, mixed], writes=[br])
                    P.add("dve", lambda bk=bk, oc=oc, x_=x_: nc.vector.tensor_tensor(out=x_[:, oc, :], in0=x_[:, oc, :], in1=bk, op=ALU.add),
                          reads=[br, x_], writes=[x_])
                rms_fm(x_, gffn_ap, lp, h2[i], 512, sq, tmp, rstd)
                dma(xTsU[u][:, :, SL(b * 512, 512)].rearrange("c p t -> p c t"), x_[:], reads=[x_])
                dma(h2TsU[u][:, :, SL(64 + b * 512, 512)].rearrange("c p t -> p c t"), h2[i][:], reads=[h2[i]])
            P.emit()
        A.reset(m)

    def phase3b(l):
        m = A.mark()
        Wu = A.alloc("w_up", [8, 2 * DFF], BF16)
        Wd = A.alloc("w_dn", [22, D], BF16)
        m2 = A.mark()
        stage = [A.alloc("wst%d" % i, [2048], F32) for i in range(2)]
        load_w(Wu, w_up_c, 8, 0, 2 * DFF, stage)
        load_w(Wd, w_dn_c, 22, 0, D, stage)
        P.emit()
        A.reset(m2)
        N = 256
        xT = [A.alloc("xT%d" % i, [8, N], F32) for i in range(2)]
        h2 = [A.alloc("h2_%d" % i, [8, N + 2], BF16) for i in range(2)]
        act_ = A.alloc("act", [22, N], BF16)
        cg = [A.alloc("cg%d" % i, [N], F32) for i in range(3)]
        cv = [A.alloc("cv%d" % i, [N], F32) for i in range(3)]
        tt = [A.alloc("tt%d" % i, [N], F32) for i in range(3)]
        sg = [A.alloc("sg%d" % i, [N], F32) for i in range(3)]
        sq = A.alloc("sq", [8, N], BF16)
        tmp = A.alloc("tmp", [N], F32)
        rstd = A.alloc("rstd", [N], F32)
        hT = [A.view(act_, [8, N], BF16)]
        convw = convw_v
        for u in range(NU):
            load_unit_params(u)

            def load(fb):
                i = fb % 2
                dma(h2[i][:], h2TsU[u][:, :, SL(63 + fb * N, N + 2)].rearrange("c p t -> p c t"), writes=[h2[i]])
                dma(xT[i][:], xTsU[u][:, :, SL(fb * N, N)].rearrange("c p t -> p c t"), writes=[xT[i]])
            load(0)
            for fb in range(UT // N):
                if fb + 1 < UT // N:
                    load(fb + 1)
                i = fb % 2
                h_, x_ = h2[i], xT[i]
                t0 = fb * N
                if fb == 0:
                    P.add("dve", lambda h_=h_: nc.vector.tensor_scalar(h_[:, :, 0], h_[:, :, 0], lk2[:, 0:1], None, ALU.mult),
                          reads=[h_, lk2], writes=[h_])
                if fb == UT // N - 1:
                    P.add("dve", lambda h_=h_: nc.vector.tensor_scalar(h_[:, :, N + 1], h_[:, :, N + 1], lk2[:, 1:2], None, ALU.mult),
                          reads=[h_, lk2], writes=[h_])
                for fc in range(22):
                    res = []
                    for (col, dst) in ((fc * 128, cg[fc % 3]), (DFF + fc * 128, cv[fc % 3])):
                        bk, br = next_bank()
                        P.add("pe", partial(mm_group, bk[:, 0:N + 2], [(Wu[:, k, col:col + 128], h_[:, k, :]) for k in range(8)]), reads=[Wu, h_], writes=[br])
                        cc = col // 128
                        P.add("act", lambda bk=bk, dst=dst, cc=cc: nc.scalar.activation(out=dst[:], in_=bk[:, 1:N + 1], func=AF.Copy, scale=convw[:, 1, cc:cc + 1]),
                              reads=[br, lp], writes=[dst])
                        if "p3b_conv" in SKIP:
                            continue
                        P.add("dve", lambda bk=bk, dst=dst, cc=cc: nc.vector.scalar_tensor_tensor(out=dst[:], in0=bk[:, 0:N], scalar=convw[:, 0, cc:cc + 1],
                                                                                                  in1=dst[:], op0=ALU.mult, op1=ALU.add),
                              reads=[br, lp, dst], writes=[dst])
                        P.add("dve", lambda bk=bk, dst=dst, cc=cc: nc.vector.scalar_tensor_tensor(out=dst[:], in0=bk[:, 2:N + 2], scalar=convw[:, 2, cc:cc + 1],
                                                                                                  in1=dst[:], op0=ALU.mult, op1=ALU.add),
                              reads=[br, lp, dst], writes=[dst])
                    def stage_a(f):
                        g_, t_, s_ = cg[f % 3], tt[f % 3], sg[f % 3]
                        P.add("act", lambda: nc.scalar.activation(out=t_[:], in_=g_[:], func=AF.Square, scale=0.044715 ** 0.5), reads=[g_], writes=[t_])
                        P.add("dve", lambda: nc.vector.scalar_tensor_tensor(out=s_[:], in0=t_[:], scalar=1.0, in1=g_[:], op0=ALU.add, op1=ALU.mult),
                              reads=[g_, t_], writes=[s_])

                    def stage_b(f):
                        g_, v_, s_ = cg[f % 3], cv[f % 3], sg[f % 3]
                        P.add("act", lambda: nc.scalar.activation(out=s_[:], in_=s_[:], func=AF.Sigmoid, scale=GELU_C), reads=[s_], writes=[s_])
                        P.add("pool", lambda: nc.gpsimd.tensor_tensor(out=v_[:], in0=g_[:], in1=v_[:], op=ALU.mult), reads=[g_, v_], writes=[v_])
                        P.add("pool", lambda: nc.gpsimd.tensor_tensor(out=act_[:, f, :], in0=s_[:], in1=v_[:], op=ALU.mult),
                              reads=[s_, v_], writes=[act_])
                    if fc >= 1:
                        stage_a(fc - 1)
                    if fc >= 2:
                        stage_b(fc - 2)
                    if fc == 21:
                        stage_b(20)
                        stage_a(21)
                        stage_b(21)
                for oc in range(8):
                    if "p3b_dn" in SKIP:
                        continue
                    bk, br = next_bank()
                    P.add("pe", partial(mm_group, bk[:, 0:N], [(Wd[:, k, oc * 128:(oc + 1) * 128], act_[:, k, :]) for k in range(22)]), reads=[Wd, act_], writes=[br])
                    P.add("dve", lambda bk=bk, oc=oc, x_=x_: nc.vector.tensor_tensor(out=x_[:, oc, :], in0=x_[:, oc, :], in1=bk[:, 0:N], op=ALU.add),
                          reads=[br, x_], writes=[x_])
                ho = hT[0]
                rms_fm(x_, gnext[:, :], gnext, ho, N, sq, tmp, rstd)
                dma(xTsU[u][:, :, SL(t0, N)].rearrange("c p t -> p c t"), x_[:], reads=[x_])
                dma(hTsU[u][:, :, SL(t0, N)].rearrange("c p t -> p c t"), ho[:], reads=[ho])
            P.emit()
        A.reset(m)

    def phase_final():
        m = A.mark()
        N = 512
        xT = [A.alloc("xT%d" % i, [8, N], F32) for i in range(2)]
        yT = A.alloc("yT", [8, N], F32)
        ytok = [A.alloc("ytok%d" % i, [1024], F32) for i in range(2)]
        sq = A.alloc("sq", [8, N], BF16)
        tmp = A.alloc("tmp", [N], F32)
        rstd = A.alloc("rstd", [N], F32)
        for u in range(NU):
            def load(fb):
                dma(xT[fb % 2][:], xTsU[u][:, :, SL(fb * N, N)].rearrange("c p t -> p c t"), writes=[xT[fb % 2]])
            load(0)
            for fb in range(UT // N):
                if fb + 1 < UT // N:
                    load(fb + 1)
                x_ = xT[fb % 2]
                t0 = fb * N
                rms_fm(x_, gfin[:, :], gfin, yT, N, sq, tmp, rstd)
                for s in range(N // 128):
                    j = s % 2
                    db = dbank(j)
                    P.add("pe", lambda db=db, s=s: [nc.tensor.transpose(db[:, c * 128:(c + 1) * 128], yT[:, c, s * 128:(s + 1) * 128], identf[:])
                                                     for c in range(8)][-1], reads=[yT, identf], writes=[psr[2 * j], psr[2 * j + 1]])
                    yt = ytok[s % 2]
                    P.add("act", lambda db=db, yt=yt: nc.scalar.copy(yt[:], db), reads=[psr[2 * j], psr[2 * j + 1]], writes=[yt])
                    dma(yU[u][SL(t0 + s * 128, 128), :], yt[:], reads=[yt])
            P.emit()
        A.reset(m)

    setup()
    load_layer_params(0)
    phase0()
    with nc.Fori(0, L) as l:
        load_layer_params(l)
        if "p1" not in SKIP:
            phase1(l)
        if "na" not in SKIP:
            phase_na(l)
        if "ret" not in SKIP:
            phase_ret(l)
        if "p3a" not in SKIP:
            phase3a(l)
        if "p3b" not in SKIP:
            phase3b(l)
    phase_final()
    stats = dict(P.tot)
    stats["sbuf_peak"] = A.peak
    return nc, stats


def _rope_tables(pos):
    f32 = np.float32
    inv_freq = (f32(10000.0) ** (-(np.arange(0, 128, 2, dtype=f32) / f32(128)))).astype(f32)
    ang = (pos.astype(f32)[None, :] * inv_freq[:, None]).astype(f32)
    c = np.cos(ang).astype(f32)
    s = np.sin(ang).astype(f32)
    return np.stack([np.concatenate([c, c], 0), np.concatenate([-s, s], 0)], 0)


def _ctab():
    f32 = np.float32
    j = np.arange(128)[:, None]
    i = np.arange(128)[None, :]
    t = np.zeros((128, CT_N), f32)
    t[:, CT_M1:CT_M1 + 128] = np.maximum(i - j, 0)
    t[:, CT_M2:CT_M2 + 128] = np.maximum(j - i, 0)
    t[:, CT_I1:CT_I1 + 128] = np.broadcast_to(i + 1, (128, 128))
    t[:, CT_I2:CT_I2 + 128] = np.broadcast_to(128 - i, (128, 128))
    t[:, CT_J1] = 127 - np.arange(128)
    t[:, CT_J2] = np.arange(128)
    return t


def _kx():
    k = np.zeros((12, 768), np.float32)
    for m in range(6):
        for a in range(2):
            k[2 * m + a, m * 128 + a * 64:m * 128 + a * 64 + 64] = 1.0
    return k


def _qx(link):
    NU = len(link) - 1
    BIG = 8.0 * NEG
    q = np.full((12, NU, 5, 2, 64), BIG, np.float32)
    for u in range(NU):
        lp, ln = link[u] > 0, link[u + 1] > 0
        for a in range(2):
            q[a:a + 8, u, 0, a, :] = 0.0
            rng = (a, a + 8) if lp else (4, 12)
            q[rng[0]:rng[1], u, 1, a, :] = 0.0
            rng = (a + 2, a + 10) if lp else (4, 12)
            q[rng[0]:rng[1], u, 2, a, :] = 0.0
            rng = (a, a + 8) if ln else (0, 8)
            q[rng[0]:rng[1], u, 3, a, :] = 0.0
            rng = (a + 2, a + 10) if ln else (0, 8)
            q[rng[0]:rng[1], u, 4, a, :] = 0.0
    return q.reshape(12, NU * 5 * 128)


def _ta(rel_bias):
    L = rel_bias.shape[0]
    ap, cp, mm, a, c = np.meshgrid(np.arange(2), np.arange(64), np.arange(6), np.arange(2), np.arange(64), indexing="ij")
    cs = np.clip(c - 8, 0, 48)
    colok = (cp >= cs) & (cp < cs + 16)
    ci = np.clip(cp - c + 15, 0, 30)
    out = np.empty((L, 2, 128, 8, 768), np.float32)
    for t, off in enumerate((4, 6)):
        j = (2 * mm + ap) - (off + a) + 7
        assert j.min() >= 0 and j.max() <= 14
        g = rel_bias[:, :, j, ci]
        g = np.where(colok[None, None], g, np.float32(NEG))
        out[:, t] = np.transpose(g.reshape(L, 8, 128, 768), (0, 2, 1, 3))
    return out


_CACHE = {}


def _get_prog(NU, L):
    key = (NU, L)
    if key not in _CACHE:
        _CACHE[key] = build(NU, L)
    return _CACHE[key]


def make_in_maps(inputs, core_tokens, core_pos, core_links, L):
    f32 = np.float32
    shared = {
        "kx": _kx(), "ctab": _ctab(),
        "wall": np.concatenate([np.asarray(inputs[k][:L], f32).reshape(L, -1) for k in
                                ("w_in", "w_branch_attn", "w_branch_ret", "w_out", "w_up", "w_down")]
                               + [_ta(np.asarray(inputs["na_rel_bias"], f32)[:L]).reshape(L, -1),
                                  np.asarray(inputs["ret_norm_g"][:L], f32).reshape(L, -1)], axis=1),
        "norm_mix_g": np.ascontiguousarray(inputs["norm_mix_g"][:L]).reshape(L, 8, 128),
        "norm_ffn_g": np.ascontiguousarray(inputs["norm_ffn_g"][:L]).reshape(L, 8, 128),
        "norm_final_g": np.ascontiguousarray(inputs["norm_final_g"]).reshape(8, 128),
        "ffn_conv_w": np.ascontiguousarray(inputs["ffn_conv_w"][:L]).reshape(L, 3, 44, 128),
        "ret_decay_fwd": np.ascontiguousarray(inputs["ret_decay_fwd"][:L]).reshape(-1),
        "ret_decay_bwd": np.ascontiguousarray(inputs["ret_decay_bwd"][:L]).reshape(-1),
    }
    maps = []
    for xt, pos, link in zip(core_tokens, core_pos, core_links):
        NU_ = len(link) - 1
        lk = np.zeros((NU_, 128, 2), f32)
        for u_ in range(NU_):
            lk[u_, :, 0] = link[u_]
            lk[u_, :, 1] = link[u_ + 1]
        d = dict(shared)
        d.update({"x": np.ascontiguousarray(xt, dtype=f32), "rope": _rope_tables(pos), "links": lk, "qx": _qx(link)})
        maps.append(d)
    return maps


def kernel(**inputs):
    L = 4
    NU = 4
    inputs = {k: np.asarray(v) for k, v in inputs.items()}
    xp = inputs["x_prompt"]
    xs = inputs["x_sample"]
    nc, _ = _get_prog(NU, L)
    toks, poss, lnks = [], [], []
    ppos = np.tile(np.arange(UT), NU)
    for c in range(4):
        toks.append(xp[4 * c:4 * c + 4].reshape(NU * UT, D))
        poss.append(ppos)
        lnks.append([0, 0, 0, 0, 0])
    toks.append(xs[0])
    poss.append(np.arange(NU * UT))
    lnks.append([0, 1, 1, 1, 0])
    for c in range(5, 8):
        toks.append(toks[0])
        poss.append(ppos)
        lnks.append([0, 0, 0, 0, 0])
    maps = make_in_maps(inputs, toks, poss, lnks, L)
    res = run_bass_kernel_spmd(nc, maps, core_ids=list(range(8)))
    outs = [r["y"] for r in res.results]
    y_prompt = np.stack([outs[c].reshape(4, UT, D) for c in range(4)], 0).reshape(16, UT, D).astype(np.float32)
    y_sample = outs[4].reshape(1, NU * UT, D).astype(np.float32)
    return (y_prompt, y_sample)
```
